# Optimizing a Trainium2 kernel written in Bass

```python
import math
import jax, jax.numpy as jnp
from jax import lax
import numpy as np

D_MODEL = 1024
BATCH = 4
SEQ = 8192
DEPTH = 1

HG_HEADS = 4
HG_HEAD_DIM = 128
HG_WIDTH = HG_HEADS * HG_HEAD_DIM
HG_CHUNK = 64
SG_GROUPS = 4
SG_GROUP_DIM = 128
SG_WIDTH = SG_GROUPS * SG_GROUP_DIM
SG_CHUNK = 128
FFN_MULT = 256
D_FF = -(-8 * D_MODEL // (3 * FFN_MULT)) * FFN_MULT
EPS = 1e-6
IN_SPLITS = (HG_WIDTH, HG_WIDTH, HG_WIDTH, HG_WIDTH, HG_WIDTH,
             SG_WIDTH, SG_WIDTH,
             D_MODEL, D_MODEL)
IN_COLS = sum(IN_SPLITS)

kernel_name = "hybrid_hgrn2_gmlp_gated_block"


def rmsnorm(x, w):
    xf = x.astype(jnp.float32)
    y = xf * lax.rsqrt(jnp.mean(xf * xf, axis=-1, keepdims=True) + EPS)
    return (y * w.astype(jnp.float32)).astype(x.dtype)


def layernorm(x, w, b):
    xf = x.astype(jnp.float32)
    mu = jnp.mean(xf, axis=-1, keepdims=True)
    xc = xf - mu
    y = xc * lax.rsqrt(jnp.mean(xc * xc, axis=-1, keepdims=True) + EPS)
    return (y * w.astype(jnp.float32) + b.astype(jnp.float32)).astype(x.dtype)


def hgrn2_chunk_scan(q, k, v, log_f):
    B, L, H, K = q.shape
    V = v.shape[-1]
    C = HG_CHUNK
    n = L // C
    q = q.astype(jnp.float32).reshape(B, n, C, H, K)
    k = k.astype(jnp.float32).reshape(B, n, C, H, K)
    v = v.astype(jnp.float32).reshape(B, n, C, H, V)
    b = jnp.cumsum(log_f.astype(jnp.float32).reshape(B, n, C, H, K), axis=2)
    b_last = b[:, :, -1]
    ref = b[:, :, C // 2 - 1:C // 2]
    qr = q * jnp.exp(b - ref)
    kr = k * jnp.exp(ref - b)
    scores = jnp.einsum('bnthk,bnshk->bnhts', qr, kr)
    mask = jnp.tril(jnp.ones((C, C), dtype=bool))
    scores = jnp.where(mask, scores, 0.0)
    o_intra = jnp.einsum('bnhts,bnshv->bnthv', scores, v)
    kv = jnp.einsum('bnshk,bnshv->bnhkv', k * jnp.exp(b_last[:, :, None] - b), v)
    decay = jnp.exp(b_last)

    def step(S, inp):
        kv_c, d_c = inp
        return d_c[..., None] * S + kv_c, S

    S0 = jnp.zeros((B, H, K, V), jnp.float32)
    _, S_prev = lax.scan(step, S0, (jnp.moveaxis(kv, 1, 0), jnp.moveaxis(decay, 1, 0)))
    S_prev = jnp.moveaxis(S_prev, 0, 1)
    o_inter = jnp.einsum('bnthk,bnhkv->bnthv', q * jnp.exp(b), S_prev)
    return (o_intra + o_inter).reshape(B, L, H, V)


def hgrn2_bidirectional(q, i, f_fwd_logit, f_bwd_logit, lb):
    B, L, _ = q.shape
    shp = (B, L, HG_HEADS, HG_HEAD_DIM)
    qh = q.reshape(shp)
    vh = i.reshape(shp)

    def gates(logit, lower):
        f = lower + (1.0 - lower) * jax.nn.sigmoid(logit.astype(jnp.float32))
        return (1.0 - f).reshape(shp), jnp.log(f).reshape(shp)

    k_f, lf_f = gates(f_fwd_logit, lb[0])
    k_b, lf_b = gates(f_bwd_logit, lb[1])
    o_fwd = hgrn2_chunk_scan(qh, k_f, vh, lf_f)
    flip = lambda t: jnp.flip(t, axis=1)
    o_bwd = flip(hgrn2_chunk_scan(flip(qh), flip(k_b), flip(vh), flip(lf_b)))
    return o_fwd + o_bwd


def spatial_gating(u, v, ln_w, ln_b, w_s, b_s):
    B, L, _ = u.shape
    n = L // SG_CHUNK
    u = jax.nn.gelu(u)
    v = layernorm(jax.nn.gelu(v), ln_w, ln_b)
    vc = v.reshape(B, n, SG_CHUNK, SG_GROUPS, SG_GROUP_DIM)
    mixed = jnp.einsum('gts,bnsgc->bntgc', w_s, vc) + jnp.transpose(b_s)[None, None, :, :, None]
    return u * mixed.reshape(B, L, SG_WIDTH)


def setup_inputs(seed: int = 0) -> dict:
    key = jax.random.key(seed)
    ks = jax.random.split(key, 20)
    f32 = jnp.float32
    nrm = lambda k, shp, s: jax.random.normal(k, shp, f32) * s
    gain = lambda k, shp: 1.0 + 0.02 * jax.random.normal(k, shp, f32)
    return {
        "x": jax.random.normal(ks[0], (BATCH, SEQ, D_MODEL), f32),
        "pre_mix_w": gain(ks[1], (DEPTH, D_MODEL)),
        "w_in": nrm(ks[2], (DEPTH, D_MODEL, IN_COLS), D_MODEL ** -0.5),
        "lb_logits": nrm(ks[3], (DEPTH + 1, 2, HG_WIDTH), 0.5),
        "hg_norm_w": gain(ks[4], (DEPTH, HG_WIDTH)),
        "sg_ln_w": gain(ks[5], (DEPTH, SG_WIDTH)),
        "sg_ln_b": nrm(ks[6], (DEPTH, SG_WIDTH), 0.02),
        "sg_spatial_w": nrm(ks[7], (DEPTH, SG_GROUPS, SG_CHUNK, SG_CHUNK), SG_CHUNK ** -0.5),
        "sg_spatial_b": gain(ks[8], (DEPTH, SG_GROUPS, SG_CHUNK)),
        "w_proj_a": nrm(ks[9], (DEPTH, HG_WIDTH, D_MODEL), HG_WIDTH ** -0.5),
        "w_proj_b": nrm(ks[10], (DEPTH, SG_WIDTH, D_MODEL), SG_WIDTH ** -0.5),
        "w_out": nrm(ks[11], (DEPTH, D_MODEL, D_MODEL), D_MODEL ** -0.5),
        "post_mix_w": gain(ks[12], (DEPTH, D_MODEL)),
        "pre_ffn_w": gain(ks[13], (DEPTH, D_MODEL)),
        "w_gate": nrm(ks[14], (DEPTH, D_MODEL, D_FF), D_MODEL ** -0.5),
        "w_up": nrm(ks[15], (DEPTH, D_MODEL, D_FF), D_MODEL ** -0.5),
        "w_down": nrm(ks[16], (DEPTH, D_FF, D_MODEL), D_FF ** -0.5),
        "post_ffn_w": gain(ks[17], (DEPTH, D_MODEL)),
    }


def reference(x, pre_mix_w, w_in, lb_logits, hg_norm_w, sg_ln_w, sg_ln_b, sg_spatial_w,
              sg_spatial_b, w_proj_a, w_proj_b, w_out, post_mix_w, pre_ffn_w, w_gate, w_up,
              w_down, post_ffn_w):
    B, L, _ = x.shape
    lb_all = jnp.cumsum(jax.nn.softmax(lb_logits.astype(jnp.float32), axis=0), axis=0)
    offs = [sum(IN_SPLITS[:j]) for j in range(1, len(IN_SPLITS))]
    for l in range(DEPTH):
        h = rmsnorm(x, pre_mix_w[l])
        proj = jnp.einsum('bld,dc->blc', h, w_in[l])
        q, i, f_fw, f_bw, g, u, v, ga, gb = jnp.split(proj, offs, axis=-1)
        o = hgrn2_bidirectional(q, i, f_fw, f_bw, lb_all[l])
        o = o * lax.rsqrt(jnp.mean(o * o, axis=-1, keepdims=True) + EPS)
        o = o.reshape(B, L, HG_WIDTH) * hg_norm_w[l].astype(jnp.float32)
        o = o.astype(x.dtype) * jax.nn.silu(g)
        y_a = jnp.einsum('blc,cd->bld', o, w_proj_a[l])
        s = spatial_gating(u, v, sg_ln_w[l], sg_ln_b[l], sg_spatial_w[l], sg_spatial_b[l])
        y_b = jnp.einsum('blc,cd->bld', s, w_proj_b[l])
        merged = jax.nn.sigmoid(ga) * y_a + jax.nn.sigmoid(gb) * y_b
        mix = jnp.einsum('bld,de->ble', merged, w_out[l])
        x = x + rmsnorm(mix, post_mix_w[l])
        h2 = rmsnorm(x, pre_ffn_w[l])
        ff = jax.nn.silu(jnp.einsum('bld,df->blf', h2, w_gate[l])) * jnp.einsum('bld,df->blf', h2, w_up[l])
        ff = jnp.einsum('blf,fd->bld', ff, w_down[l])
        x = x + rmsnorm(ff, post_ffn_w[l])
    return x
```

```python
import numpy as np
import ml_dtypes
import concourse.bass as bass
import concourse.mybir as mybir
from concourse.bass_utils import run_bass_kernel_spmd

F32 = mybir.dt.float32
BF16 = mybir.dt.bfloat16
AF = mybir.ActivationFunctionType
ALU = mybir.AluOpType

D = 1024
NH = 4
HD = 128
DFF = 2816
EPS = 1e-6
T = 512
NKC = 8
NFC = 22
G_ZF, G_ZB, G_Q, G_I, G_V, G_G, G_U, G_GA0, G_GA1, G_GB0, G_GB1 = range(11)
SL_AB = 11
SL_O = 13
SL_GU = 15
SL_D = 26
NSLOT = 34
NRING = 4


class Buf:
    __slots__ = ("name", "writer", "readers", "dsem", "dtotal", "alias", "excl")

    def __init__(self, name, excl=False):
        self.name = name
        self.excl = excl
        self.writer = None
        self.readers = []
        self.dsem = None
        self.dtotal = 0
        self.alias = ()


class Op:
    __slots__ = ("eng", "fn", "deps", "milestone", "count", "is_dma", "dma_sem", "dma_val", "tag")

    def __init__(self, eng, fn, deps):
        self.eng = eng
        self.fn = fn
        self.deps = deps
        self.milestone = False
        self.count = None
        self.is_dma = False
        self.dma_sem = None
        self.dma_val = 0


class Sched:
    ENGS = ("pe", "act", "dve", "pool", "sp")

    def __init__(self, nc):
        self.nc = nc
        self.ops = {e: [] for e in self.ENGS}
        self.sems = {e: nc.alloc_semaphore("sem_" + e) for e in self.ENGS}
        self.tag = ""
        self.names = {}

    def _deps(self, reads, writes):
        deps = []
        for b in reads:
            if b.writer is not None:
                deps.append(b.writer)
            if b.excl:
                deps.extend(b.readers)
        for b in writes:
            if b.writer is not None:
                deps.append(b.writer)
            deps.extend(b.readers)
            for a in b.alias:
                if a.writer is not None:
                    deps.append(a.writer)
                deps.extend(a.readers)
        return deps

    def _update(self, op, reads, writes):
        for b in writes:
            b.writer = op
            b.readers = []
        for b in reads:
            if b in writes:
                continue
            if not op.is_dma:
                b.readers = [r for r in b.readers if r.is_dma or r.eng != op.eng]
            b.readers.append(op)

    def op(self, eng, fn, reads=(), writes=()):
        o = Op(eng, fn, self._deps(reads, writes))
        o.tag = self.tag
        self.ops[eng].append(o)
        self._update(o, reads, writes)
        return o

    def dma(self, eng, fn, reads=(), writes=(), sembuf=None):
        o = Op(eng, fn, self._deps(reads, writes))
        o.tag = self.tag
        o.is_dma = True
        if sembuf.dsem is None:
            sembuf.dsem = self.nc.alloc_semaphore("dsem_" + sembuf.name)
        sembuf.dtotal += 16
        o.dma_sem = sembuf.dsem
        o.dma_val = sembuf.dtotal
        self.ops[eng].append(o)
        self._update(o, reads, writes)
        return o

    def final_wait(self, eng, deps):
        o = Op(eng, None, list(deps))
        self.ops[eng].append(o)
        return o

    def emit(self):
        nc = self.nc
        for e in self.ENGS:
            for o in self.ops[e]:
                for d in o.deps:
                    if not d.is_dma:
                        d.milestone = True
        for e in self.ENGS:
            c = 0
            for o in self.ops[e]:
                if o.milestone and not o.is_dma:
                    c += 1
                    o.count = c

        def replay(e):
            def body(engine):
                seen = {}
                for o in self.ops[e]:
                    need = {}
                    for d in o.deps:
                        if d.is_dma:
                            key, val, sem = ("d", id(d.dma_sem)), d.dma_val, d.dma_sem
                        else:
                            key, val, sem = ("e", d.eng), d.count, self.sems[d.eng]
                        if seen.get(key, 0) >= val:
                            continue
                        if key not in need or need[key][1] < val:
                            need[key] = (sem, val)
                    for key, (sem, val) in need.items():
                        engine.wait_ge(sem, val)
                        seen[key] = val
                    if o.fn is None:
                        continue
                    inst = o.fn(engine)
                    try:
                        self.names[inst.ins.name] = o.tag
                    except Exception:
                        pass
                    if o.is_dma:
                        inst.then_inc(o.dma_sem, 16)
                    elif o.milestone:
                        inst.then_inc(self.sems[e], 1)
            return body

        with nc.Block() as block:
            block.tensor(replay("pe"))
            block.scalar(replay("act"))
            block.vector(replay("dve"))
            block.gpsimd(replay("pool"))
            block.sync(replay("sp"))


def build_program(L, n_own_tiles=None, n_p1_tiles=None, debug=False, last_stage=10, do_cast=1, skip_s10=0, s1_steps=9, s3_steps=9):
    own = L // 2
    NT2 = own // T if n_own_tiles is None else n_own_tiles
    NTALL = L // T
    nc = bass.Bass("TRN2", target_bir_lowering=False)
    S = Sched(nc)

    def din(name, shape, dt=F32):
        return nc.dram_tensor(name, list(shape), dt, kind="ExternalInput").ap()

    xs = din("xs", [L, D])
    w_in = din("w_in", [D, 5632])
    w_pa = din("w_pa", [512, D])
    w_pb = din("w_pb", [512, D])
    w_out = din("w_out", [D, D])
    w_gate = din("w_gate", [D, DFF])
    w_up = din("w_up", [D, DFF])
    w_down = din("w_down", [DFF, D])
    smalls_d = din("smalls", [128, 56])
    lnb_d = din("lnb_bc", [128, 512])
    wst_d = din("wst", [128, 4, 128])
    bs_d = din("bs", [1, 512])
    identf_d = din("identf", [128, 128])
    masks_d = din("masks", [4, 128, 512])
    y = nc.dram_tensor("y", [own, D], F32, kind="ExternalOutput").ap()
    wsc = nc.dram_tensor("wsc", [NSLOT, 128, 4096], BF16).ap()
    bnd = nc.dram_tensor("bnd", [max(NT2, 1), 128, 512], F32).ap()

    def sb(name, cols, dt=F32, parts=128):
        return nc.alloc_sbuf_tensor(name + "_sb", [parts, cols], dt)

    Bc = Buf("const")
    identf = sb("identf", 128)
    identb = sb("identb", 128, BF16)
    onesb = sb("onesb", 128, BF16)
    onesf = sb("onesf", 128)
    masks = sb("masks", 4 * 512)
    maskF4, maskB4 = masks[:, 0:512], masks[:, 512:1024]
    mF, mB = masks[:, 1024:1536], masks[:, 1536:2048]
    smalls = sb("smalls", 56)
    vecsT = smalls[:, 0:32]
    hgw = smalls[:, 32:36]
    lnw = smalls[:, 36:40]
    lblT = smalls[:, 40:56]
    oml = sb("oml", 8)
    noml = sb("noml", 8)
    lnb_bc = sb("lnb_bc", 512)
    wstf = sb("wstf", 512)
    wstb = sb("wstb", 512, BF16)
    bs_sb = sb("bs_sb", 512, F32, parts=1)
    extra = sb("extra", 512)

    def cdma(out, in_):
        S.dma("sp", lambda e: e.dma_start(out=out, in_=in_), writes=[Bc], sembuf=Bc)

    cdma(identf[:], identf_d)
    cdma(masks[:].rearrange("p (a c) -> p a c", a=4), masks_d.rearrange("a p c -> p a c"))
    cdma(smalls[:], smalls_d)
    cdma(lnb_bc[:], lnb_d)
    cdma(wstf[:].rearrange("p (g t) -> p g t", g=4), wst_d)
    cdma(bs_sb[:], bs_d)

    Bwsc = [Buf(f"wsc{i}") for i in range(NSLOT)]
    w_in_v = w_in.rearrange("(kc p) c -> p kc c", p=128)
    pa_v = w_pa.rearrange("(kc p) c -> p kc c", p=128)
    pb_v = w_pb.rearrange("(kc p) c -> p kc c", p=128)
    wo_v = w_out.rearrange("(kc p) c -> p kc c", p=128)
    wg_v = w_gate.rearrange("(kc p) c -> p kc c", p=128)
    wu_v = w_up.rearrange("(kc p) c -> p kc c", p=128)
    wd_v = w_down.rearrange("(kc p) c -> p kc c", p=128)
    pieces_A, pieces_B = [], []

    def win_pieces(g, lst):
        for k0 in (0, 4):
            lst.append((g, k0 * 512, 2048, [(0, 4, 512, 512, w_in_v[:, k0:k0 + 4, g * 512:(g + 1) * 512])]))

    win_pieces(G_ZB, pieces_A)
    win_pieces(G_I, pieces_A)
    for g in range(11):
        if g not in (G_ZB, G_I):
            win_pieces(g, pieces_B)
    for j in range(2):
        pieces_B.append((SL_AB + j, 0, 2048, [(0, 4, 512, 512, pa_v[:, :, j * 512:(j + 1) * 512])]))
        pieces_B.append((SL_AB + j, 2048, 2048, [(0, 4, 512, 512, pb_v[:, :, j * 512:(j + 1) * 512])]))
    for j in range(2):
        for k0 in (0, 4):
            pieces_B.append((SL_O + j, k0 * 512, 2048, [(0, 4, 512, 512, wo_v[:, k0:k0 + 4, j * 512:(j + 1) * 512])]))
    for j in range(11):
        for k0 in (0, 4):
            pieces_B.append((SL_GU + j, k0 * 512, 2048, [(0, 4, 256, 512, wg_v[:, k0:k0 + 4, j * 256:(j + 1) * 256]),
                                                       (256, 4, 256, 512, wu_v[:, k0:k0 + 4, j * 256:(j + 1) * 256])]))
    for oc in range(8):
        pieces_B.append((SL_D + oc, 0, 2048, [(0, 16, 128, 128, wd_v[:, 0:16, oc * 128:(oc + 1) * 128])]))
        pieces_B.append((SL_D + oc, 2048, 768, [(0, 6, 128, 128, wd_v[:, 16:22, oc * 128:(oc + 1) * 128])]))

    xT = sb("xT", NKC * T)
    BxT = [Buf(f"xT{k}") for k in range(NKC)]
    hT = sb("hT", NKC * T, BF16)
    BhT = [Buf(f"hT{k}") for k in range(NKC)]
    sq = [sb(f"sq{i}", 1024, BF16) for i in range(2)]
    Bsq = [Buf(f"sq{i}") for i in range(2)]
    xtm = [sb(f"xtm{i}", D) for i in range(2)]
    Bxtm = [Buf(f"xtm{i}") for i in range(2)]
    Sst = [sb(f"Sst{d}", 512) for d in range(2)]
    BS = [Buf(f"S{d}") for d in range(2)]
    dec = [sb(f"dec{d}", 32) for d in range(2)]
    Bdec = [Buf(f"dec{d}") for d in range(2)]
    ring = [sb(f"ring{i}", 4096, BF16) for i in range(NRING)]
    Bring = [Buf(f"ring{i}") for i in range(NRING)]
    sgT = sb("sgT", 4 * T, BF16)
    BsgT = [Buf(f"sgT{j}") for j in range(4)]
    guT = sb("guT", 4 * T, BF16)
    BguT = [Buf(f"guT{j}") for j in range(4)]
    xhat = sb("xhat", 4 * 512, BF16)
    Bxhat = [Buf(f"xhat{b}") for b in range(4)]
    gv = [sb(f"gv{i}", 512) for i in range(4)]
    Bgv = [Buf(f"gv{i}") for i in range(4)]
    stat = [sb(f"stat{i}", 16) for i in range(4)]
    Bstat = [Buf(f"stat{i}") for i in range(4)]
    onT = sb("onT", 4 * T, BF16)
    BonT = [Buf(f"onT{b}") for b in range(4)]
    sTt = sb("sT", 4 * T, BF16)
    BsT = [Buf(f"sT{b}") for b in range(4)]
    sga = [sb(f"sga{i}", T, BF16) for i in range(2)]
    Bsga = [Buf(f"sga{i}") for i in range(2)]
    sgb = [sb(f"sgb{i}", T, BF16) for i in range(2)]
    Bsgb = [Buf(f"sgb{i}") for i in range(2)]
    osq = [sb(f"osq{i}", 512, BF16) for i in range(2)]
    Bosq = [Buf(f"osq{i}") for i in range(2)]
    rso = [sb(f"rso{i}", 512) for i in range(2)]
    Brso = [Buf(f"rso{i}") for i in range(2)]
    t1 = [sb(f"t1_{i}", 512, BF16) for i in range(2)]
    Bt1 = [Buf(f"t1_{i}") for i in range(2)]
    tmpr = [sb(f"tmpr{i}", 512) for i in range(2)]
    Btmpr = [Buf(f"tmpr{i}") for i in range(2)]

    ARENA_B = 72 * 1024
    arena = sb("arena", ARENA_B // 2, BF16)

    def av(off, nbytes, dt):
        a = arena[:, off // 2:(off + nbytes) // 2]
        if dt == F32:
            a = a.bitcast(F32)
        return a

    K = 1024
    gun = []
    for i in range(3):
        base = i * 8 * K
        gun.append(dict(sp=av(base, 2 * K, F32), lf=av(base + 2 * K, 2 * K, F32), c=av(base + 4 * K, 2 * K, F32),
                        e1=av(base + 6 * K, K, BF16), e2=av(base + 7 * K, K, BF16),
                        Bsp=Buf(f"sp{i}"), Blf=Buf(f"lf{i}"), Bc=Buf(f"c{i}"), Be1=Buf(f"e1{i}"), Be2=Buf(f"e2{i}")))
    kr = [av(24 * K + u * K, K, BF16) for u in range(8)]
    Bkr = [Buf(f"kr{u}") for u in range(8)]
    qr = [av(32 * K + u * K, K, BF16) for u in range(8)]
    Bqr = [Buf(f"qr{u}") for u in range(8)]
    krTM = [[av(40 * K + (d * 4 + b) * K, K, BF16) for b in range(4)] for d in range(2)]
    BkrTM = [[Buf(f"krTM{d}{b}") for b in range(4)] for d in range(2)]
    vTM = [av(48 * K + b * K, K, BF16) for b in range(4)]
    BvTM = [Buf(f"vTM{b}") for b in range(4)]
    PT = [[av(52 * K + (d * 2 + i) * K, K, BF16) for i in range(2)] for d in range(2)]
    BPT = [[Buf(f"PT{d}{i}") for i in range(2)] for d in range(2)]
    Sbf = [[av(56 * K + (d * 8 + n) * K, K, BF16) for n in range(8)] for d in range(2)]
    BSbf = [[Buf(f"Sbf{d}{n}") for n in range(8)] for d in range(2)]
    hid = [av(j * K, K, BF16) for j in range(NFC)]
    Bhid = [Buf(f"hid{j}") for j in range(NFC)]
    mix = [av(22 * K + oc * 2 * K, 2 * K, F32) for oc in range(8)]
    Bmix = [Buf(f"mix{oc}") for oc in range(8)]
    merged = [av(38 * K + oc * K, K, BF16) for oc in range(8)]
    Bmerged = [Buf(f"mg{oc}") for oc in range(8)]
    m1 = [av(46 * K + i * 2 * K, 2 * K, F32) for i in range(2)]
    Bm1 = [Buf(f"m1{i}") for i in range(2)]
    m2 = [av(50 * K + i * 2 * K, 2 * K, F32) for i in range(2)]
    Bm2 = [Buf(f"m2{i}") for i in range(2)]
    tsc = [av(54 * K + i * 2 * K, 2 * K, F32) for i in range(2)]
    Btsc = [Buf(f"tsc{i}") for i in range(2)]
    sgate = [av(58 * K + i * K, K, BF16) for i in range(2)]
    Bsgate = [Buf(f"sgate{i}") for i in range(2)]

    early = []
    for g_ in gun:
        early += [g_["Bsp"], g_["Blf"], g_["Bc"], g_["Be1"], g_["Be2"]]
    early += Bkr + Bqr + BkrTM[0] + BkrTM[1] + BvTM + BPT[0] + BPT[1] + BSbf[0] + BSbf[1]
    late = Bhid + Bmix + Bmerged + Bm1 + Bm2 + Btsc + Bsgate
    for b in early:
        b.alias = tuple(late)
    for b in late:
        b.alias = tuple(early)

    stf = [av(56 * K + i * 8 * K, 8 * K, F32) for i in range(2)]
    Bstf = [Buf(f"stf{i}") for i in range(2)]
    stb = [av(32 * K + i * 4 * K, 4 * K, BF16) for i in range(2)]
    Bstb = [Buf(f"stb{i}") for i in range(2)]
    for b_ in Bstf:
        b_.alias = tuple(BSbf[0] + BSbf[1] + late)
    for b_ in Bstb:
        b_.alias = tuple(Bqr + late)
    for b_ in BSbf[0] + BSbf[1]:
        b_.alias = tuple(list(b_.alias) + Bstf)
    for b_ in Bqr:
        b_.alias = tuple(list(b_.alias) + Bstb)
    for b_ in late:
        b_.alias = tuple(list(b_.alias) + Bstf + Bstb)
    piece_ctr = [0]
    pending_store = []

    def emit_piece_store():
        if pending_store:
            (slot, doff, n, i) = pending_store.pop(0)
            S.dma("sp", lambda e: e.dma_start(out=wsc[slot][:, doff:doff + n], in_=stb[i][:, 0:n]),
                  reads=[Bstb[i]], writes=[Bwsc[slot]], sembuf=Bstb[i])

    def emit_piece(piece):
        if not do_cast:
            return
        (slot, doff, n, parts) = piece
        i = piece_ctr[0] % 2
        piece_ctr[0] += 1
        for (soff, kcn, cols, rowlen, src) in parts:
            dstv = stf[i][:, 0:kcn * rowlen].rearrange("p (k c) -> p k c", c=rowlen)[:, :, soff:soff + cols]
            S.dma("sp", lambda e, dstv=dstv, src=src: e.dma_start(out=dstv, in_=src), writes=[Bstf[i]], sembuf=Bstf[i])
        emit_piece_store()
        S.op("pool", lambda e: e.tensor_copy(out=stb[i][:, 0:n], in_=stf[i][:, 0:n]), reads=[Bstf[i]], writes=[Bstb[i]])
        pending_store.append((slot, doff, n, i))

    banks = [nc.alloc_psum_tensor(f"bank{i}", [128, 512], F32) for i in range(8)]
    Bbank = [Buf(f"bank{i}", excl=True) for i in range(8)]
    bank_ctr = [0]

    def nb():
        i = bank_ctr[0] % 7
        bank_ctr[0] += 1
        return banks[i], Bbank[i]

    def statbank():
        return banks[7], Bbank[7]

    def ACT(out, in_, func, r, w, scale=None, bias=None, accum=None):
        kw = {}
        if scale is not None:
            kw["scale"] = scale
        if bias is not None:
            kw["bias"] = bias
        if accum is not None:
            kw["accum_out"] = accum
        return S.op("act", lambda e: e.activation(out=out, in_=in_, func=func, **kw), reads=r, writes=w)

    def TT(eng, out, in0, in1, op, r, w):
        return S.op(eng, lambda e: e.tensor_tensor(out=out, in0=in0, in1=in1, op=op), reads=r, writes=w)

    def TS(eng, out, in0, s1, s2, op0, op1, r, w):
        if op1 is None:
            return S.op(eng, lambda e: e.tensor_scalar(out=out, in0=in0, scalar1=s1, scalar2=None, op0=op0), reads=r, writes=w)
        return S.op(eng, lambda e: e.tensor_scalar(out=out, in0=in0, scalar1=s1, scalar2=s2, op0=op0, op1=op1), reads=r, writes=w)

    def STT(out, in0, sc, in1, op0, op1, r, w):
        return S.op("dve", lambda e: e.scalar_tensor_tensor(out=out, in0=in0, scalar=sc, in1=in1, op0=op0, op1=op1), reads=r, writes=w)

    def COPY(eng, out, in_, r, w):
        if eng == "act":
            return ACT(out, in_, AF.Copy, r, w)
        return S.op(eng, lambda e: e.tensor_copy(out=out, in_=in_), reads=r, writes=w)

    def MM(specs, r, w):
        def fn(e):
            inst = None
            for (o_, l_, r_, st, sp_) in specs:
                inst = e.matmul(o_, l_, r_, start=st, stop=sp_)
            return inst
        return S.op("pe", fn, reads=r, writes=w)

    def TR(specs, ident, r, w):
        def fn(e):
            inst = None
            for (o_, i_) in specs:
                inst = e.transpose(o_, i_, ident)
            return inst
        return S.op("pe", fn, reads=r, writes=w)

    S.op("pool", lambda e: e.memset(onesf[:], 1.0), writes=[Bc], reads=[Bc])
    S.op("pool", lambda e: e.memset(onesb[:], 1.0), writes=[Bc], reads=[Bc])
    S.op("dve", lambda e: e.tensor_copy(out=identb[:], in_=identf[:]), writes=[Bc], reads=[Bc])
    lv = lblT.rearrange("p (l x) -> p l x", l=2)
    TT("dve", oml[:], lv[:, 0, :], lv[:, 1, :], ALU.subtract, [Bc], [Bc])
    ACT(oml[:], oml[:], AF.Exp, [Bc], [Bc])
    ACT(oml[:], oml[:], AF.Ln, [Bc], [Bc], bias=1.0)
    ACT(oml[:], oml[:], AF.Exp, [Bc], [Bc], scale=-1.0)
    TS("dve", noml[:], oml[:], -1.0, None, ALU.mult, None, [Bc], [Bc])
    COPY("act", wstb[:], wstf[:], [Bc], [Bc])
    bk, Bbk = nb()
    specs = []
    for g in range(4):
        specs.append((bk[:, g * 128:(g + 1) * 128], lnb_bc[:, g * 128:(g + 1) * 128], wstf[:, g * 128:(g + 1) * 128], True, False))
        specs.append((bk[:, g * 128:(g + 1) * 128], onesf[0:1, :], bs_sb[0:1, g * 128:(g + 1) * 128], False, True))
    MM(specs, [Bc], [Bbk])
    COPY("dve", extra[:], bk[:], [Bbk], [Bc])

    pmw = vecsT[:, 0:8]
    pmw2 = vecsT[:, 8:16]
    pfw = vecsT[:, 16:24]
    pfw2 = vecsT[:, 24:32]

    ring_ctr = [0]

    def load_slot(slot, ncols, wbuf=None):
        wbuf = Bwsc[slot]
        i = ring_ctr[0] % NRING
        ring_ctr[0] += 1
        rt, rb = ring[i], Bring[i]
        S.dma("sp", lambda e: e.dma_start(out=rt[:, 0:ncols], in_=wsc[slot][:, 0:ncols]),
              reads=[wbuf], writes=[rb], sembuf=rb)
        return rt, rb

    store_ops = []
    xtm_ctr = [0]
    gun_ctr = [0]
    rot = {"sq": 0, "gv": 0, "osq": 0, "m": 0, "sg": 0, "tsc": 0, "sgate": 0, "tmpr": 0, "pt": 0}

    def rmsnorm_rstd(msbank, Bms, n_feat):
        i = rot["tmpr"] % 2
        rot["tmpr"] += 1
        ACT(tmpr[i][:], msbank[:], AF.Ln, [Bms], [Btmpr[i]], scale=1.0 / n_feat, bias=EPS)
        ACT(msbank[:], tmpr[i][:], AF.Exp, [Btmpr[i]], [Bms], scale=-0.5)

    def stage1(tile_idx):
        msb, Bms = statbank()
        for blk in range(4):
            si = xtm_ctr[0] % 2
            xtm_ctr[0] += 1
            r0 = tile_idx * T + blk * 128
            xt_, bx_ = xtm[si], Bxtm[si]
            S.dma("sp", lambda e, xt_=xt_, r0=r0: e.dma_start(out=xt_[:], in_=xs[r0:r0 + 128, :]), writes=[bx_], sembuf=bx_)
            qi = rot["sq"] % 2
            rot["sq"] += 1
            for half in range(2):
                bk_, Bbk_ = nb()
                TR([(bk_[:, j * 128:(j + 1) * 128], xt_[:, (half * 4 + j) * 128:(half * 4 + j + 1) * 128]) for j in range(4)],
                   identf[:], [bx_, Bc], [Bbk_])
                xv = xT[:].rearrange("p (k t) -> p k t", k=NKC)[:, half * 4:half * 4 + 4, blk * 128:(blk + 1) * 128]
                if s1_steps >= 2:
                    COPY("dve", xv, bk_[:].rearrange("p (k t) -> p k t", k=4), [Bbk_], BxT[half * 4:half * 4 + 4])
                if s1_steps >= 3:
                    ACT(sq[qi][:, half * 512:(half + 1) * 512], bk_[:], AF.Square, [Bbk_], [Bsq[qi]])
            if s1_steps >= 4:
                MM([(msb[:, blk * 128:(blk + 1) * 128], onesb[:], sq[qi][:, k * 128:(k + 1) * 128], k == 0, k == NKC - 1) for k in range(NKC)],
                   [Bsq[qi], Bc], [Bms])
        if s1_steps >= 5:
            rmsnorm_rstd(msb, Bms, D)
        for k in range(NKC if s1_steps >= 6 else 0):
            STT(hT[:, k * T:(k + 1) * T], xT[:, k * T:(k + 1) * T], pmw[:, k:k + 1], msb[:], ALU.mult, ALU.mult,
                [BxT[k], Bms, Bc], [BhT[k]])

    def fm_proj(rt, rb, j, kcn, rhs_t, Brhs, stride=512, off=0):
        bk_, Bbk_ = nb()
        MM([(bk_[:], rt[:, kc * stride + off + j * 128: kc * stride + off + (j + 1) * 128], rhs_t[:, kc * T:(kc + 1) * T], kc == 0, kc == kcn - 1)
            for kc in range(kcn)], [rb] + list(Brhs), [Bbk_])
        return bk_, Bbk_

    def gate_unit(d, h, zb_, Bz_):
        u = d * 4 + h
        S.tag = S.tag.split("|")[0] + f"|gate d{d} h{h}"
        g_ = gun[gun_ctr[0] % 3]
        gun_ctr[0] += 1
        ACT(g_["sp"], zb_[:], AF.Exp, [Bz_], [g_["Bsp"]])
        ACT(g_["sp"], g_["sp"], AF.Ln, [g_["Bsp"]], [g_["Bsp"]], bias=1.0)
        ACT(g_["sp"], g_["sp"], AF.Exp, [g_["Bsp"]], [g_["Bsp"]], scale=-1.0)
        ACT(g_["lf"], g_["sp"], AF.Ln, [g_["Bsp"], Bc], [g_["Blf"]], scale=noml[:, u:u + 1], bias=1.0)
        if d == 0:
            S.op("dve", lambda e: e.tensor_tensor_scan(out=g_["c"], data0=mF, data1=g_["lf"], initial=0.0, op0=ALU.mult, op1=ALU.add),
                 reads=[g_["Blf"], Bc], writes=[g_["Bc"]])
        else:
            S.op("dve", lambda e: e.tensor_tensor_scan(out=g_["c"][:, ::-1], data0=mB[:, ::-1], data1=g_["lf"][:, ::-1], initial=0.0,
                                                       op0=ALU.mult, op1=ALU.add),
                 reads=[g_["Blf"], Bc], writes=[g_["Bc"]])
        ACT(g_["e2"], g_["c"], AF.Exp, [g_["Bc"]], [g_["Be2"]], scale=-1.0)
        cl = g_["c"][:, 63::64] if d == 0 else g_["c"][:, 0::64]
        ACT(dec[d][:, h * 8:(h + 1) * 8], cl, AF.Exp, [g_["Bc"]], [Bdec[d]])
        STT(kr[u], g_["sp"], oml[:, u:u + 1], g_["e2"], ALU.mult, ALU.mult, [g_["Bsp"], g_["Be2"], Bc], [Bkr[u]])
        return g_

    def stage2(phase1):
        if phase1:
            rt_zb, rb_zb = p1_slots[0]
            rt_i, rb_i = p1_slots[1]
        else:
            rt_zf, rb_zf = load_slot(G_ZF, 4096)
            rt_zb, rb_zb = load_slot(G_ZB, 4096)
            rt_q, rb_q = load_slot(G_Q, 4096)
        for h in range(NH):
            gus = {}
            for d in ((1,) if phase1 else (0, 1)):
                rt, rb = (rt_zf, rb_zf) if d == 0 else (rt_zb, rb_zb)
                zb_, Bz_ = fm_proj(rt, rb, h, NKC, hT, BhT)
                gus[d] = gate_unit(d, h, zb_, Bz_)
            if not phase1:
                for d in (0, 1):
                    g_ = gus[d]
                    ACT(g_["e1"], g_["c"], AF.Exp, [g_["Bc"]], [g_["Be1"]])
                qb_, Bq_ = fm_proj(rt_q, rb_q, h, NKC, hT, BhT)
                for d in (0, 1):
                    g_ = gus[d]
                    TT("dve", qr[d * 4 + h], qb_[:], g_["e1"], ALU.mult, [Bq_, g_["Be1"]], [Bqr[d * 4 + h]])
        if not phase1:
            rt_i, rb_i = load_slot(G_I, 4096)
            rt_v, rb_v = load_slot(G_V, 4096)
        for blk in range(4):
            bk_, Bbk_ = nb()
            MM([(bk_[:], hT[:, kc * T + blk * 128: kc * T + (blk + 1) * 128], rt_i[:, kc * 512:(kc + 1) * 512], kc == 0, kc == NKC - 1)
                for kc in range(NKC)], [rb_i] + BhT, [Bbk_])
            COPY("act", vTM[blk], bk_[:], [Bbk_], [BvTM[blk]])
        if phase1:
            return
        for blk in range(4):
            bk_, Bbk_ = nb()
            MM([(bk_[:], hT[:, kc * T + blk * 128: kc * T + (blk + 1) * 128], rt_v[:, kc * 512:(kc + 1) * 512], kc == 0, kc == NKC - 1)
                for kc in range(NKC)], [rb_v] + BhT, [Bbk_])
            gi = rot["gv"] % 4
            rot["gv"] += 1
            ACT(gv[gi][:], bk_[:], AF.Gelu_apprx_tanh, [Bbk_], [Bgv[gi]])
            st_, Bst_ = stat[gi], Bstat[gi]
            S.op("dve", lambda e, st_=st_, gi=gi: e.bn_stats(out=st_[:, 0:6], in_=gv[gi][:]), reads=[Bgv[gi]], writes=[Bst_])
            S.op("dve", lambda e, st_=st_: e.bn_aggr(out=st_[:, 6:8], in_=st_[:, 0:6]), reads=[Bst_], writes=[Bst_])
            stats_pending.append((blk, gi))
        rt_u, rb_u = load_slot(G_U, 4096)
        for j in range(4):
            bk_, Bbk_ = fm_proj(rt_u, rb_u, j, NKC, hT, BhT)
            ACT(guT[:, j * T:(j + 1) * T], bk_[:], AF.Gelu_apprx_tanh, [Bbk_], [BguT[j]])
        rt_g, rb_g = load_slot(G_G, 4096)
        for j in range(4):
            bk_, Bbk_ = fm_proj(rt_g, rb_g, j, NKC, hT, BhT)
            ACT(sgT[:, j * T:(j + 1) * T], bk_[:], AF.Silu, [Bbk_], [BsgT[j]])
        for (blk, gi) in stats_pending:
            st_, Bst_ = stat[gi], Bstat[gi]
            ACT(st_[:, 8:9], st_[:, 7:8], AF.Ln, [Bst_], [Bst_], bias=EPS)
            ACT(st_[:, 9:10], st_[:, 8:9], AF.Exp, [Bst_], [Bst_], scale=-0.5)
            STT(st_[:, 10:11], st_[:, 6:7], -1.0, st_[:, 9:10], ALU.mult, ALU.mult, [Bst_], [Bst_])
            TS("dve", xhat[:, blk * 512:(blk + 1) * 512], gv[gi][:], st_[:, 9:10], st_[:, 10:11], ALU.mult, ALU.add,
               [Bgv[gi], Bst_], [Bxhat[blk]])
        stats_pending.clear()

    stats_pending = []

    def state_chain(d, order, keep_bf):
        for n in order:
            blk, c = n // 2, n % 2
            if keep_bf:
                COPY("pool", Sbf[d][n], Sst[d][:], [BS[d]], [BSbf[d][n]])
            bk_, Bbk_ = nb()
            specs = []
            rows = slice(c * 64, (c + 1) * 64)
            for h in range(NH):
                specs.append((bk_[:, h * 128:(h + 1) * 128], krTM[d][blk][rows, h * 128:(h + 1) * 128],
                              vTM[blk][rows, h * 128:(h + 1) * 128], True, False))
                specs.append((bk_[:, h * 128:(h + 1) * 128], identf[:], Sst[d][:, h * 128:(h + 1) * 128], False, True))
            MM(specs, [BkrTM[d][blk], BvTM[blk], BS[d], Bc], [Bbk_])
            dv = dec[d][:].rearrange("p (h n) -> p h n", h=NH)[:, :, n:n + 1].to_broadcast([128, NH, 128])
            TT("dve", Sst[d][:].rearrange("p (h v) -> p h v", h=NH), bk_[:].rearrange("p (h v) -> p h v", h=NH), dv, ALU.mult,
               [Bbk_, Bdec[d]], [BS[d]])

    def kr_transposes(d):
        for blk in range(4):
            bk_, Bbk_ = nb()
            bkb = bk_[:].bitcast(BF16)
            TR([(bkb[:, h * 128:(h + 1) * 128], kr[d * 4 + h][:, blk * 128:(blk + 1) * 128]) for h in range(NH)],
               identb[:], [Bkr[d * 4 + h] for h in range(NH)] + [Bc], [Bbk_])
            COPY("dve", krTM[d][blk], bkb[:, 0:512], [Bbk_], [BkrTM[d][blk]])

    def stage3():
        for d in (0, 1):
            kr_transposes(d)
        if s3_steps < 2:
            return
        state_chain(0, range(8), True)
        state_chain(1, range(7, -1, -1), True)
        for blk in range(4 if s3_steps >= 3 else 0):
            pts = []
            for d in (0, 1):
                bk_, Bbk_ = nb()
                MM([(bk_[:, h * 128:(h + 1) * 128], kr[d * 4 + h][:, blk * 128:(blk + 1) * 128],
                     qr[d * 4 + h][:, blk * 128:(blk + 1) * 128], True, True) for h in range(NH)],
                   [Bkr[d * 4 + h] for h in range(NH)] + [Bqr[d * 4 + h] for h in range(NH)], [Bbk_])
                pi = rot["pt"] % 2
                TT("dve", PT[d][pi], bk_[:], maskF4 if d == 0 else maskB4, ALU.mult, [Bbk_, Bc], [BPT[d][pi]])
                pts.append((PT[d][pi], BPT[d][pi]))
            rot["pt"] += 1
            if s3_steps < 4:
                continue
            ob, Bob = nb()
            specs = []
            for h in range(NH):
                hs = slice(h * 128, (h + 1) * 128)
                specs.append((ob[:, hs], vTM[blk][:, hs], pts[0][0][:, hs], True, False))
                specs.append((ob[:, hs], vTM[blk][:, hs], pts[1][0][:, hs], False, False))
                k_ = 0
                for d in (0, 1):
                    for c in (0, 1):
                        k_ += 1
                        specs.append((ob[:, h * 128 + c * 64: h * 128 + (c + 1) * 64], Sbf[d][2 * blk + c][:, hs],
                                      qr[d * 4 + h][:, blk * 128 + c * 64: blk * 128 + (c + 1) * 64], False, k_ == 4))
            MM(specs, [BvTM[blk], pts[0][1], pts[1][1]] + [BSbf[d][2 * blk + c] for d in (0, 1) for c in (0, 1)] + Bqr, [Bob])
            if s3_steps < 5:
                continue
            oi = rot["osq"] % 2
            rot["osq"] += 1
            ACT(osq[oi][:], ob[:], AF.Square, [Bob], [Bosq[oi]])
            msb, Bms = nb()
            MM([(msb[:], onesb[:], osq[oi][:], True, True)], [Bosq[oi], Bc], [Bms])
            ti = rot["tmpr"] % 2
            rot["tmpr"] += 1
            ACT(tmpr[ti][:], msb[:], AF.Ln, [Bms], [Btmpr[ti]], scale=1.0 / HD, bias=EPS)
            ACT(rso[oi][:], tmpr[ti][:], AF.Exp, [Btmpr[ti]], [Brso[oi]], scale=-0.5)

            def fn(e, ob=ob, oi=oi):
                inst = None
                for h in range(NH):
                    hs = slice(h * 128, (h + 1) * 128)
                    inst = e.scalar_tensor_tensor(out=t1[oi][:, hs], in0=ob[:, hs], scalar=hgw[:, h:h + 1], in1=rso[oi][:, hs],
                                                  op0=ALU.mult, op1=ALU.mult)
                return inst
            S.op("dve", fn, reads=[Bob, Brso[oi], Bc], writes=[Bt1[oi]])
            TT("dve", onT[:].rearrange("p (h t) -> p h t", h=NH)[:, :, blk * 128:(blk + 1) * 128],
               t1[oi][:].rearrange("p (h t) -> p h t", h=NH),
               sgT[:].rearrange("p (h t) -> p h t", h=NH)[:, :, blk * 128:(blk + 1) * 128], ALU.mult,
               [Bt1[oi]] + BsgT, [BonT[blk]])

    def stage4():
        for blk in range(4):
            bk_, Bbk_ = nb()
            MM([(bk_[:, g * 128:(g + 1) * 128], xhat[:, blk * 512 + g * 128: blk * 512 + (g + 1) * 128], wstb[:, g * 128:(g + 1) * 128], True, True)
                for g in range(4)], [Bxhat[blk], Bc], [Bbk_])
            oi = rot["osq"] % 2
            rot["osq"] += 1

            def fn(e, bk_=bk_, oi=oi):
                inst = None
                for g in range(4):
                    gs = slice(g * 128, (g + 1) * 128)
                    inst = e.scalar_tensor_tensor(out=t1[oi][:, gs], in0=bk_[:, gs], scalar=lnw[:, g:g + 1], in1=extra[:, gs],
                                                  op0=ALU.mult, op1=ALU.add)
                return inst
            S.op("dve", fn, reads=[Bbk_, Bc], writes=[Bt1[oi]])
            TT("dve", sTt[:].rearrange("p (h t) -> p h t", h=4)[:, :, blk * 128:(blk + 1) * 128],
               t1[oi][:].rearrange("p (h t) -> p h t", h=4),
               guT[:].rearrange("p (h t) -> p h t", h=4)[:, :, blk * 128:(blk + 1) * 128], ALU.mult,
               [Bt1[oi]] + BguT, [BsT[blk]])

    def stage5():
        for j in range(2):
            rt_ga, rb_ga = load_slot(G_GA0 + j, 4096)
            rt_gb, rb_gb = load_slot(G_GB0 + j, 4096)
            rt_ab, rb_ab = load_slot(SL_AB + j, 4096)
            for jj in range(4):
                oc = j * 4 + jj
                si = rot["sg"] % 2
                rot["sg"] += 1
                mi = rot["m"] % 2
                rot["m"] += 1
                bga, Bbga = fm_proj(rt_ga, rb_ga, jj, NKC, hT, BhT)
                ACT(sga[si][:], bga[:], AF.Sigmoid, [Bbga], [Bsga[si]])
                bgb, Bbgb = fm_proj(rt_gb, rb_gb, jj, NKC, hT, BhT)
                ACT(sgb[si][:], bgb[:], AF.Sigmoid, [Bbgb], [Bsgb[si]])
                bya, Bbya = fm_proj(rt_ab, rb_ab, jj, 4, onT, BonT, stride=512, off=0)
                TT("dve", m1[mi], bya[:], sga[si][:], ALU.mult, [Bbya, Bsga[si]], [Bm1[mi]])
                byb, Bbyb = fm_proj(rt_ab, rb_ab, jj, 4, sTt, BsT, stride=512, off=2048)
                TT("dve", m2[mi], byb[:], sgb[si][:], ALU.mult, [Bbyb, Bsgb[si]], [Bm2[mi]])
                TT("pool", merged[oc], m1[mi], m2[mi], ALU.add, [Bm1[mi], Bm2[mi]], [Bmerged[oc]])

    def out_norm_residual(src, Bsrc, wvec, msb, Bms):
        for oc in range(8):
            ti = rot["tsc"] % 2
            rot["tsc"] += 1
            STT(tsc[ti], src[oc], wvec[:, oc:oc + 1], msb[:], ALU.mult, ALU.mult, [Bsrc[oc], Bms, Bc], [Btsc[ti]])
            TT("pool", xT[:, oc * T:(oc + 1) * T], xT[:, oc * T:(oc + 1) * T], tsc[ti], ALU.add, [BxT[oc], Btsc[ti]], [BxT[oc]])

    def stage6():
        msb, Bms = statbank()
        for j in range(2):
            rt_o, rb_o = load_slot(SL_O + j, 4096)
            for jj in range(4):
                oc = j * 4 + jj
                bk_, Bbk_ = nb()
                MM([(bk_[:], rt_o[:, kc * 512 + jj * 128: kc * 512 + (jj + 1) * 128], merged[kc], kc == 0, kc == 7) for kc in range(8)],
                   [rb_o] + Bmerged, [Bbk_])
                qi = rot["sq"] % 2
                rot["sq"] += 1
                ACT(sq[qi][:, 0:512], bk_[:], AF.Square, [Bbk_], [Bsq[qi]])
                COPY("dve", mix[oc], bk_[:], [Bbk_], [Bmix[oc]])
                MM([(msb[:], onesb[:], sq[qi][:, 0:512], oc == 0, oc == 7)], [Bsq[qi], Bc], [Bms])
        rmsnorm_rstd(msb, Bms, D)
        out_norm_residual(mix, Bmix, pmw2, msb, Bms)

    def stage7():
        msb, Bms = statbank()
        for k in range(NKC):
            qi = rot["sq"] % 2
            rot["sq"] += 1
            ACT(sq[qi][:, 0:512], xT[:, k * T:(k + 1) * T], AF.Square, [BxT[k]], [Bsq[qi]])
            MM([(msb[:], onesb[:], sq[qi][:, 0:512], k == 0, k == 7)], [Bsq[qi], Bc], [Bms])
        rmsnorm_rstd(msb, Bms, D)
        for k in range(NKC):
            STT(hT[:, k * T:(k + 1) * T], xT[:, k * T:(k + 1) * T], pfw[:, k:k + 1], msb[:], ALU.mult, ALU.mult,
                [BxT[k], Bms, Bc], [BhT[k]])

    def stage8():
        for j in range(11):
            rt, rb = load_slot(SL_GU + j, 4096)
            for half in range(2):
                bg_, Bbg_ = fm_proj(rt, rb, half, NKC, hT, BhT, stride=512, off=0)
                bu_, Bbu_ = fm_proj(rt, rb, half, NKC, hT, BhT, stride=512, off=256)
                si = rot["sgate"] % 2
                rot["sgate"] += 1
                ACT(sgate[si], bg_[:], AF.Silu, [Bbg_], [Bsgate[si]])
                TT("dve", hid[2 * j + half], bu_[:], sgate[si], ALU.mult, [Bbu_, Bsgate[si]], [Bhid[2 * j + half]])

    def stage9():
        msb, Bms = statbank()
        for oc in range(8):
            rt, rb = load_slot(SL_D + oc, NFC * 128)
            bk_, Bbk_ = nb()
            MM([(bk_[:], rt[:, kc * 128:(kc + 1) * 128], hid[kc], kc == 0, kc == NFC - 1) for kc in range(NFC)],
               [rb] + Bhid, [Bbk_])
            qi = rot["sq"] % 2
            rot["sq"] += 1
            ACT(sq[qi][:, 0:512], bk_[:], AF.Square, [Bbk_], [Bsq[qi]])
            COPY("dve", mix[oc], bk_[:], [Bbk_], [Bmix[oc]])
            MM([(msb[:], onesb[:], sq[qi][:, 0:512], oc == 0, oc == 7)], [Bsq[qi], Bc], [Bms])
        rmsnorm_rstd(msb, Bms, D)
        out_norm_residual(mix, Bmix, pfw2, msb, Bms)

    def stage10(tile_idx):
        for blk in range(4):
            si = xtm_ctr[0] % 2
            xtm_ctr[0] += 1
            for half in range(2):
                bk_, Bbk_ = nb()
                TR([(bk_[:, j * 128:(j + 1) * 128], xT[:, (half * 4 + j) * T + blk * 128:(half * 4 + j) * T + (blk + 1) * 128]) for j in range(4)],
                   identf[:], BxT[half * 4:half * 4 + 4] + [Bc], [Bbk_])
                COPY("act", xtm[si][:, half * 512:(half + 1) * 512], bk_[:], [Bbk_], [Bxtm[si]])
            r0 = tile_idx * T + blk * 128
            xt_ = xtm[si]
            store_ops.append(S.dma("sp", lambda e, xt_=xt_, r0=r0: e.dma_start(out=y[r0:r0 + 128, :], in_=xt_[:]),
                                   reads=[Bxtm[si]], sembuf=Bxtm[si]))

    Bbnd = Buf("bnd")
    S.op("pool", lambda e: e.memset(Sst[0][:], 0.0), writes=[BS[0]])
    S.op("pool", lambda e: e.memset(Sst[1][:], 0.0), writes=[BS[1]])
    p1_tiles = list(range(NTALL - 1, 0, -1))
    if n_p1_tiles is not None:
        p1_tiles = p1_tiles[:n_p1_tiles] if n_p1_tiles > 0 else []
    p1_slots = None
    for pc in pieces_A:
        emit_piece(pc)
    emit_piece_store()
    restB = list(pieces_B)
    per_tile = -(-len(restB) // max(len(p1_tiles), 1)) if p1_tiles else len(restB)
    if p1_tiles:
        p1_slots = [load_slot(G_ZB, 4096), load_slot(G_I, 4096)]
    def tg(t):
        S.tag = t

    for j in p1_tiles:
        tg(f"p1 t{j} cast")
        for _ in range(min(per_tile, len(restB))):
            emit_piece(restB.pop(0))
        tg(f"p1 t{j} s1")
        stage1(j)
        tg(f"p1 t{j} s2")
        stage2(True)
        tg(f"p1 t{j} s3")
        kr_transposes(1)
        state_chain(1, range(7, -1, -1), False)
        if j <= NT2:
            S.dma("sp", lambda e, j=j: e.dma_start(out=bnd[j - 1], in_=Sst[1][:]), reads=[BS[1]], writes=[Bbnd], sembuf=Bbnd)
    tg("cast rest")
    while restB:
        emit_piece(restB.pop(0))
    emit_piece_store()
    for j in range(NT2):
        if p1_tiles:
            S.dma("sp", lambda e, j=j: e.dma_start(out=Sst[1][:], in_=bnd[j]), reads=[Bbnd], writes=[BS[1]], sembuf=BS[1])
        else:
            S.op("pool", lambda e: e.memset(Sst[1][:], 0.0), writes=[BS[1]])
        tg(f"p2 t{j} s1")
        stage1(j)
        for si_, fn_ in ((2, lambda: stage2(False)), (3, stage3), (4, stage4), (5, stage5), (6, stage6), (7, stage7),
                         (8, stage8), (9, stage9)):
            if si_ <= last_stage:
                tg(f"p2 t{j} s{si_}")
                fn_()
        tg(f"p2 t{j} s10")
        if not skip_s10:
            stage10(j)
    S.final_wait("sp", (store_ops[-2:] if len(store_ops) >= 2 else store_ops) + [b.writer for b in Bwsc if b.writer is not None])
    S.emit()
    global LAST_NAMES
    LAST_NAMES = S.names
    return nc


def _host_consts():
    identf = np.eye(128, dtype=np.float32)
    s = np.arange(128)[:, None]
    t = np.arange(128)[None, :]
    same = (s // 64) == (t // 64)
    mf = (same & (s <= t)).astype(np.float32)
    mb = (same & (s >= t)).astype(np.float32)
    masks = np.zeros((4, 128, 512), np.float32)
    masks[0] = np.tile(mf, (1, 4))
    masks[1] = np.tile(mb, (1, 4))
    tt = np.arange(512)
    masks[2] = np.broadcast_to((tt % 64 != 0).astype(np.float32), (128, 512))
    masks[3] = np.broadcast_to((tt % 64 != 63).astype(np.float32), (128, 512))
    return identf, masks


def _weights_for(flip, w_in, lb_logits, sg_spatial_w, sg_spatial_b):
    q, i_, ff, fb, g, u, v = [w_in[:, k * 512:(k + 1) * 512] for k in range(7)]
    ga = w_in[:, 3584:4608]
    gb = w_in[:, 4608:5632]
    zf, zb = (fb, ff) if flip else (ff, fb)
    w_in_r = np.ascontiguousarray(np.concatenate([zf, zb, q, i_, v, g, u, ga, gb], axis=1))
    lbl = lb_logits[:, ::-1, :] if flip else lb_logits
    ws = sg_spatial_w
    bs = sg_spatial_b
    if flip:
        ws = ws[:, ::-1, ::-1]
        bs = bs[:, ::-1]
    wst = np.ascontiguousarray(np.transpose(ws, (2, 0, 1)))
    lblT = np.transpose(lbl.reshape(2, 2, 4, 128), (3, 0, 1, 2)).reshape(128, 16)
    return w_in_r, lblT, wst, np.ascontiguousarray(bs.reshape(1, 512))


_PROGRAM_CACHE = {}
LAST_NAMES = {}


def _run(x, pre_mix_w, w_in, lb_logits, hg_norm_w, sg_ln_w, sg_ln_b, sg_spatial_w, sg_spatial_b,
         w_proj_a, w_proj_b, w_out, post_mix_w, pre_ffn_w, w_gate, w_up, w_down, post_ffn_w, **build_kw):
    x = np.asarray(x, np.float32)
    B, L, _ = x.shape
    f = lambda a: np.ascontiguousarray(np.asarray(a, np.float32))
    identf, masks = _host_consts()
    vecs = np.stack([f(pre_mix_w)[0], f(post_mix_w)[0], f(pre_ffn_w)[0], f(post_ffn_w)[0]], axis=0)
    vecsT = np.transpose(vecs.reshape(4, 8, 128), (2, 0, 1)).reshape(128, 32)
    hgwT = np.transpose(f(hg_norm_w)[0].reshape(4, 128), (1, 0))
    lnwT = np.transpose(f(sg_ln_w)[0].reshape(4, 128), (1, 0))
    lnb_bc = np.ascontiguousarray(np.broadcast_to(f(sg_ln_b)[0][None, :], (128, 512)))
    common = dict(w_pa=f(w_proj_a)[0], w_pb=f(w_proj_b)[0], w_out=f(w_out)[0], w_gate=f(w_gate)[0], w_up=f(w_up)[0],
                  w_down=f(w_down)[0], lnb_bc=lnb_bc, identf=identf, masks=masks)
    per_flip = []
    for flip in (False, True):
        w_in_r, lbl, wst, bs = _weights_for(flip, f(w_in)[0], f(lb_logits), f(sg_spatial_w)[0], f(sg_spatial_b)[0])
        smalls = np.ascontiguousarray(np.concatenate([vecsT, hgwT, lnwT, lbl], axis=1).astype(np.float32))
        per_flip.append(dict(w_in=w_in_r, smalls=smalls, wst=wst, bs=bs))
    in_maps = []
    for b in range(B):
        for flip in (False, True):
            xs = x[b, ::-1] if flip else x[b]
            m = dict(common)
            m.update(per_flip[int(flip)])
            m["xs"] = np.ascontiguousarray(xs)
            in_maps.append(m)
    key = (L, tuple(sorted(build_kw.items())))
    nc = build_program(L, **build_kw)
    res = run_bass_kernel_spmd(nc, in_maps, core_ids=list(range(len(in_maps))))
    own = L // 2
    out = np.zeros((B, L, D), np.float32)
    for b in range(B):
        y0 = res.results[2 * b]["y"]
        y1 = res.results[2 * b + 1]["y"]
        out[b, :own] = y0
        out[b, own:] = y1[::-1]
    return out


def kernel(**inputs):
    return _run(**inputs)
```

```python
import numpy as np
import ml_dtypes
import concourse.bass as bass
import concourse.mybir as mybir
from concourse.bass_utils import run_bass_kernel_spmd

F32 = mybir.dt.float32
BF16 = mybir.dt.bfloat16
AF = mybir.ActivationFunctionType
ALU = mybir.AluOpType

D = 1024
NH = 4
HD = 128
DFF = 2816
EPS = 1e-6
T = 512
NKC = 8
NFC = 22
G_ZF, G_ZB, G_Q, G_I, G_V, G_G, G_U, G_GA0, G_GA1, G_GB0, G_GB1 = range(11)
SL_AB = 11
SL_O = 13
SL_GU = 15
SL_D = 26
NSLOT = 34
NRING = 4


class Buf:
    __slots__ = ("name", "writer", "readers", "dsem", "dtotal", "alias", "excl")

    def __init__(self, name, excl=False):
        self.name = name
        self.excl = excl
        self.writer = None
        self.readers = []
        self.dsem = None
        self.dtotal = 0
        self.alias = ()


class Op:
    __slots__ = ("eng", "fn", "deps", "milestone", "count", "is_dma", "dma_sem", "dma_val", "tag")

    def __init__(self, eng, fn, deps):
        self.eng = eng
        self.fn = fn
        self.deps = deps
        self.milestone = False
        self.count = None
        self.is_dma = False
        self.dma_sem = None
        self.dma_val = 0


class Sched:
    ENGS = ("pe", "act", "dve", "pool", "sp")

    def __init__(self, nc):
        self.nc = nc
        self.ops = {e: [] for e in self.ENGS}
        self.sems = {e: nc.alloc_semaphore("sem_" + e) for e in self.ENGS}
        self.tag = ""
        self.names = {}

    def _deps(self, reads, writes):
        deps = []
        for b in reads:
            if b.writer is not None:
                deps.append(b.writer)
            if b.excl:
                deps.extend(b.readers)
        for b in writes:
            if b.writer is not None:
                deps.append(b.writer)
            deps.extend(b.readers)
            for a in b.alias:
                if a.writer is not None:
                    deps.append(a.writer)
                deps.extend(a.readers)
        return deps

    def _update(self, op, reads, writes):
        for b in writes:
            b.writer = op
            b.readers = []
        for b in reads:
            if b in writes:
                continue
            if not op.is_dma:
                b.readers = [r for r in b.readers if r.is_dma or r.eng != op.eng]
            b.readers.append(op)

    def op(self, eng, fn, reads=(), writes=()):
        o = Op(eng, fn, self._deps(reads, writes))
        o.tag = self.tag
        self.ops[eng].append(o)
        self._update(o, reads, writes)
        return o

    def dma(self, eng, fn, reads=(), writes=(), sembuf=None):
        o = Op(eng, fn, self._deps(reads, writes))
        o.tag = self.tag
        o.is_dma = True
        if sembuf.dsem is None:
            sembuf.dsem = self.nc.alloc_semaphore("dsem_" + sembuf.name)
        sembuf.dtotal += 16
        o.dma_sem = sembuf.dsem
        o.dma_val = sembuf.dtotal
        self.ops[eng].append(o)
        self._update(o, reads, writes)
        return o

    def final_wait(self, eng, deps):
        o = Op(eng, None, list(deps))
        self.ops[eng].append(o)
        return o

    def emit(self):
        nc = self.nc
        for e in self.ENGS:
            for o in self.ops[e]:
                for d in o.deps:
                    if not d.is_dma:
                        d.milestone = True
        for e in self.ENGS:
            c = 0
            for o in self.ops[e]:
                if o.milestone and not o.is_dma:
                    c += 1
                    o.count = c

        def replay(e):
            def body(engine):
                seen = {}
                for o in self.ops[e]:
                    need = {}
                    for d in o.deps:
                        if d.is_dma:
                            key, val, sem = ("d", id(d.dma_sem)), d.dma_val, d.dma_sem
                        else:
                            key, val, sem = ("e", d.eng), d.count, self.sems[d.eng]
                        if seen.get(key, 0) >= val:
                            continue
                        if key not in need or need[key][1] < val:
                            need[key] = (sem, val)
                    for key, (sem, val) in need.items():
                        engine.wait_ge(sem, val)
                        seen[key] = val
                    if o.fn is None:
                        continue
                    inst = o.fn(engine)
                    try:
                        self.names[inst.ins.name] = o.tag
                    except Exception:
                        pass
                    if o.is_dma:
                        inst.then_inc(o.dma_sem, 16)
                    elif o.milestone:
                        inst.then_inc(self.sems[e], 1)
            return body

        with nc.Block() as block:
            block.tensor(replay("pe"))
            block.scalar(replay("act"))
            block.vector(replay("dve"))
            block.gpsimd(replay("pool"))
            block.sync(replay("sp"))


def build_program(L, n_own_tiles=None, n_p1_tiles=None, debug=False, last_stage=10, do_cast=1, skip_s10=0, s1_steps=9, s3_steps=9):
    own = L // 2
    NT2 = own // T if n_own_tiles is None else n_own_tiles
    NTALL = L // T
    nc = bass.Bass("TRN2", target_bir_lowering=False)
    S = Sched(nc)

    def din(name, shape, dt=F32):
        return nc.dram_tensor(name, list(shape), dt, kind="ExternalInput").ap()

    xs = din("xs", [L, D])
    w_in = din("w_in", [D, 5632])
    w_pa = din("w_pa", [512, D])
    w_pb = din("w_pb", [512, D])
    w_out = din("w_out", [D, D])
    w_gate = din("w_gate", [D, DFF])
    w_up = din("w_up", [D, DFF])
    w_down = din("w_down", [DFF, D])
    smalls_d = din("smalls", [128, 56])
    lnb_d = din("lnb_bc", [128, 512])
    wst_d = din("wst", [128, 4, 128])
    bs_d = din("bs", [1, 512])
    identf_d = din("identf", [128, 128])
    masks_d = din("masks", [4, 128, 512])
    y = nc.dram_tensor("y", [own, D], F32, kind="ExternalOutput").ap()
    wsc = nc.dram_tensor("wsc", [NSLOT, 128, 4096], BF16).ap()
    bnd = nc.dram_tensor("bnd", [max(NT2, 1), 128, 512], F32).ap()

    def sb(name, cols, dt=F32, parts=128):
        return nc.alloc_sbuf_tensor(name + "_sb", [parts, cols], dt)

    Bc = Buf("const")
    identf = sb("identf", 128)
    identb = sb("identb", 128, BF16)
    onesb = sb("onesb", 128, BF16)
    onesf = sb("onesf", 128)
    masks = sb("masks", 4 * 512)
    maskF4, maskB4 = masks[:, 0:512], masks[:, 512:1024]
    mF, mB = masks[:, 1024:1536], masks[:, 1536:2048]
    smalls = sb("smalls", 56)
    vecsT = smalls[:, 0:32]
    hgw = smalls[:, 32:36]
    lnw = smalls[:, 36:40]
    lblT = smalls[:, 40:56]
    oml = sb("oml", 8)
    noml = sb("noml", 8)
    lnb_bc = sb("lnb_bc", 512)
    wstf = sb("wstf", 512)
    wstb = sb("wstb", 512, BF16)
    bs_sb = sb("bs_sb", 512, F32, parts=1)
    extra = sb("extra", 512)

    def cdma(out, in_):
        S.dma("sp", lambda e: e.dma_start(out=out, in_=in_), writes=[Bc], sembuf=Bc)

    cdma(identf[:], identf_d)
    cdma(masks[:].rearrange("p (a c) -> p a c", a=4), masks_d.rearrange("a p c -> p a c"))
    cdma(smalls[:], smalls_d)
    cdma(lnb_bc[:], lnb_d)
    cdma(wstf[:].rearrange("p (g t) -> p g t", g=4), wst_d)
    cdma(bs_sb[:], bs_d)

    Bwsc = [Buf(f"wsc{i}") for i in range(NSLOT)]
    w_in_v = w_in.rearrange("(kc p) c -> p kc c", p=128)
    pa_v = w_pa.rearrange("(kc p) c -> p kc c", p=128)
    pb_v = w_pb.rearrange("(kc p) c -> p kc c", p=128)
    wo_v = w_out.rearrange("(kc p) c -> p kc c", p=128)
    wg_v = w_gate.rearrange("(kc p) c -> p kc c", p=128)
    wu_v = w_up.rearrange("(kc p) c -> p kc c", p=128)
    wd_v = w_down.rearrange("(kc p) c -> p kc c", p=128)
    pieces_A, pieces_B = [], []

    def win_pieces(g, lst):
        for k0 in (0, 4):
            lst.append((g, k0 * 512, 2048, [(0, 4, 512, 512, w_in_v[:, k0:k0 + 4, g * 512:(g + 1) * 512])]))

    win_pieces(G_ZB, pieces_A)
    win_pieces(G_I, pieces_A)
    for g in range(11):
        if g not in (G_ZB, G_I):
            win_pieces(g, pieces_B)
    for j in range(2):
        pieces_B.append((SL_AB + j, 0, 2048, [(0, 4, 512, 512, pa_v[:, :, j * 512:(j + 1) * 512])]))
        pieces_B.append((SL_AB + j, 2048, 2048, [(0, 4, 512, 512, pb_v[:, :, j * 512:(j + 1) * 512])]))
    for j in range(2):
        for k0 in (0, 4):
            pieces_B.append((SL_O + j, k0 * 512, 2048, [(0, 4, 512, 512, wo_v[:, k0:k0 + 4, j * 512:(j + 1) * 512])]))
    for j in range(11):
        for k0 in (0, 4):
            pieces_B.append((SL_GU + j, k0 * 512, 2048, [(0, 4, 256, 512, wg_v[:, k0:k0 + 4, j * 256:(j + 1) * 256]),
                                                       (256, 4, 256, 512, wu_v[:, k0:k0 + 4, j * 256:(j + 1) * 256])]))
    for oc in range(8):
        pieces_B.append((SL_D + oc, 0, 2048, [(0, 16, 128, 128, wd_v[:, 0:16, oc * 128:(oc + 1) * 128])]))
        pieces_B.append((SL_D + oc, 2048, 768, [(0, 6, 128, 128, wd_v[:, 16:22, oc * 128:(oc + 1) * 128])]))

    xT = sb("xT", NKC * T)
    BxT = [Buf(f"xT{k}") for k in range(NKC)]
    hT = sb("hT", NKC * T, BF16)
    BhT = [Buf(f"hT{k}") for k in range(NKC)]
    sq = [sb(f"sq{i}", 1024, BF16) for i in range(2)]
    Bsq = [Buf(f"sq{i}") for i in range(2)]
    xtm = [sb(f"xtm{i}", D) for i in range(2)]
    Bxtm = [Buf(f"xtm{i}") for i in range(2)]
    Sst = [sb(f"Sst{d}", 512) for d in range(2)]
    BS = [Buf(f"S{d}") for d in range(2)]
    dec = [sb(f"dec{d}", 32) for d in range(2)]
    Bdec = [Buf(f"dec{d}") for d in range(2)]
    ring = [sb(f"ring{i}", 4096, BF16) for i in range(NRING)]
    Bring = [Buf(f"ring{i}") for i in range(NRING)]
    sgT = sb("sgT", 4 * T, BF16)
    BsgT = [Buf(f"sgT{j}") for j in range(4)]
    guT = sb("guT", 4 * T, BF16)
    BguT = [Buf(f"guT{j}") for j in range(4)]
    xhat = sb("xhat", 4 * 512, BF16)
    Bxhat = [Buf(f"xhat{b}") for b in range(4)]
    gv = [sb(f"gv{i}", 512) for i in range(4)]
    Bgv = [Buf(f"gv{i}") for i in range(4)]
    stat = [sb(f"stat{i}", 16) for i in range(4)]
    Bstat = [Buf(f"stat{i}") for i in range(4)]
    onT = sb("onT", 4 * T, BF16)
    BonT = [Buf(f"onT{b}") for b in range(4)]
    sTt = sb("sT", 4 * T, BF16)
    BsT = [Buf(f"sT{b}") for b in range(4)]
    sga = [sb(f"sga{i}", T, BF16) for i in range(2)]
    Bsga = [Buf(f"sga{i}") for i in range(2)]
    sgb = [sb(f"sgb{i}", T, BF16) for i in range(2)]
    Bsgb = [Buf(f"sgb{i}") for i in range(2)]
    osq = [sb(f"osq{i}", 512, BF16) for i in range(2)]
    Bosq = [Buf(f"osq{i}") for i in range(2)]
    rso = [sb(f"rso{i}", 512) for i in range(2)]
    Brso = [Buf(f"rso{i}") for i in range(2)]
    t1 = [sb(f"t1_{i}", 512, BF16) for i in range(2)]
    Bt1 = [Buf(f"t1_{i}") for i in range(2)]
    tmpr = [sb(f"tmpr{i}", 512) for i in range(2)]
    Btmpr = [Buf(f"tmpr{i}") for i in range(2)]

    ARENA_B = 72 * 1024
    arena = sb("arena", ARENA_B // 2, BF16)

    def av(off, nbytes, dt):
        a = arena[:, off // 2:(off + nbytes) // 2]
        if dt == F32:
            a = a.bitcast(F32)
        return a

    K = 1024
    gun = []
    for i in range(3):
        base = i * 8 * K
        gun.append(dict(sp=av(base, 2 * K, F32), lf=av(base + 2 * K, 2 * K, F32), c=av(base + 4 * K, 2 * K, F32),
                        e1=av(base + 6 * K, K, BF16), e2=av(base + 7 * K, K, BF16),
                        Bsp=Buf(f"sp{i}"), Blf=Buf(f"lf{i}"), Bc=Buf(f"c{i}"), Be1=Buf(f"e1{i}"), Be2=Buf(f"e2{i}")))
    kr = [av(24 * K + u * K, K, BF16) for u in range(8)]
    Bkr = [Buf(f"kr{u}") for u in range(8)]
    qr = [av(32 * K + u * K, K, BF16) for u in range(8)]
    Bqr = [Buf(f"qr{u}") for u in range(8)]
    krTM = [[av(40 * K + (d * 4 + b) * K, K, BF16) for b in range(4)] for d in range(2)]
    BkrTM = [[Buf(f"krTM{d}{b}") for b in range(4)] for d in range(2)]
    vTM = [av(48 * K + b * K, K, BF16) for b in range(4)]
    BvTM = [Buf(f"vTM{b}") for b in range(4)]
    PT = [[av(52 * K + (d * 2 + i) * K, K, BF16) for i in range(2)] for d in range(2)]
    BPT = [[Buf(f"PT{d}{i}") for i in range(2)] for d in range(2)]
    Sbf = [[av(56 * K + (d * 8 + n) * K, K, BF16) for n in range(8)] for d in range(2)]
    BSbf = [[Buf(f"Sbf{d}{n}") for n in range(8)] for d in range(2)]
    hid = [av(j * K, K, BF16) for j in range(NFC)]
    Bhid = [Buf(f"hid{j}") for j in range(NFC)]
    mix = [av(22 * K + oc * 2 * K, 2 * K, F32) for oc in range(8)]
    Bmix = [Buf(f"mix{oc}") for oc in range(8)]
    merged = [av(38 * K + oc * K, K, BF16) for oc in range(8)]
    Bmerged = [Buf(f"mg{oc}") for oc in range(8)]
    m1 = [av(46 * K + i * 2 * K, 2 * K, F32) for i in range(2)]
    Bm1 = [Buf(f"m1{i}") for i in range(2)]
    m2 = [av(50 * K + i * 2 * K, 2 * K, F32) for i in range(2)]
    Bm2 = [Buf(f"m2{i}") for i in range(2)]
    tsc = [av(54 * K + i * 2 * K, 2 * K, F32) for i in range(2)]
    Btsc = [Buf(f"tsc{i}") for i in range(2)]
    sgate = [av(58 * K + i * K, K, BF16) for i in range(2)]
    Bsgate = [Buf(f"sgate{i}") for i in range(2)]

    early = []
    for g_ in gun:
        early += [g_["Bsp"], g_["Blf"], g_["Bc"], g_["Be1"], g_["Be2"]]
    early += Bkr + Bqr + BkrTM[0] + BkrTM[1] + BvTM + BPT[0] + BPT[1] + BSbf[0] + BSbf[1]
    late = Bhid + Bmix + Bmerged + Bm1 + Bm2 + Btsc + Bsgate
    for b in early:
        b.alias = tuple(late)
    for b in late:
        b.alias = tuple(early)

    stf = [av(56 * K + i * 8 * K, 8 * K, F32) for i in range(2)]
    Bstf = [Buf(f"stf{i}") for i in range(2)]
    stb = [av(32 * K + i * 4 * K, 4 * K, BF16) for i in range(2)]
    Bstb = [Buf(f"stb{i}") for i in range(2)]
    for b_ in Bstf:
        b_.alias = tuple(BSbf[0] + BSbf[1] + late)
    for b_ in Bstb:
        b_.alias = tuple(Bqr + late)
    for b_ in BSbf[0] + BSbf[1]:
        b_.alias = tuple(list(b_.alias) + Bstf)
    for b_ in Bqr:
        b_.alias = tuple(list(b_.alias) + Bstb)
    for b_ in late:
        b_.alias = tuple(list(b_.alias) + Bstf + Bstb)
    piece_ctr = [0]
    pending_store = []

    def emit_piece_store():
        if pending_store:
            (slot, doff, n, i) = pending_store.pop(0)
            S.dma("sp", lambda e: e.dma_start(out=wsc[slot][:, doff:doff + n], in_=stb[i][:, 0:n]),
                  reads=[Bstb[i]], writes=[Bwsc[slot]], sembuf=Bstb[i])

    def emit_piece(piece):
        if not do_cast:
            return
        (slot, doff, n, parts) = piece
        i = piece_ctr[0] % 2
        piece_ctr[0] += 1
        for (soff, kcn, cols, rowlen, src) in parts:
            dstv = stf[i][:, 0:kcn * rowlen].rearrange("p (k c) -> p k c", c=rowlen)[:, :, soff:soff + cols]
            S.dma("sp", lambda e, dstv=dstv, src=src: e.dma_start(out=dstv, in_=src), writes=[Bstf[i]], sembuf=Bstf[i])
        emit_piece_store()
        S.op("pool", lambda e: e.tensor_copy(out=stb[i][:, 0:n], in_=stf[i][:, 0:n]), reads=[Bstf[i]], writes=[Bstb[i]])
        pending_store.append((slot, doff, n, i))

    banks = [nc.alloc_psum_tensor(f"bank{i}", [128, 512], F32) for i in range(8)]
    Bbank = [Buf(f"bank{i}", excl=True) for i in range(8)]
    bank_ctr = [0]

    def nb():
        i = bank_ctr[0] % 7
        bank_ctr[0] += 1
        return banks[i], Bbank[i]

    def statbank():
        return banks[7], Bbank[7]

    def ACT(out, in_, func, r, w, scale=None, bias=None, accum=None):
        kw = {}
        if scale is not None:
            kw["scale"] = scale
        if bias is not None:
            kw["bias"] = bias
        if accum is not None:
            kw["accum_out"] = accum
        return S.op("act", lambda e: e.activation(out=out, in_=in_, func=func, **kw), reads=r, writes=w)

    def TT(eng, out, in0, in1, op, r, w):
        return S.op(eng, lambda e: e.tensor_tensor(out=out, in0=in0, in1=in1, op=op), reads=r, writes=w)

    def TS(eng, out, in0, s1, s2, op0, op1, r, w):
        if op1 is None:
            return S.op(eng, lambda e: e.tensor_scalar(out=out, in0=in0, scalar1=s1, scalar2=None, op0=op0), reads=r, writes=w)
        return S.op(eng, lambda e: e.tensor_scalar(out=out, in0=in0, scalar1=s1, scalar2=s2, op0=op0, op1=op1), reads=r, writes=w)

    def STT(out, in0, sc, in1, op0, op1, r, w):
        return S.op("dve", lambda e: e.scalar_tensor_tensor(out=out, in0=in0, scalar=sc, in1=in1, op0=op0, op1=op1), reads=r, writes=w)

    def COPY(eng, out, in_, r, w):
        if eng == "act":
            return ACT(out, in_, AF.Copy, r, w)
        return S.op(eng, lambda e: e.tensor_copy(out=out, in_=in_), reads=r, writes=w)

    def MM(specs, r, w):
        def fn(e):
            inst = None
            for (o_, l_, r_, st, sp_) in specs:
                inst = e.matmul(o_, l_, r_, start=st, stop=sp_)
            return inst
        return S.op("pe", fn, reads=r, writes=w)

    def TR(specs, ident, r, w):
        def fn(e):
            inst = None
            for (o_, i_) in specs:
                inst = e.transpose(o_, i_, ident)
            return inst
        return S.op("pe", fn, reads=r, writes=w)

    S.op("pool", lambda e: e.memset(onesf[:], 1.0), writes=[Bc], reads=[Bc])
    S.op("pool", lambda e: e.memset(onesb[:], 1.0), writes=[Bc], reads=[Bc])
    S.op("dve", lambda e: e.tensor_copy(out=identb[:], in_=identf[:]), writes=[Bc], reads=[Bc])
    lv = lblT.rearrange("p (l x) -> p l x", l=2)
    TT("dve", oml[:], lv[:, 0, :], lv[:, 1, :], ALU.subtract, [Bc], [Bc])
    ACT(oml[:], oml[:], AF.Exp, [Bc], [Bc])
    ACT(oml[:], oml[:], AF.Ln, [Bc], [Bc], bias=1.0)
    ACT(oml[:], oml[:], AF.Exp, [Bc], [Bc], scale=-1.0)
    TS("dve", noml[:], oml[:], -1.0, None, ALU.mult, None, [Bc], [Bc])
    COPY("act", wstb[:], wstf[:], [Bc], [Bc])
    bk, Bbk = nb()
    specs = []
    for g in range(4):
        specs.append((bk[:, g * 128:(g + 1) * 128], lnb_bc[:, g * 128:(g + 1) * 128], wstf[:, g * 128:(g + 1) * 128], True, False))
        specs.append((bk[:, g * 128:(g + 1) * 128], onesf[0:1, :], bs_sb[0:1, g * 128:(g + 1) * 128], False, True))
    MM(specs, [Bc], [Bbk])
    COPY("dve", extra[:], bk[:], [Bbk], [Bc])

    pmw = vecsT[:, 0:8]
    pmw2 = vecsT[:, 8:16]
    pfw = vecsT[:, 16:24]
    pfw2 = vecsT[:, 24:32]

    ring_ctr = [0]

    def load_slot(slot, ncols, wbuf=None):
        wbuf = Bwsc[slot]
        i = ring_ctr[0] % NRING
        ring_ctr[0] += 1
        rt, rb = ring[i], Bring[i]
        S.dma("sp", lambda e: e.dma_start(out=rt[:, 0:ncols], in_=wsc[slot][:, 0:ncols]),
              reads=[wbuf], writes=[rb], sembuf=rb)
        return rt, rb

    store_ops = []
    xtm_ctr = [0]
    gun_ctr = [0]
    rot = {"sq": 0, "gv": 0, "osq": 0, "m": 0, "sg": 0, "tsc": 0, "sgate": 0, "tmpr": 0, "pt": 0}

    def rmsnorm_rstd(msbank, Bms, n_feat):
        i = rot["tmpr"] % 2
        rot["tmpr"] += 1
        ACT(tmpr[i][:], msbank[:], AF.Ln, [Bms], [Btmpr[i]], scale=1.0 / n_feat, bias=EPS)
        ACT(msbank[:], tmpr[i][:], AF.Exp, [Btmpr[i]], [Bms], scale=-0.5)

    def stage1(tile_idx):
        msb, Bms = statbank()
        for blk in range(4):
            si = xtm_ctr[0] % 2
            xtm_ctr[0] += 1
            r0 = tile_idx * T + blk * 128
            xt_, bx_ = xtm[si], Bxtm[si]
            S.dma("sp", lambda e, xt_=xt_, r0=r0: e.dma_start(out=xt_[:], in_=xs[r0:r0 + 128, :]), writes=[bx_], sembuf=bx_)
            qi = rot["sq"] % 2
            rot["sq"] += 1
            for half in range(2):
                bk_, Bbk_ = nb()
                TR([(bk_[:, j * 128:(j + 1) * 128], xt_[:, (half * 4 + j) * 128:(half * 4 + j + 1) * 128]) for j in range(4)],
                   identf[:], [bx_, Bc], [Bbk_])
                xv = xT[:].rearrange("p (k t) -> p k t", k=NKC)[:, half * 4:half * 4 + 4, blk * 128:(blk + 1) * 128]
                if s1_steps >= 2:
                    COPY("dve", xv, bk_[:].rearrange("p (k t) -> p k t", k=4), [Bbk_], BxT[half * 4:half * 4 + 4])
                if s1_steps >= 3:
                    ACT(sq[qi][:, half * 512:(half + 1) * 512], bk_[:], AF.Square, [Bbk_], [Bsq[qi]])
            if s1_steps >= 4:
                MM([(msb[:, blk * 128:(blk + 1) * 128], onesb[:], sq[qi][:, k * 128:(k + 1) * 128], k == 0, k == NKC - 1) for k in range(NKC)],
                   [Bsq[qi], Bc], [Bms])
            yield
        if s1_steps >= 5:
            rmsnorm_rstd(msb, Bms, D)
        for k in range(NKC if s1_steps >= 6 else 0):
            STT(hT[:, k * T:(k + 1) * T], xT[:, k * T:(k + 1) * T], pmw[:, k:k + 1], msb[:], ALU.mult, ALU.mult,
                [BxT[k], Bms, Bc], [BhT[k]])

    def fm_proj(rt, rb, j, kcn, rhs_t, Brhs, stride=512, off=0):
        bk_, Bbk_ = nb()
        MM([(bk_[:], rt[:, kc * stride + off + j * 128: kc * stride + off + (j + 1) * 128], rhs_t[:, kc * T:(kc + 1) * T], kc == 0, kc == kcn - 1)
            for kc in range(kcn)], [rb] + list(Brhs), [Bbk_])
        return bk_, Bbk_

    def gate_unit(d, h, zb_, Bz_, dct=None, Bdct=None):
        u = d * 4 + h
        S.tag = S.tag.split("|")[0] + f"|gate d{d} h{h}"
        g_ = gun[gun_ctr[0] % 3]
        gun_ctr[0] += 1
        ACT(g_["sp"], zb_[:], AF.Exp, [Bz_], [g_["Bsp"]])
        ACT(g_["sp"], g_["sp"], AF.Ln, [g_["Bsp"]], [g_["Bsp"]], bias=1.0)
        ACT(g_["sp"], g_["sp"], AF.Exp, [g_["Bsp"]], [g_["Bsp"]], scale=-1.0)
        ACT(g_["lf"], g_["sp"], AF.Ln, [g_["Bsp"], Bc], [g_["Blf"]], scale=noml[:, u:u + 1], bias=1.0)
        if d == 0:
            S.op("dve", lambda e: e.tensor_tensor_scan(out=g_["c"], data0=mF, data1=g_["lf"], initial=0.0, op0=ALU.mult, op1=ALU.add),
                 reads=[g_["Blf"], Bc], writes=[g_["Bc"]])
        else:
            S.op("dve", lambda e: e.tensor_tensor_scan(out=g_["c"][:, ::-1], data0=mB[:, ::-1], data1=g_["lf"][:, ::-1], initial=0.0,
                                                       op0=ALU.mult, op1=ALU.add),
                 reads=[g_["Blf"], Bc], writes=[g_["Bc"]])
        ACT(g_["e2"], g_["c"], AF.Exp, [g_["Bc"]], [g_["Be2"]], scale=-1.0)
        cl = g_["c"][:, 63::64] if d == 0 else g_["c"][:, 0::64]
        if dct is None:
            dct, Bdct = dec[d], Bdec[d]
        ACT(dct[:, h * 8:(h + 1) * 8], cl, AF.Exp, [g_["Bc"]], [Bdct])
        STT(kr[u], g_["sp"], oml[:, u:u + 1], g_["e2"], ALU.mult, ALU.mult, [g_["Bsp"], g_["Be2"], Bc], [Bkr[u]])
        return g_

    def run(g):
        for _ in g:
            pass

    def seq(*gens):
        for g in gens:
            yield from g

    def interleave(*gens, weights=None):
        gens = list(gens)
        weights = list(weights) if weights else [1] * len(gens)
        alive = [True] * len(gens)
        while any(alive):
            for i, g in enumerate(gens):
                if not alive[i]:
                    continue
                for _ in range(weights[i]):
                    try:
                        next(g)
                    except StopIteration:
                        alive[i] = False
                        break

    def stage2a(phase1, vt=None, Bvt=None, dct=None, Bdct=None):
        vt = vTM if vt is None else vt
        Bvt = BvTM if Bvt is None else Bvt
        if phase1:
            rt_zb, rb_zb = p1_slots[0]
            rt_i, rb_i = p1_slots[1]
        else:
            rt_zf, rb_zf = load_slot(G_ZF, 4096)
            rt_zb, rb_zb = load_slot(G_ZB, 4096)
            rt_q, rb_q = load_slot(G_Q, 4096)
        for h in range(NH):
            gus = {}
            for d in ((1,) if phase1 else (0, 1)):
                rt, rb = (rt_zf, rb_zf) if d == 0 else (rt_zb, rb_zb)
                zb_, Bz_ = fm_proj(rt, rb, h, NKC, hT, BhT)
                if d == 1 and dct is not None:
                    gus[d] = gate_unit(d, h, zb_, Bz_, dct, Bdct)
                else:
                    gus[d] = gate_unit(d, h, zb_, Bz_)
                yield
            if not phase1:
                for d in (0, 1):
                    g_ = gus[d]
                    ACT(g_["e1"], g_["c"], AF.Exp, [g_["Bc"]], [g_["Be1"]])
                qb_, Bq_ = fm_proj(rt_q, rb_q, h, NKC, hT, BhT)
                for d in (0, 1):
                    g_ = gus[d]
                    TT("dve", qr[d * 4 + h], qb_[:], g_["e1"], ALU.mult, [Bq_, g_["Be1"]], [Bqr[d * 4 + h]])
                yield
        if not phase1:
            rt_i, rb_i = load_slot(G_I, 4096)
        for blk in range(4):
            bk_, Bbk_ = nb()
            MM([(bk_[:], hT[:, kc * T + blk * 128: kc * T + (blk + 1) * 128], rt_i[:, kc * 512:(kc + 1) * 512], kc == 0, kc == NKC - 1)
                for kc in range(NKC)], [rb_i] + BhT, [Bbk_])
            COPY("act", vt[blk], bk_[:], [Bbk_], [Bvt[blk]])
            yield

    def stage2b():
        rt_v, rb_v = load_slot(G_V, 4096)
        for blk in range(4):
            bk_, Bbk_ = nb()
            MM([(bk_[:], hT[:, kc * T + blk * 128: kc * T + (blk + 1) * 128], rt_v[:, kc * 512:(kc + 1) * 512], kc == 0, kc == NKC - 1)
                for kc in range(NKC)], [rb_v] + BhT, [Bbk_])
            gi = rot["gv"] % 4
            rot["gv"] += 1
            ACT(gv[gi][:], bk_[:], AF.Gelu_apprx_tanh, [Bbk_], [Bgv[gi]])
            st_, Bst_ = stat[gi], Bstat[gi]
            S.op("dve", lambda e, st_=st_, gi=gi: e.bn_stats(out=st_[:, 0:6], in_=gv[gi][:]), reads=[Bgv[gi]], writes=[Bst_])
            S.op("dve", lambda e, st_=st_: e.bn_aggr(out=st_[:, 6:8], in_=st_[:, 0:6]), reads=[Bst_], writes=[Bst_])
            stats_pending.append((blk, gi))
            yield
        rt_u, rb_u = load_slot(G_U, 4096)
        for j in range(4):
            bk_, Bbk_ = fm_proj(rt_u, rb_u, j, NKC, hT, BhT)
            ACT(guT[:, j * T:(j + 1) * T], bk_[:], AF.Gelu_apprx_tanh, [Bbk_], [BguT[j]])
            yield
        rt_g, rb_g = load_slot(G_G, 4096)
        for j in range(4):
            bk_, Bbk_ = fm_proj(rt_g, rb_g, j, NKC, hT, BhT)
            ACT(sgT[:, j * T:(j + 1) * T], bk_[:], AF.Silu, [Bbk_], [BsgT[j]])
            yield
        for (blk, gi) in stats_pending:
            st_, Bst_ = stat[gi], Bstat[gi]
            ACT(st_[:, 8:9], st_[:, 7:8], AF.Ln, [Bst_], [Bst_], bias=EPS)
            ACT(st_[:, 9:10], st_[:, 8:9], AF.Exp, [Bst_], [Bst_], scale=-0.5)
            STT(st_[:, 10:11], st_[:, 6:7], -1.0, st_[:, 9:10], ALU.mult, ALU.mult, [Bst_], [Bst_])
            TS("dve", xhat[:, blk * 512:(blk + 1) * 512], gv[gi][:], st_[:, 9:10], st_[:, 10:11], ALU.mult, ALU.add,
               [Bgv[gi], Bst_], [Bxhat[blk]])
            yield
        stats_pending.clear()

    stats_pending = []

    def state_chain(d, order, keep_bf, kt=None, Bkt=None, vt=None, Bvt=None, dct=None, Bdct=None, after=None):
        kt = krTM[d] if kt is None else kt
        Bkt = BkrTM[d] if Bkt is None else Bkt
        vt = vTM if vt is None else vt
        Bvt = BvTM if Bvt is None else Bvt
        dct = dec[d] if dct is None else dct
        Bdct = Bdec[d] if Bdct is None else Bdct
        for n in order:
            blk, c = n // 2, n % 2
            if keep_bf:
                COPY("pool", Sbf[d][n], Sst[d][:], [BS[d]], [BSbf[d][n]])
            bk_, Bbk_ = nb()
            specs = []
            rows = slice(c * 64, (c + 1) * 64)
            for h in range(NH):
                specs.append((bk_[:, h * 128:(h + 1) * 128], kt[blk][rows, h * 128:(h + 1) * 128],
                              vt[blk][rows, h * 128:(h + 1) * 128], True, False))
                specs.append((bk_[:, h * 128:(h + 1) * 128], identf[:], Sst[d][:, h * 128:(h + 1) * 128], False, True))
            MM(specs, [Bkt[blk], Bvt[blk], BS[d], Bc], [Bbk_])
            dv = dct[:].rearrange("p (h n) -> p h n", h=NH)[:, :, n:n + 1].to_broadcast([128, NH, 128])
            TT("dve", Sst[d][:].rearrange("p (h v) -> p h v", h=NH), bk_[:].rearrange("p (h v) -> p h v", h=NH), dv, ALU.mult,
               [Bbk_, Bdct], [BS[d]])
            yield
        if after is not None:
            after()
            yield

    def kr_transposes(d, kt=None, Bkt=None):
        kt = krTM[d] if kt is None else kt
        Bkt = BkrTM[d] if Bkt is None else Bkt
        for blk in range(4):
            bk_, Bbk_ = nb()
            bkb = bk_[:].bitcast(BF16)
            TR([(bkb[:, h * 128:(h + 1) * 128], kr[d * 4 + h][:, blk * 128:(blk + 1) * 128]) for h in range(NH)],
               identb[:], [Bkr[d * 4 + h] for h in range(NH)] + [Bc], [Bbk_])
            COPY("dve", kt[blk], bkb[:, 0:512], [Bbk_], [Bkt[blk]])
            yield

    def stage3():
        run(kr_transposes(0))
        run(kr_transposes(1))
        interleave(state_chain(0, range(8), True), state_chain(1, range(7, -1, -1), True), stage2b())
        for blk in range(4 if s3_steps >= 3 else 0):
            pts = []
            for d in (0, 1):
                bk_, Bbk_ = nb()
                MM([(bk_[:, h * 128:(h + 1) * 128], kr[d * 4 + h][:, blk * 128:(blk + 1) * 128],
                     qr[d * 4 + h][:, blk * 128:(blk + 1) * 128], True, True) for h in range(NH)],
                   [Bkr[d * 4 + h] for h in range(NH)] + [Bqr[d * 4 + h] for h in range(NH)], [Bbk_])
                pi = rot["pt"] % 2
                TT("dve", PT[d][pi], bk_[:], maskF4 if d == 0 else maskB4, ALU.mult, [Bbk_, Bc], [BPT[d][pi]])
                pts.append((PT[d][pi], BPT[d][pi]))
            rot["pt"] += 1
            if s3_steps < 4:
                continue
            ob, Bob = nb()
            specs = []
            for h in range(NH):
                hs = slice(h * 128, (h + 1) * 128)
                specs.append((ob[:, hs], vTM[blk][:, hs], pts[0][0][:, hs], True, False))
                specs.append((ob[:, hs], vTM[blk][:, hs], pts[1][0][:, hs], False, False))
                k_ = 0
                for d in (0, 1):
                    for c in (0, 1):
                        k_ += 1
                        specs.append((ob[:, h * 128 + c * 64: h * 128 + (c + 1) * 64], Sbf[d][2 * blk + c][:, hs],
                                      qr[d * 4 + h][:, blk * 128 + c * 64: blk * 128 + (c + 1) * 64], False, k_ == 4))
            MM(specs, [BvTM[blk], pts[0][1], pts[1][1]] + [BSbf[d][2 * blk + c] for d in (0, 1) for c in (0, 1)] + Bqr, [Bob])
            if s3_steps < 5:
                continue
            oi = rot["osq"] % 2
            rot["osq"] += 1
            ACT(osq[oi][:], ob[:], AF.Square, [Bob], [Bosq[oi]])
            msb, Bms = nb()
            MM([(msb[:], onesb[:], osq[oi][:], True, True)], [Bosq[oi], Bc], [Bms])
            ti = rot["tmpr"] % 2
            rot["tmpr"] += 1
            ACT(tmpr[ti][:], msb[:], AF.Ln, [Bms], [Btmpr[ti]], scale=1.0 / HD, bias=EPS)
            ACT(rso[oi][:], tmpr[ti][:], AF.Exp, [Btmpr[ti]], [Brso[oi]], scale=-0.5)

            def fn(e, ob=ob, oi=oi):
                inst = None
                for h in range(NH):
                    hs = slice(h * 128, (h + 1) * 128)
                    inst = e.scalar_tensor_tensor(out=t1[oi][:, hs], in0=ob[:, hs], scalar=hgw[:, h:h + 1], in1=rso[oi][:, hs],
                                                  op0=ALU.mult, op1=ALU.mult)
                return inst
            S.op("dve", fn, reads=[Bob, Brso[oi], Bc], writes=[Bt1[oi]])
            TT("dve", onT[:].rearrange("p (h t) -> p h t", h=NH)[:, :, blk * 128:(blk + 1) * 128],
               t1[oi][:].rearrange("p (h t) -> p h t", h=NH),
               sgT[:].rearrange("p (h t) -> p h t", h=NH)[:, :, blk * 128:(blk + 1) * 128], ALU.mult,
               [Bt1[oi]] + BsgT, [BonT[blk]])

    def stage4():
        for blk in range(4):
            bk_, Bbk_ = nb()
            MM([(bk_[:, g * 128:(g + 1) * 128], xhat[:, blk * 512 + g * 128: blk * 512 + (g + 1) * 128], wstb[:, g * 128:(g + 1) * 128], True, True)
                for g in range(4)], [Bxhat[blk], Bc], [Bbk_])
            oi = rot["osq"] % 2
            rot["osq"] += 1

            def fn(e, bk_=bk_, oi=oi):
                inst = None
                for g in range(4):
                    gs = slice(g * 128, (g + 1) * 128)
                    inst = e.scalar_tensor_tensor(out=t1[oi][:, gs], in0=bk_[:, gs], scalar=lnw[:, g:g + 1], in1=extra[:, gs],
                                                  op0=ALU.mult, op1=ALU.add)
                return inst
            S.op("dve", fn, reads=[Bbk_, Bc], writes=[Bt1[oi]])
            TT("dve", sTt[:].rearrange("p (h t) -> p h t", h=4)[:, :, blk * 128:(blk + 1) * 128],
               t1[oi][:].rearrange("p (h t) -> p h t", h=4),
               guT[:].rearrange("p (h t) -> p h t", h=4)[:, :, blk * 128:(blk + 1) * 128], ALU.mult,
               [Bt1[oi]] + BguT, [BsT[blk]])

    def stage5():
        for j in range(2):
            rt_ga, rb_ga = load_slot(G_GA0 + j, 4096)
            rt_gb, rb_gb = load_slot(G_GB0 + j, 4096)
            rt_ab, rb_ab = load_slot(SL_AB + j, 4096)
            for jj in range(4):
                oc = j * 4 + jj
                si = rot["sg"] % 2
                rot["sg"] += 1
                mi = rot["m"] % 2
                rot["m"] += 1
                bga, Bbga = fm_proj(rt_ga, rb_ga, jj, NKC, hT, BhT)
                ACT(sga[si][:], bga[:], AF.Sigmoid, [Bbga], [Bsga[si]])
                bgb, Bbgb = fm_proj(rt_gb, rb_gb, jj, NKC, hT, BhT)
                ACT(sgb[si][:], bgb[:], AF.Sigmoid, [Bbgb], [Bsgb[si]])
                bya, Bbya = fm_proj(rt_ab, rb_ab, jj, 4, onT, BonT, stride=512, off=0)
                TT("dve", m1[mi], bya[:], sga[si][:], ALU.mult, [Bbya, Bsga[si]], [Bm1[mi]])
                byb, Bbyb = fm_proj(rt_ab, rb_ab, jj, 4, sTt, BsT, stride=512, off=2048)
                TT("dve", m2[mi], byb[:], sgb[si][:], ALU.mult, [Bbyb, Bsgb[si]], [Bm2[mi]])
                TT("pool", merged[oc], m1[mi], m2[mi], ALU.add, [Bm1[mi], Bm2[mi]], [Bmerged[oc]])

    def out_norm_residual(src, Bsrc, wvec, msb, Bms):
        for oc in range(8):
            ti = rot["tsc"] % 2
            rot["tsc"] += 1
            STT(tsc[ti], src[oc], wvec[:, oc:oc + 1], msb[:], ALU.mult, ALU.mult, [Bsrc[oc], Bms, Bc], [Btsc[ti]])
            TT("pool", xT[:, oc * T:(oc + 1) * T], xT[:, oc * T:(oc + 1) * T], tsc[ti], ALU.add, [BxT[oc], Btsc[ti]], [BxT[oc]])

    def stage6():
        msb, Bms = statbank()
        for j in range(2):
            rt_o, rb_o = load_slot(SL_O + j, 4096)
            for jj in range(4):
                oc = j * 4 + jj
                bk_, Bbk_ = nb()
                MM([(bk_[:], rt_o[:, kc * 512 + jj * 128: kc * 512 + (jj + 1) * 128], merged[kc], kc == 0, kc == 7) for kc in range(8)],
                   [rb_o] + Bmerged, [Bbk_])
                qi = rot["sq"] % 2
                rot["sq"] += 1
                ACT(sq[qi][:, 0:512], bk_[:], AF.Square, [Bbk_], [Bsq[qi]])
                COPY("dve", mix[oc], bk_[:], [Bbk_], [Bmix[oc]])
                MM([(msb[:], onesb[:], sq[qi][:, 0:512], oc == 0, oc == 7)], [Bsq[qi], Bc], [Bms])
        rmsnorm_rstd(msb, Bms, D)
        out_norm_residual(mix, Bmix, pmw2, msb, Bms)

    def stage7():
        msb, Bms = statbank()
        for k in range(NKC):
            qi = rot["sq"] % 2
            rot["sq"] += 1
            ACT(sq[qi][:, 0:512], xT[:, k * T:(k + 1) * T], AF.Square, [BxT[k]], [Bsq[qi]])
            MM([(msb[:], onesb[:], sq[qi][:, 0:512], k == 0, k == 7)], [Bsq[qi], Bc], [Bms])
        rmsnorm_rstd(msb, Bms, D)
        for k in range(NKC):
            STT(hT[:, k * T:(k + 1) * T], xT[:, k * T:(k + 1) * T], pfw[:, k:k + 1], msb[:], ALU.mult, ALU.mult,
                [BxT[k], Bms, Bc], [BhT[k]])

    def stage8():
        for j in range(11):
            rt, rb = load_slot(SL_GU + j, 4096)
            for half in range(2):
                bg_, Bbg_ = fm_proj(rt, rb, half, NKC, hT, BhT, stride=512, off=0)
                bu_, Bbu_ = fm_proj(rt, rb, half, NKC, hT, BhT, stride=512, off=256)
                si = rot["sgate"] % 2
                rot["sgate"] += 1
                ACT(sgate[si], bg_[:], AF.Silu, [Bbg_], [Bsgate[si]])
                TT("dve", hid[2 * j + half], bu_[:], sgate[si], ALU.mult, [Bbu_, Bsgate[si]], [Bhid[2 * j + half]])

    def stage9():
        msb, Bms = statbank()
        for oc in range(8):
            rt, rb = load_slot(SL_D + oc, NFC * 128)
            bk_, Bbk_ = nb()
            MM([(bk_[:], rt[:, kc * 128:(kc + 1) * 128], hid[kc], kc == 0, kc == NFC - 1) for kc in range(NFC)],
               [rb] + Bhid, [Bbk_])
            qi = rot["sq"] % 2
            rot["sq"] += 1
            ACT(sq[qi][:, 0:512], bk_[:], AF.Square, [Bbk_], [Bsq[qi]])
            COPY("dve", mix[oc], bk_[:], [Bbk_], [Bmix[oc]])
            MM([(msb[:], onesb[:], sq[qi][:, 0:512], oc == 0, oc == 7)], [Bsq[qi], Bc], [Bms])
        rmsnorm_rstd(msb, Bms, D)
        out_norm_residual(mix, Bmix, pfw2, msb, Bms)

    def stage10(tile_idx):
        for blk in range(4):
            si = xtm_ctr[0] % 2
            xtm_ctr[0] += 1
            for half in range(2):
                bk_, Bbk_ = nb()
                TR([(bk_[:, j * 128:(j + 1) * 128], xT[:, (half * 4 + j) * T + blk * 128:(half * 4 + j) * T + (blk + 1) * 128]) for j in range(4)],
                   identf[:], BxT[half * 4:half * 4 + 4] + [Bc], [Bbk_])
                COPY("act", xtm[si][:, half * 512:(half + 1) * 512], bk_[:], [Bbk_], [Bxtm[si]])
            r0 = tile_idx * T + blk * 128
            xt_ = xtm[si]
            store_ops.append(S.dma("sp", lambda e, xt_=xt_, r0=r0: e.dma_start(out=y[r0:r0 + 128, :], in_=xt_[:]),
                                   reads=[Bxtm[si]], sembuf=Bxtm[si]))

    Bbnd = Buf("bnd")
    S.op("pool", lambda e: e.memset(Sst[0][:], 0.0), writes=[BS[0]])
    S.op("pool", lambda e: e.memset(Sst[1][:], 0.0), writes=[BS[1]])
    p1_tiles = list(range(NTALL - 1, 0, -1))
    if n_p1_tiles is not None:
        p1_tiles = p1_tiles[:n_p1_tiles] if n_p1_tiles > 0 else []
    p1_slots = None
    for pc in pieces_A:
        emit_piece(pc)
    emit_piece_store()
    restB = list(pieces_B)
    per_tile = -(-len(restB) // max(len(p1_tiles), 1)) if p1_tiles else len(restB)
    if p1_tiles:
        p1_slots = [load_slot(G_ZB, 4096), load_slot(G_I, 4096)]

    def tg(t):
        S.tag = t

    def g_casts(n):
        for _ in range(n):
            if restB:
                emit_piece(restB.pop(0))
            yield

    dec1b = sb("dec1b", 32)
    Bdec1b = Buf("dec1b")
    p1sets = [dict(kt=krTM[1], Bkt=BkrTM[1], vt=vTM, Bvt=BvTM, dct=dec[1], Bdct=Bdec[1]),
              dict(kt=krTM[0], Bkt=BkrTM[0], vt=[PT[0][0], PT[0][1], PT[1][0], PT[1][1]],
                   Bvt=[BPT[0][0], BPT[0][1], BPT[1][0], BPT[1][1]], dct=dec1b, Bdct=Bdec1b)]
    prev_chain = None
    for idx, j in enumerate(p1_tiles):
        ps = p1sets[idx % 2]
        tg(f"p1 t{j}")

        def after(j=j):
            if j <= NT2:
                S.dma("sp", lambda e: e.dma_start(out=bnd[j - 1], in_=Sst[1][:]), reads=[BS[1]], writes=[Bbnd], sembuf=Bbnd)

        genA = seq(stage1(j), stage2a(True, vt=ps["vt"], Bvt=ps["Bvt"], dct=ps["dct"], Bdct=ps["Bdct"]),
                   kr_transposes(1, kt=ps["kt"], Bkt=ps["Bkt"]))
        if prev_chain is None:
            interleave(genA, g_casts(per_tile))
        else:
            interleave(prev_chain, genA, g_casts(per_tile))
        prev_chain = state_chain(1, range(7, -1, -1), False, after=after, **ps)
    if prev_chain is not None:
        run(prev_chain)
    tg("cast rest")
    while restB:
        emit_piece(restB.pop(0))
    emit_piece_store()
    for j in range(NT2):
        if p1_tiles:
            S.dma("sp", lambda e, j=j: e.dma_start(out=Sst[1][:], in_=bnd[j]), reads=[Bbnd], writes=[BS[1]], sembuf=BS[1])
        else:
            S.op("pool", lambda e: e.memset(Sst[1][:], 0.0), writes=[BS[1]])
        tg(f"p2 t{j} s1")
        run(stage1(j))
        tg(f"p2 t{j} s2")
        if last_stage >= 2:
            run(stage2a(False))
        for si_, fn_ in ((3, stage3), (4, stage4), (5, stage5), (6, stage6), (7, stage7),
                         (8, stage8), (9, stage9)):
            if si_ <= last_stage:
                tg(f"p2 t{j} s{si_}")
                fn_()
        tg(f"p2 t{j} s10")
        if not skip_s10:
            stage10(j)
    S.final_wait("sp", (store_ops[-2:] if len(store_ops) >= 2 else store_ops) + [b.writer for b in Bwsc if b.writer is not None])
    S.emit()
    global LAST_NAMES
    LAST_NAMES = S.names
    return nc


def _host_consts():
    identf = np.eye(128, dtype=np.float32)
    s = np.arange(128)[:, None]
    t = np.arange(128)[None, :]
    same = (s // 64) == (t // 64)
    mf = (same & (s <= t)).astype(np.float32)
    mb = (same & (s >= t)).astype(np.float32)
    masks = np.zeros((4, 128, 512), np.float32)
    masks[0] = np.tile(mf, (1, 4))
    masks[1] = np.tile(mb, (1, 4))
    tt = np.arange(512)
    masks[2] = np.broadcast_to((tt % 64 != 0).astype(np.float32), (128, 512))
    masks[3] = np.broadcast_to((tt % 64 != 63).astype(np.float32), (128, 512))
    return identf, masks


def _weights_for(flip, w_in, lb_logits, sg_spatial_w, sg_spatial_b):
    q, i_, ff, fb, g, u, v = [w_in[:, k * 512:(k + 1) * 512] for k in range(7)]
    ga = w_in[:, 3584:4608]
    gb = w_in[:, 4608:5632]
    zf, zb = (fb, ff) if flip else (ff, fb)
    w_in_r = np.ascontiguousarray(np.concatenate([zf, zb, q, i_, v, g, u, ga, gb], axis=1))
    lbl = lb_logits[:, ::-1, :] if flip else lb_logits
    ws = sg_spatial_w
    bs = sg_spatial_b
    if flip:
        ws = ws[:, ::-1, ::-1]
        bs = bs[:, ::-1]
    wst = np.ascontiguousarray(np.transpose(ws, (2, 0, 1)))
    lblT = np.transpose(lbl.reshape(2, 2, 4, 128), (3, 0, 1, 2)).reshape(128, 16)
    return w_in_r, lblT, wst, np.ascontiguousarray(bs.reshape(1, 512))


_PROGRAM_CACHE = {}
LAST_NAMES = {}


def _run(x, pre_mix_w, w_in, lb_logits, hg_norm_w, sg_ln_w, sg_ln_b, sg_spatial_w, sg_spatial_b,
         w_proj_a, w_proj_b, w_out, post_mix_w, pre_ffn_w, w_gate, w_up, w_down, post_ffn_w, **build_kw):
    x = np.asarray(x, np.float32)
    B, L, _ = x.shape
    f = lambda a: np.ascontiguousarray(np.asarray(a, np.float32))
    identf, masks = _host_consts()
    vecs = np.stack([f(pre_mix_w)[0], f(post_mix_w)[0], f(pre_ffn_w)[0], f(post_ffn_w)[0]], axis=0)
    vecsT = np.transpose(vecs.reshape(4, 8, 128), (2, 0, 1)).reshape(128, 32)
    hgwT = np.transpose(f(hg_norm_w)[0].reshape(4, 128), (1, 0))
    lnwT = np.transpose(f(sg_ln_w)[0].reshape(4, 128), (1, 0))
    lnb_bc = np.ascontiguousarray(np.broadcast_to(f(sg_ln_b)[0][None, :], (128, 512)))
    common = dict(w_pa=f(w_proj_a)[0], w_pb=f(w_proj_b)[0], w_out=f(w_out)[0], w_gate=f(w_gate)[0], w_up=f(w_up)[0],
                  w_down=f(w_down)[0], lnb_bc=lnb_bc, identf=identf, masks=masks)
    per_flip = []
    for flip in (False, True):
        w_in_r, lbl, wst, bs = _weights_for(flip, f(w_in)[0], f(lb_logits), f(sg_spatial_w)[0], f(sg_spatial_b)[0])
        smalls = np.ascontiguousarray(np.concatenate([vecsT, hgwT, lnwT, lbl], axis=1).astype(np.float32))
        per_flip.append(dict(w_in=w_in_r, smalls=smalls, wst=wst, bs=bs))
    in_maps = []
    for b in range(B):
        for flip in (False, True):
            xs = x[b, ::-1] if flip else x[b]
            m = dict(common)
            m.update(per_flip[int(flip)])
            m["xs"] = np.ascontiguousarray(xs)
            in_maps.append(m)
    key = (L, tuple(sorted(build_kw.items())))
    nc = build_program(L, **build_kw)
    res = run_bass_kernel_spmd(nc, in_maps, core_ids=list(range(len(in_maps))))
    own = L // 2
    out = np.zeros((B, L, D), np.float32)
    for b in range(B):
        y0 = res.results[2 * b]["y"]
        y1 = res.results[2 * b + 1]["y"]
        out[b, :own] = y0
        out[b, own:] = y1[::-1]
    return out


def kernel(**inputs):
    return _run(**inputs)
```

```python
import numpy as np
import ml_dtypes
import concourse.bass as bass
import concourse.mybir as mybir
from concourse.bass_utils import run_bass_kernel_spmd

F32 = mybir.dt.float32
BF16 = mybir.dt.bfloat16
AF = mybir.ActivationFunctionType
ALU = mybir.AluOpType

D = 1024
NH = 4
HD = 128
DFF = 2816
EPS = 1e-6
T = 512
NKC = 8
NFC = 22
G_ZF, G_ZB, G_Q, G_I, G_V, G_G, G_U, G_GA0, G_GA1, G_GB0, G_GB1 = range(11)
SL_AB = 11
SL_O = 13
SL_GU = 15
SL_D = 26
NSLOT = 34
NRING = 4


class Buf:
    __slots__ = ("name", "writer", "readers", "dsem", "dtotal", "alias", "excl")

    def __init__(self, name, excl=False):
        self.name = name
        self.excl = excl
        self.writer = None
        self.readers = []
        self.dsem = None
        self.dtotal = 0
        self.alias = ()


class Op:
    __slots__ = ("eng", "fn", "deps", "milestone", "count", "is_dma", "dma_sem", "dma_val", "tag")

    def __init__(self, eng, fn, deps):
        self.eng = eng
        self.fn = fn
        self.deps = deps
        self.milestone = False
        self.count = None
        self.is_dma = False
        self.dma_sem = None
        self.dma_val = 0


class Sched:
    ENGS = ("pe", "act", "dve", "pool", "sp")

    def __init__(self, nc):
        self.nc = nc
        self.ops = {e: [] for e in self.ENGS}
        self.sems = {e: nc.alloc_semaphore("sem_" + e) for e in self.ENGS}
        self.tag = ""
        self.names = {}

    def _deps(self, reads, writes):
        deps = []
        for b in reads:
            if b.writer is not None:
                deps.append(b.writer)
            if b.excl:
                deps.extend(b.readers)
        for b in writes:
            if b.writer is not None:
                deps.append(b.writer)
            deps.extend(b.readers)
            for a in b.alias:
                if a.writer is not None:
                    deps.append(a.writer)
                deps.extend(a.readers)
        return deps

    def _update(self, op, reads, writes):
        for b in writes:
            b.writer = op
            b.readers = []
        for b in reads:
            if b in writes:
                continue
            if not op.is_dma:
                b.readers = [r for r in b.readers if r.is_dma or r.eng != op.eng]
            b.readers.append(op)

    def op(self, eng, fn, reads=(), writes=()):
        o = Op(eng, fn, self._deps(reads, writes))
        o.tag = self.tag
        self.ops[eng].append(o)
        self._update(o, reads, writes)
        return o

    def dma(self, eng, fn, reads=(), writes=(), sembuf=None):
        o = Op(eng, fn, self._deps(reads, writes))
        o.tag = self.tag
        o.is_dma = True
        if sembuf.dsem is None:
            sembuf.dsem = self.nc.alloc_semaphore("dsem_" + sembuf.name)
        sembuf.dtotal += 16
        o.dma_sem = sembuf.dsem
        o.dma_val = sembuf.dtotal
        self.ops[eng].append(o)
        self._update(o, reads, writes)
        return o

    def final_wait(self, eng, deps):
        o = Op(eng, None, list(deps))
        self.ops[eng].append(o)
        return o

    def emit(self):
        nc = self.nc
        for e in self.ENGS:
            for o in self.ops[e]:
                for d in o.deps:
                    if not d.is_dma:
                        d.milestone = True
        for e in self.ENGS:
            c = 0
            for o in self.ops[e]:
                if o.milestone and not o.is_dma:
                    c += 1
                    o.count = c

        def replay(e):
            def body(engine):
                seen = {}
                for o in self.ops[e]:
                    need = {}
                    for d in o.deps:
                        if d.is_dma:
                            key, val, sem = ("d", id(d.dma_sem)), d.dma_val, d.dma_sem
                        else:
                            key, val, sem = ("e", d.eng), d.count, self.sems[d.eng]
                        if seen.get(key, 0) >= val:
                            continue
                        if key not in need or need[key][1] < val:
                            need[key] = (sem, val)
                    for key, (sem, val) in need.items():
                        engine.wait_ge(sem, val)
                        seen[key] = val
                    if o.fn is None:
                        continue
                    inst = o.fn(engine)
                    try:
                        self.names[inst.ins.name] = o.tag
                    except Exception:
                        pass
                    if o.is_dma:
                        inst.then_inc(o.dma_sem, 16)
                    elif o.milestone:
                        inst.then_inc(self.sems[e], 1)
            return body

        with nc.Block() as block:
            block.tensor(replay("pe"))
            block.scalar(replay("act"))
            block.vector(replay("dve"))
            block.gpsimd(replay("pool"))
            block.sync(replay("sp"))


def build_program(L, n_own_tiles=None, n_p1_tiles=None, debug=False, last_stage=10, do_cast=1, skip_s10=0, s1_steps=9, s3_steps=9):
    own = L // 2
    NT2 = own // T if n_own_tiles is None else n_own_tiles
    NTALL = L // T
    nc = bass.Bass("TRN2", target_bir_lowering=False)
    S = Sched(nc)

    def din(name, shape, dt=F32):
        return nc.dram_tensor(name, list(shape), dt, kind="ExternalInput").ap()

    xs = din("xs", [L, D])
    w_in = din("w_in", [D, 5632])
    w_pa = din("w_pa", [512, D])
    w_pb = din("w_pb", [512, D])
    w_out = din("w_out", [D, D])
    w_gate = din("w_gate", [D, DFF])
    w_up = din("w_up", [D, DFF])
    w_down = din("w_down", [DFF, D])
    smalls_d = din("smalls", [128, 56])
    lnb_d = din("lnb_bc", [128, 512])
    wst_d = din("wst", [128, 4, 128])
    bs_d = din("bs", [1, 512])
    identf_d = din("identf", [128, 128])
    masks_d = din("masks", [4, 128, 512])
    y = nc.dram_tensor("y", [own, D], F32, kind="ExternalOutput").ap()
    wsc = nc.dram_tensor("wsc", [NSLOT, 128, 4096], BF16).ap()
    bnd = nc.dram_tensor("bnd", [max(NT2, 1), 128, 512], F32).ap()

    def sb(name, cols, dt=F32, parts=128):
        return nc.alloc_sbuf_tensor(name + "_sb", [parts, cols], dt)

    Bc = Buf("const")
    identf = sb("identf", 128)
    identb = sb("identb", 128, BF16)
    onesb = sb("onesb", 128, BF16)
    onesf = sb("onesf", 128)
    masks = sb("masks", 4 * 512)
    maskF4, maskB4 = masks[:, 0:512], masks[:, 512:1024]
    mF, mB = masks[:, 1024:1536], masks[:, 1536:2048]
    smalls = sb("smalls", 56)
    vecsT = smalls[:, 0:32]
    hgw = smalls[:, 32:36]
    lnw = smalls[:, 36:40]
    lblT = smalls[:, 40:56]
    oml = sb("oml", 8)
    noml = sb("noml", 8)
    lnb_bc = sb("lnb_bc", 512)
    wstf = sb("wstf", 512)
    wstb = sb("wstb", 512, BF16)
    bs_sb = sb("bs_sb", 512, F32, parts=1)
    extra = sb("extra", 512)

    def cdma(out, in_):
        S.dma("sp", lambda e: e.dma_start(out=out, in_=in_), writes=[Bc], sembuf=Bc)

    cdma(identf[:], identf_d)
    cdma(masks[:].rearrange("p (a c) -> p a c", a=4), masks_d.rearrange("a p c -> p a c"))
    cdma(smalls[:], smalls_d)
    cdma(lnb_bc[:], lnb_d)
    cdma(wstf[:].rearrange("p (g t) -> p g t", g=4), wst_d)
    cdma(bs_sb[:], bs_d)

    Bwsc = [Buf(f"wsc{i}") for i in range(NSLOT)]
    w_in_v = w_in.rearrange("(kc p) c -> p kc c", p=128)
    pa_v = w_pa.rearrange("(kc p) c -> p kc c", p=128)
    pb_v = w_pb.rearrange("(kc p) c -> p kc c", p=128)
    wo_v = w_out.rearrange("(kc p) c -> p kc c", p=128)
    wg_v = w_gate.rearrange("(kc p) c -> p kc c", p=128)
    wu_v = w_up.rearrange("(kc p) c -> p kc c", p=128)
    wd_v = w_down.rearrange("(kc p) c -> p kc c", p=128)
    pieces_A, pieces_B = [], []

    def win_pieces(g, lst):
        for k0 in (0, 4):
            lst.append((g, k0 * 512, 2048, [(0, 4, 512, 512, w_in_v[:, k0:k0 + 4, g * 512:(g + 1) * 512])]))

    win_pieces(G_ZB, pieces_A)
    win_pieces(G_I, pieces_A)
    for g in range(11):
        if g not in (G_ZB, G_I):
            win_pieces(g, pieces_B)
    for j in range(2):
        pieces_B.append((SL_AB + j, 0, 2048, [(0, 4, 512, 512, pa_v[:, :, j * 512:(j + 1) * 512])]))
        pieces_B.append((SL_AB + j, 2048, 2048, [(0, 4, 512, 512, pb_v[:, :, j * 512:(j + 1) * 512])]))
    for j in range(2):
        for k0 in (0, 4):
            pieces_B.append((SL_O + j, k0 * 512, 2048, [(0, 4, 512, 512, wo_v[:, k0:k0 + 4, j * 512:(j + 1) * 512])]))
    for j in range(11):
        for k0 in (0, 4):
            pieces_B.append((SL_GU + j, k0 * 512, 2048, [(0, 4, 256, 512, wg_v[:, k0:k0 + 4, j * 256:(j + 1) * 256]),
                                                       (256, 4, 256, 512, wu_v[:, k0:k0 + 4, j * 256:(j + 1) * 256])]))
    for oc in range(8):
        pieces_B.append((SL_D + oc, 0, 2048, [(0, 16, 128, 128, wd_v[:, 0:16, oc * 128:(oc + 1) * 128])]))
        pieces_B.append((SL_D + oc, 2048, 768, [(0, 6, 128, 128, wd_v[:, 16:22, oc * 128:(oc + 1) * 128])]))

    xT = sb("xT", NKC * T)
    BxT = [Buf(f"xT{k}") for k in range(NKC)]
    hT = sb("hT", NKC * T, BF16)
    BhT = [Buf(f"hT{k}") for k in range(NKC)]
    sq = [sb(f"sq{i}", 1024, BF16) for i in range(2)]
    Bsq = [Buf(f"sq{i}") for i in range(2)]
    NXTM = 3
    xtm = [sb(f"xtm{i}", D) for i in range(NXTM)]
    Bxtm = [Buf(f"xtm{i}") for i in range(NXTM)]
    Sst = [sb(f"Sst{d}", 512) for d in range(2)]
    BS = [Buf(f"S{d}") for d in range(2)]
    dec = [sb(f"dec{d}", 32) for d in range(2)]
    Bdec = [Buf(f"dec{d}") for d in range(2)]
    ring = [sb(f"ring{i}", 4096, BF16) for i in range(NRING)]
    Bring = [Buf(f"ring{i}") for i in range(NRING)]
    sgT = sb("sgT", 4 * T, BF16)
    BsgT = [Buf(f"sgT{j}") for j in range(4)]
    guT = sb("guT", 4 * T, BF16)
    BguT = [Buf(f"guT{j}") for j in range(4)]
    xhat = sb("xhat", 4 * 512, BF16)
    Bxhat = [Buf(f"xhat{b}") for b in range(4)]
    gv = [sb(f"gv{i}", 512) for i in range(4)]
    Bgv = [Buf(f"gv{i}") for i in range(4)]
    stat = [sb(f"stat{i}", 16) for i in range(4)]
    Bstat = [Buf(f"stat{i}") for i in range(4)]
    onT = sb("onT", 4 * T, BF16)
    BonT = [Buf(f"onT{b}") for b in range(4)]
    sTt = sb("sT", 4 * T, BF16)
    BsT = [Buf(f"sT{b}") for b in range(4)]
    sga = [sb(f"sga{i}", T, BF16) for i in range(2)]
    Bsga = [Buf(f"sga{i}") for i in range(2)]
    sgb = [sb(f"sgb{i}", T, BF16) for i in range(2)]
    Bsgb = [Buf(f"sgb{i}") for i in range(2)]
    osq = [sb(f"osq{i}", 512, BF16) for i in range(2)]
    Bosq = [Buf(f"osq{i}") for i in range(2)]
    t1 = [sb(f"t1_{i}", 512, BF16) for i in range(2)]
    Bt1 = [Buf(f"t1_{i}") for i in range(2)]
    tmpr = [sb(f"tmpr{i}", 512) for i in range(2)]
    Btmpr = [Buf(f"tmpr{i}") for i in range(2)]

    ARENA_B = 72 * 1024
    arena = sb("arena", ARENA_B // 2, BF16)

    def av(off, nbytes, dt):
        a = arena[:, off // 2:(off + nbytes) // 2]
        if dt == F32:
            a = a.bitcast(F32)
        return a

    K = 1024
    gun = []
    for i in range(3):
        base = i * 8 * K
        gun.append(dict(sp=av(base, 2 * K, F32), lf=av(base + 2 * K, 2 * K, F32), c=av(base + 4 * K, 2 * K, F32),
                        e1=av(base + 6 * K, K, BF16), e2=av(base + 7 * K, K, BF16),
                        Bsp=Buf(f"sp{i}"), Blf=Buf(f"lf{i}"), Bc=Buf(f"c{i}"), Be1=Buf(f"e1{i}"), Be2=Buf(f"e2{i}")))
    kr = [av(24 * K + u * K, K, BF16) for u in range(8)]
    Bkr = [Buf(f"kr{u}") for u in range(8)]
    qr = [av(32 * K + u * K, K, BF16) for u in range(8)]
    Bqr = [Buf(f"qr{u}") for u in range(8)]
    krTM = [[av(40 * K + (d * 4 + b) * K, K, BF16) for b in range(4)] for d in range(2)]
    BkrTM = [[Buf(f"krTM{d}{b}") for b in range(4)] for d in range(2)]
    vTM = [av(48 * K + b * K, K, BF16) for b in range(4)]
    BvTM = [Buf(f"vTM{b}") for b in range(4)]
    PT = [[av(52 * K + (d * 2 + i) * K, K, BF16) for i in range(2)] for d in range(2)]
    BPT = [[Buf(f"PT{d}{i}") for i in range(2)] for d in range(2)]
    Sbf = [[av(56 * K + (d * 8 + n) * K, K, BF16) for n in range(8)] for d in range(2)]
    BSbf = [[Buf(f"Sbf{d}{n}") for n in range(8)] for d in range(2)]
    hid = [av(j * K, K, BF16) for j in range(NFC)]
    Bhid = [Buf(f"hid{j}") for j in range(NFC)]
    mix = [av(22 * K + oc * 2 * K, 2 * K, F32) for oc in range(8)]
    Bmix = [Buf(f"mix{oc}") for oc in range(8)]
    merged = [av(38 * K + oc * K, K, BF16) for oc in range(8)]
    Bmerged = [Buf(f"mg{oc}") for oc in range(8)]
    m1 = [av(46 * K + i * 2 * K, 2 * K, F32) for i in range(2)]
    Bm1 = [Buf(f"m1{i}") for i in range(2)]
    m2 = [av(50 * K + i * 2 * K, 2 * K, F32) for i in range(2)]
    Bm2 = [Buf(f"m2{i}") for i in range(2)]
    tsc = [av(54 * K + i * 2 * K, 2 * K, F32) for i in range(2)]
    Btsc = [Buf(f"tsc{i}") for i in range(2)]
    sgate = [av(58 * K + i * K, K, BF16) for i in range(2)]
    Bsgate = [Buf(f"sgate{i}") for i in range(2)]

    sgaL = [av(oc * K, K, BF16) for oc in range(8)]
    BsgaL = [Buf(f"sgaL{oc}") for oc in range(8)]
    sgbL = [av(8 * K + oc * K, K, BF16) for oc in range(8)]
    BsgbL = [Buf(f"sgbL{oc}") for oc in range(8)]
    early = []
    for g_ in gun:
        early += [g_["Bsp"], g_["Blf"], g_["Bc"], g_["Be1"], g_["Be2"]]
    early += Bkr + Bqr + BkrTM[0] + BkrTM[1] + BvTM + BPT[0] + BPT[1] + BSbf[0] + BSbf[1]
    late = Bhid + Bmix + Bmerged + Bm1 + Bm2 + Btsc + Bsgate
    gunB = list(early[:15])
    early += BsgaL + BsgbL
    for b in early:
        b.alias = tuple(late)
    for b in late:
        b.alias = tuple(early)
    for b in gunB:
        b.alias = tuple(list(b.alias) + BsgaL + BsgbL)
    for b in BsgaL + BsgbL:
        b.alias = tuple(list(b.alias) + gunB)

    stf = [av(56 * K + i * 8 * K, 8 * K, F32) for i in range(2)]
    Bstf = [Buf(f"stf{i}") for i in range(2)]
    stb = [av(32 * K + i * 4 * K, 4 * K, BF16) for i in range(2)]
    Bstb = [Buf(f"stb{i}") for i in range(2)]
    for b_ in Bstf:
        b_.alias = tuple(BSbf[0] + BSbf[1] + late)
    for b_ in Bstb:
        b_.alias = tuple(Bqr + late)
    for b_ in BSbf[0] + BSbf[1]:
        b_.alias = tuple(list(b_.alias) + Bstf)
    for b_ in Bqr:
        b_.alias = tuple(list(b_.alias) + Bstb)
    for b_ in late:
        b_.alias = tuple(list(b_.alias) + Bstf + Bstb)
    piece_ctr = [0]
    pending_store = []

    def emit_piece_store():
        if pending_store:
            (slot, doff, n, i) = pending_store.pop(0)
            S.dma("sp", lambda e: e.dma_start(out=wsc[slot][:, doff:doff + n], in_=stb[i][:, 0:n]),
                  reads=[Bstb[i]], writes=[Bwsc[slot]], sembuf=Bstb[i])

    def emit_piece(piece):
        if not do_cast:
            return
        (slot, doff, n, parts) = piece
        i = piece_ctr[0] % 2
        piece_ctr[0] += 1
        for (soff, kcn, cols, rowlen, src) in parts:
            dstv = stf[i][:, 0:kcn * rowlen].rearrange("p (k c) -> p k c", c=rowlen)[:, :, soff:soff + cols]
            S.dma("sp", lambda e, dstv=dstv, src=src: e.dma_start(out=dstv, in_=src), writes=[Bstf[i]], sembuf=Bstf[i])
        emit_piece_store()
        S.op("pool", lambda e: e.tensor_copy(out=stb[i][:, 0:n], in_=stf[i][:, 0:n]), reads=[Bstf[i]], writes=[Bstb[i]])
        pending_store.append((slot, doff, n, i))

    banks = [nc.alloc_psum_tensor(f"bank{i}", [128, 512], F32) for i in range(8)]
    Bbank = [Buf(f"bank{i}", excl=True) for i in range(8)]
    bank_ctr = [0]

    def nb():
        i = bank_ctr[0] % 7
        bank_ctr[0] += 1
        return banks[i], Bbank[i]

    def statbank():
        return banks[7], Bbank[7]

    def ACT(out, in_, func, r, w, scale=None, bias=None, accum=None):
        kw = {}
        if scale is not None:
            kw["scale"] = scale
        if bias is not None:
            kw["bias"] = bias
        if accum is not None:
            kw["accum_out"] = accum
        return S.op("act", lambda e: e.activation(out=out, in_=in_, func=func, **kw), reads=r, writes=w)

    def TT(eng, out, in0, in1, op, r, w):
        return S.op(eng, lambda e: e.tensor_tensor(out=out, in0=in0, in1=in1, op=op), reads=r, writes=w)

    def TS(eng, out, in0, s1, s2, op0, op1, r, w):
        if op1 is None:
            return S.op(eng, lambda e: e.tensor_scalar(out=out, in0=in0, scalar1=s1, scalar2=None, op0=op0), reads=r, writes=w)
        return S.op(eng, lambda e: e.tensor_scalar(out=out, in0=in0, scalar1=s1, scalar2=s2, op0=op0, op1=op1), reads=r, writes=w)

    def STT(out, in0, sc, in1, op0, op1, r, w):
        return S.op("dve", lambda e: e.scalar_tensor_tensor(out=out, in0=in0, scalar=sc, in1=in1, op0=op0, op1=op1), reads=r, writes=w)

    def COPY(eng, out, in_, r, w):
        if eng == "act":
            return ACT(out, in_, AF.Copy, r, w)
        return S.op(eng, lambda e: e.tensor_copy(out=out, in_=in_), reads=r, writes=w)

    def MM(specs, r, w):
        def fn(e):
            inst = None
            for (o_, l_, r_, st, sp_) in specs:
                inst = e.matmul(o_, l_, r_, start=st, stop=sp_)
            return inst
        return S.op("pe", fn, reads=r, writes=w)

    def TR(specs, ident, r, w):
        def fn(e):
            inst = None
            for (o_, i_) in specs:
                inst = e.transpose(o_, i_, ident)
            return inst
        return S.op("pe", fn, reads=r, writes=w)

    S.op("pool", lambda e: e.memset(onesf[:], 1.0), writes=[Bc], reads=[Bc])
    S.op("pool", lambda e: e.memset(onesb[:], 1.0), writes=[Bc], reads=[Bc])
    S.op("dve", lambda e: e.tensor_copy(out=identb[:], in_=identf[:]), writes=[Bc], reads=[Bc])
    lv = lblT.rearrange("p (l x) -> p l x", l=2)
    TT("dve", oml[:], lv[:, 0, :], lv[:, 1, :], ALU.subtract, [Bc], [Bc])
    ACT(oml[:], oml[:], AF.Exp, [Bc], [Bc])
    ACT(oml[:], oml[:], AF.Ln, [Bc], [Bc], bias=1.0)
    ACT(oml[:], oml[:], AF.Exp, [Bc], [Bc], scale=-1.0)
    TS("dve", noml[:], oml[:], -1.0, None, ALU.mult, None, [Bc], [Bc])
    COPY("act", wstb[:], wstf[:], [Bc], [Bc])
    bk, Bbk = nb()
    specs = []
    for g in range(4):
        specs.append((bk[:, g * 128:(g + 1) * 128], lnb_bc[:, g * 128:(g + 1) * 128], wstf[:, g * 128:(g + 1) * 128], True, False))
        specs.append((bk[:, g * 128:(g + 1) * 128], onesf[0:1, :], bs_sb[0:1, g * 128:(g + 1) * 128], False, True))
    MM(specs, [Bc], [Bbk])
    COPY("dve", extra[:], bk[:], [Bbk], [Bc])

    pmw = vecsT[:, 0:8]
    pmw2 = vecsT[:, 8:16]
    pfw = vecsT[:, 16:24]
    pfw2 = vecsT[:, 24:32]

    ring_ctr = [0]

    def load_slot(slot, ncols, wbuf=None):
        wbuf = Bwsc[slot]
        i = ring_ctr[0] % NRING
        ring_ctr[0] += 1
        rt, rb = ring[i], Bring[i]
        S.dma("sp", lambda e: e.dma_start(out=rt[:, 0:ncols], in_=wsc[slot][:, 0:ncols]),
              reads=[wbuf], writes=[rb], sembuf=rb)
        return rt, rb

    store_ops = []
    xtm_ctr = [0]
    gun_ctr = [0]
    rot = {"sq": 0, "gv": 0, "osq": 0, "m": 0, "sg": 0, "tsc": 0, "sgate": 0, "tmpr": 0, "pt": 0}

    def rmsnorm_rstd(msbank, Bms, n_feat):
        i = rot["tmpr"] % 2
        rot["tmpr"] += 1
        ACT(tmpr[i][:], msbank[:], AF.Ln, [Bms], [Btmpr[i]], scale=1.0 / n_feat, bias=EPS)
        ACT(msbank[:], tmpr[i][:], AF.Exp, [Btmpr[i]], [Bms], scale=-0.5)

    def stage1(tile_idx):
        msb, Bms = statbank()
        for blk in range(4):
            si = xtm_ctr[0] % NXTM
            xtm_ctr[0] += 1
            r0 = tile_idx * T + blk * 128
            xt_, bx_ = xtm[si], Bxtm[si]
            S.dma("sp", lambda e, xt_=xt_, r0=r0: e.dma_start(out=xt_[:], in_=xs[r0:r0 + 128, :]), writes=[bx_], sembuf=bx_)
            qi = rot["sq"] % 2
            rot["sq"] += 1
            for half in range(2):
                bk_, Bbk_ = nb()
                TR([(bk_[:, j * 128:(j + 1) * 128], xt_[:, (half * 4 + j) * 128:(half * 4 + j + 1) * 128]) for j in range(4)],
                   identf[:], [bx_, Bc], [Bbk_])
                xv = xT[:].rearrange("p (k t) -> p k t", k=NKC)[:, half * 4:half * 4 + 4, blk * 128:(blk + 1) * 128]
                if s1_steps >= 2:
                    COPY("dve", xv, bk_[:].rearrange("p (k t) -> p k t", k=4), [Bbk_], BxT[half * 4:half * 4 + 4])
                if s1_steps >= 3:
                    ACT(sq[qi][:, half * 512:(half + 1) * 512], bk_[:], AF.Square, [Bbk_], [Bsq[qi]])
            if s1_steps >= 4:
                MM([(msb[:, blk * 128:(blk + 1) * 128], onesb[:], sq[qi][:, k * 128:(k + 1) * 128], k == 0, k == NKC - 1) for k in range(NKC)],
                   [Bsq[qi], Bc], [Bms])
            yield
        if s1_steps >= 5:
            rmsnorm_rstd(msb, Bms, D)
        for k in range(NKC if s1_steps >= 6 else 0):
            STT(hT[:, k * T:(k + 1) * T], xT[:, k * T:(k + 1) * T], pmw[:, k:k + 1], msb[:], ALU.mult, ALU.mult,
                [BxT[k], Bms, Bc], [BhT[k]])

    def fm_proj(rt, rb, j, kcn, rhs_t, Brhs, stride=512, off=0):
        bk_, Bbk_ = nb()
        MM([(bk_[:], rt[:, kc * stride + off + j * 128: kc * stride + off + (j + 1) * 128], rhs_t[:, kc * T:(kc + 1) * T], kc == 0, kc == kcn - 1)
            for kc in range(kcn)], [rb] + list(Brhs), [Bbk_])
        return bk_, Bbk_

    def gate_unit(d, h, zb_, Bz_, dct=None, Bdct=None):
        u = d * 4 + h
        S.tag = S.tag.split("|")[0] + f"|gate d{d} h{h}"
        g_ = gun[gun_ctr[0] % 3]
        gun_ctr[0] += 1
        ACT(g_["sp"], zb_[:], AF.Exp, [Bz_], [g_["Bsp"]])
        ACT(g_["sp"], g_["sp"], AF.Ln, [g_["Bsp"]], [g_["Bsp"]], bias=1.0)
        ACT(g_["sp"], g_["sp"], AF.Exp, [g_["Bsp"]], [g_["Bsp"]], scale=-1.0)
        ACT(g_["lf"], g_["sp"], AF.Ln, [g_["Bsp"], Bc], [g_["Blf"]], scale=noml[:, u:u + 1], bias=1.0)
        if d == 0:
            S.op("dve", lambda e: e.tensor_tensor_scan(out=g_["c"], data0=mF, data1=g_["lf"], initial=0.0, op0=ALU.mult, op1=ALU.add),
                 reads=[g_["Blf"], Bc], writes=[g_["Bc"]])
        else:
            S.op("dve", lambda e: e.tensor_tensor_scan(out=g_["c"][:, ::-1], data0=mB[:, ::-1], data1=g_["lf"][:, ::-1], initial=0.0,
                                                       op0=ALU.mult, op1=ALU.add),
                 reads=[g_["Blf"], Bc], writes=[g_["Bc"]])
        ACT(g_["e2"], g_["c"], AF.Exp, [g_["Bc"]], [g_["Be2"]], scale=-1.0)
        cl = g_["c"][:, 63::64] if d == 0 else g_["c"][:, 0::64]
        if dct is None:
            dct, Bdct = dec[d], Bdec[d]
        ACT(dct[:, h * 8:(h + 1) * 8], cl, AF.Exp, [g_["Bc"]], [Bdct])
        STT(kr[u], g_["sp"], oml[:, u:u + 1], g_["e2"], ALU.mult, ALU.mult, [g_["Bsp"], g_["Be2"], Bc], [Bkr[u]])
        return g_

    def run(g):
        for _ in g:
            pass

    def seq(*gens):
        for g in gens:
            yield from g

    def interleave(*gens, weights=None):
        gens = list(gens)
        weights = list(weights) if weights else [1] * len(gens)
        alive = [True] * len(gens)
        while any(alive):
            for i, g in enumerate(gens):
                if not alive[i]:
                    continue
                for _ in range(weights[i]):
                    try:
                        next(g)
                    except StopIteration:
                        alive[i] = False
                        break

    def stage2a(phase1, vt=None, Bvt=None, dct=None, Bdct=None):
        vt = vTM if vt is None else vt
        Bvt = BvTM if Bvt is None else Bvt
        if phase1:
            rt_zb, rb_zb = p1_slots[0]
            rt_i, rb_i = p1_slots[1]
        else:
            rt_zf, rb_zf = load_slot(G_ZF, 4096)
            rt_zb, rb_zb = load_slot(G_ZB, 4096)
            rt_q, rb_q = load_slot(G_Q, 4096)
        for h in range(NH):
            gus = {}
            for d in ((1,) if phase1 else (0, 1)):
                rt, rb = (rt_zf, rb_zf) if d == 0 else (rt_zb, rb_zb)
                zb_, Bz_ = fm_proj(rt, rb, h, NKC, hT, BhT)
                if d == 1 and dct is not None:
                    gus[d] = gate_unit(d, h, zb_, Bz_, dct, Bdct)
                else:
                    gus[d] = gate_unit(d, h, zb_, Bz_)
                yield
            if not phase1:
                for d in (0, 1):
                    g_ = gus[d]
                    ACT(g_["e1"], g_["c"], AF.Exp, [g_["Bc"]], [g_["Be1"]])
                qb_, Bq_ = fm_proj(rt_q, rb_q, h, NKC, hT, BhT)
                for d in (0, 1):
                    g_ = gus[d]
                    TT("dve", qr[d * 4 + h], qb_[:], g_["e1"], ALU.mult, [Bq_, g_["Be1"]], [Bqr[d * 4 + h]])
                yield
        if not phase1:
            rt_i, rb_i = load_slot(G_I, 4096)
        for blk in range(4):
            bk_, Bbk_ = nb()
            MM([(bk_[:], hT[:, kc * T + blk * 128: kc * T + (blk + 1) * 128], rt_i[:, kc * 512:(kc + 1) * 512], kc == 0, kc == NKC - 1)
                for kc in range(NKC)], [rb_i] + BhT, [Bbk_])
            COPY("act", vt[blk], bk_[:], [Bbk_], [Bvt[blk]])
            yield

    def stage2b():
        rt_v, rb_v = load_slot(G_V, 4096)
        for blk in range(4):
            bk_, Bbk_ = nb()
            MM([(bk_[:], hT[:, kc * T + blk * 128: kc * T + (blk + 1) * 128], rt_v[:, kc * 512:(kc + 1) * 512], kc == 0, kc == NKC - 1)
                for kc in range(NKC)], [rb_v] + BhT, [Bbk_])
            gi = rot["gv"] % 4
            rot["gv"] += 1
            ACT(gv[gi][:], bk_[:], AF.Gelu_apprx_tanh, [Bbk_], [Bgv[gi]])
            st_, Bst_ = stat[gi], Bstat[gi]
            S.op("dve", lambda e, st_=st_, gi=gi: e.bn_stats(out=st_[:, 0:6], in_=gv[gi][:]), reads=[Bgv[gi]], writes=[Bst_])
            S.op("dve", lambda e, st_=st_: e.bn_aggr(out=st_[:, 6:8], in_=st_[:, 0:6]), reads=[Bst_], writes=[Bst_])
            stats_pending.append((blk, gi))
            yield
        rt_u, rb_u = load_slot(G_U, 4096)
        for j in range(4):
            bk_, Bbk_ = fm_proj(rt_u, rb_u, j, NKC, hT, BhT)
            ACT(guT[:, j * T:(j + 1) * T], bk_[:], AF.Gelu_apprx_tanh, [Bbk_], [BguT[j]])
            yield
        rt_g, rb_g = load_slot(G_G, 4096)
        for j in range(4):
            bk_, Bbk_ = fm_proj(rt_g, rb_g, j, NKC, hT, BhT)
            ACT(sgT[:, j * T:(j + 1) * T], bk_[:], AF.Silu, [Bbk_], [BsgT[j]])
            yield
        for (blk, gi) in stats_pending:
            st_, Bst_ = stat[gi], Bstat[gi]
            ACT(st_[:, 8:9], st_[:, 7:8], AF.Ln, [Bst_], [Bst_], bias=EPS)
            ACT(st_[:, 9:10], st_[:, 8:9], AF.Exp, [Bst_], [Bst_], scale=-0.5)
            STT(st_[:, 10:11], st_[:, 6:7], -1.0, st_[:, 9:10], ALU.mult, ALU.mult, [Bst_], [Bst_])
            TS("dve", xhat[:, blk * 512:(blk + 1) * 512], gv[gi][:], st_[:, 9:10], st_[:, 10:11], ALU.mult, ALU.add,
               [Bgv[gi], Bst_], [Bxhat[blk]])
            yield
        stats_pending.clear()

    stats_pending = []

    def state_chain(d, order, keep_bf, kt=None, Bkt=None, vt=None, Bvt=None, dct=None, Bdct=None, after=None):
        kt = krTM[d] if kt is None else kt
        Bkt = BkrTM[d] if Bkt is None else Bkt
        vt = vTM if vt is None else vt
        Bvt = BvTM if Bvt is None else Bvt
        dct = dec[d] if dct is None else dct
        Bdct = Bdec[d] if Bdct is None else Bdct
        for n in order:
            blk, c = n // 2, n % 2
            if keep_bf:
                COPY("pool", Sbf[d][n], Sst[d][:], [BS[d]], [BSbf[d][n]])
            bk_, Bbk_ = nb()
            specs = []
            rows = slice(c * 64, (c + 1) * 64)
            for h in range(NH):
                specs.append((bk_[:, h * 128:(h + 1) * 128], kt[blk][rows, h * 128:(h + 1) * 128],
                              vt[blk][rows, h * 128:(h + 1) * 128], True, False))
                specs.append((bk_[:, h * 128:(h + 1) * 128], identf[:], Sst[d][:, h * 128:(h + 1) * 128], False, True))
            MM(specs, [Bkt[blk], Bvt[blk], BS[d], Bc], [Bbk_])
            dv = dct[:].rearrange("p (h n) -> p h n", h=NH)[:, :, n:n + 1].to_broadcast([128, NH, 128])
            TT("dve", Sst[d][:].rearrange("p (h v) -> p h v", h=NH), bk_[:].rearrange("p (h v) -> p h v", h=NH), dv, ALU.mult,
               [Bbk_, Bdct], [BS[d]])
            yield
        if after is not None:
            after()
            yield

    def kr_transposes(d, kt=None, Bkt=None):
        kt = krTM[d] if kt is None else kt
        Bkt = BkrTM[d] if Bkt is None else Bkt
        for blk in range(4):
            bk_, Bbk_ = nb()
            bkb = bk_[:].bitcast(BF16)
            TR([(bkb[:, h * 128:(h + 1) * 128], kr[d * 4 + h][:, blk * 128:(blk + 1) * 128]) for h in range(NH)],
               identb[:], [Bkr[d * 4 + h] for h in range(NH)] + [Bc], [Bbk_])
            COPY("dve", kt[blk], bkb[:, 0:512], [Bbk_], [Bkt[blk]])
            yield

    def stage3():
        run(kr_transposes(0))
        run(kr_transposes(1))
        interleave(state_chain(0, range(8), True), state_chain(1, range(7, -1, -1), True), stage2b())
        def scores(blk):
            pts = []
            for d in (0, 1):
                bk_, Bbk_ = nb()
                MM([(bk_[:, h * 128:(h + 1) * 128], kr[d * 4 + h][:, blk * 128:(blk + 1) * 128],
                     qr[d * 4 + h][:, blk * 128:(blk + 1) * 128], True, True) for h in range(NH)],
                   [Bkr[d * 4 + h] for h in range(NH)] + [Bqr[d * 4 + h] for h in range(NH)], [Bbk_])
                pi = rot["pt"] % 2
                TT("dve", PT[d][pi], bk_[:], maskF4 if d == 0 else maskB4, ALU.mult, [Bbk_, Bc], [BPT[d][pi]])
                pts.append((PT[d][pi], BPT[d][pi]))
            rot["pt"] += 1
            return pts

        interleave(blocks_body_gen(scores), seq(stage4(), g_gates()))

    def blocks_body_gen(scores):
        nxt = scores(0)
        for blk in range(4 if s3_steps >= 3 else 0):
            pts = nxt
            if blk < 3:
                nxt = scores(blk + 1)
            if s3_steps < 4:
                continue
            ob, Bob = nb()
            specs = []
            for h in range(NH):
                hs = slice(h * 128, (h + 1) * 128)
                specs.append((ob[:, hs], vTM[blk][:, hs], pts[0][0][:, hs], True, False))
                specs.append((ob[:, hs], vTM[blk][:, hs], pts[1][0][:, hs], False, False))
                k_ = 0
                for d in (0, 1):
                    for c in (0, 1):
                        k_ += 1
                        specs.append((ob[:, h * 128 + c * 64: h * 128 + (c + 1) * 64], Sbf[d][2 * blk + c][:, hs],
                                      qr[d * 4 + h][:, blk * 128 + c * 64: blk * 128 + (c + 1) * 64], False, k_ == 4))
            MM(specs, [BvTM[blk], pts[0][1], pts[1][1]] + [BSbf[d][2 * blk + c] for d in (0, 1) for c in (0, 1)] + Bqr, [Bob])
            if s3_steps < 5:
                continue
            oi = rot["osq"] % 2
            rot["osq"] += 1
            ACT(osq[oi][:], ob[:], AF.Square, [Bob], [Bosq[oi]])
            msb, Bms = nb()
            MM([(msb[:], onesb[:], osq[oi][:], True, True)], [Bosq[oi], Bc], [Bms])
            ti = rot["tmpr"] % 2
            rot["tmpr"] += 1
            ACT(tmpr[ti][:], msb[:], AF.Ln, [Bms], [Btmpr[ti]], scale=1.0 / HD, bias=EPS)
            ACT(tmpr[ti][:], tmpr[ti][:], AF.Exp, [Btmpr[ti]], [Btmpr[ti]], scale=-0.5)

            def fn(e, ob=ob, oi=oi, ti=ti):
                inst = None
                for h in range(NH):
                    hs = slice(h * 128, (h + 1) * 128)
                    inst = e.scalar_tensor_tensor(out=t1[oi][:, hs], in0=ob[:, hs], scalar=hgw[:, h:h + 1], in1=tmpr[ti][:, hs],
                                                  op0=ALU.mult, op1=ALU.mult)
                return inst
            S.op("dve", fn, reads=[Bob, Btmpr[ti], Bc], writes=[Bt1[oi]])
            TT("dve", onT[:].rearrange("p (h t) -> p h t", h=NH)[:, :, blk * 128:(blk + 1) * 128],
               t1[oi][:].rearrange("p (h t) -> p h t", h=NH),
               sgT[:].rearrange("p (h t) -> p h t", h=NH)[:, :, blk * 128:(blk + 1) * 128], ALU.mult,
               [Bt1[oi]] + BsgT, [BonT[blk]])
            yield

    def stage4():
        for blk in range(4):
            bk_, Bbk_ = nb()
            MM([(bk_[:, g * 128:(g + 1) * 128], xhat[:, blk * 512 + g * 128: blk * 512 + (g + 1) * 128], wstb[:, g * 128:(g + 1) * 128], True, True)
                for g in range(4)], [Bxhat[blk], Bc], [Bbk_])
            oi = rot["osq"] % 2
            rot["osq"] += 1

            def fn(e, bk_=bk_, oi=oi):
                inst = None
                for g in range(4):
                    gs = slice(g * 128, (g + 1) * 128)
                    inst = e.scalar_tensor_tensor(out=t1[oi][:, gs], in0=bk_[:, gs], scalar=lnw[:, g:g + 1], in1=extra[:, gs],
                                                  op0=ALU.mult, op1=ALU.add)
                return inst
            S.op("dve", fn, reads=[Bbk_, Bc], writes=[Bt1[oi]])
            TT("dve", sTt[:].rearrange("p (h t) -> p h t", h=4)[:, :, blk * 128:(blk + 1) * 128],
               t1[oi][:].rearrange("p (h t) -> p h t", h=4),
               guT[:].rearrange("p (h t) -> p h t", h=4)[:, :, blk * 128:(blk + 1) * 128], ALU.mult,
               [Bt1[oi]] + BguT, [BsT[blk]])
            yield

    def g_gates():
        for j in range(2):
            rt_ga, rb_ga = load_slot(G_GA0 + j, 4096)
            rt_gb, rb_gb = load_slot(G_GB0 + j, 4096)
            for jj in range(4):
                oc = j * 4 + jj
                for (rt_, rb_, dst, Bdst) in ((rt_ga, rb_ga, sgaL[oc], BsgaL[oc]), (rt_gb, rb_gb, sgbL[oc], BsgbL[oc])):
                    bg_, Bbg_ = fm_proj(rt_, rb_, jj, NKC, hT, BhT)
                    gi = rot["gv"] % 4
                    rot["gv"] += 1
                    ACT(gv[gi][:], bg_[:], AF.Exp, [Bbg_], [Bgv[gi]], scale=-1.0)
                    ACT(gv[gi][:], gv[gi][:], AF.Ln, [Bgv[gi]], [Bgv[gi]], bias=1.0)
                    ACT(dst, gv[gi][:], AF.Exp, [Bgv[gi]], [Bdst], scale=-1.0)
                    yield

    def stage5():
        for j in range(2):
            rt_ab, rb_ab = load_slot(SL_AB + j, 4096)
            for jj in range(4):
                oc = j * 4 + jj
                mi = rot["m"] % 2
                rot["m"] += 1
                bya, Bbya = fm_proj(rt_ab, rb_ab, jj, 4, onT, BonT, stride=512, off=0)
                TT("dve", m1[mi], bya[:], sgaL[oc], ALU.mult, [Bbya, BsgaL[oc]], [Bm1[mi]])
                byb, Bbyb = fm_proj(rt_ab, rb_ab, jj, 4, sTt, BsT, stride=512, off=2048)
                TT("dve", m2[mi], byb[:], sgbL[oc], ALU.mult, [Bbyb, BsgbL[oc]], [Bm2[mi]])
                TT("pool", merged[oc], m1[mi], m2[mi], ALU.add, [Bm1[mi], Bm2[mi]], [Bmerged[oc]])

    def out_norm_residual(src, Bsrc, wvec, msb, Bms):
        for oc in range(8):
            ti = rot["tsc"] % 2
            rot["tsc"] += 1
            STT(tsc[ti], src[oc], wvec[:, oc:oc + 1], msb[:], ALU.mult, ALU.mult, [Bsrc[oc], Bms, Bc], [Btsc[ti]])
            TT("pool", xT[:, oc * T:(oc + 1) * T], xT[:, oc * T:(oc + 1) * T], tsc[ti], ALU.add, [BxT[oc], Btsc[ti]], [BxT[oc]])

    def stage6():
        msb, Bms = statbank()
        for j in range(2):
            rt_o, rb_o = load_slot(SL_O + j, 4096)
            for jj in range(4):
                oc = j * 4 + jj
                bk_, Bbk_ = nb()
                MM([(bk_[:], rt_o[:, kc * 512 + jj * 128: kc * 512 + (jj + 1) * 128], merged[kc], kc == 0, kc == 7) for kc in range(8)],
                   [rb_o] + Bmerged, [Bbk_])
                qi = rot["sq"] % 2
                rot["sq"] += 1
                ACT(sq[qi][:, 0:512], bk_[:], AF.Square, [Bbk_], [Bsq[qi]])
                COPY("dve", mix[oc], bk_[:], [Bbk_], [Bmix[oc]])
                MM([(msb[:], onesb[:], sq[qi][:, 0:512], oc == 0, oc == 7)], [Bsq[qi], Bc], [Bms])
        rmsnorm_rstd(msb, Bms, D)
        out_norm_residual(mix, Bmix, pmw2, msb, Bms)

    def stage7():
        msb, Bms = statbank()
        for k in range(NKC):
            qi = rot["sq"] % 2
            rot["sq"] += 1
            ACT(sq[qi][:, 0:512], xT[:, k * T:(k + 1) * T], AF.Square, [BxT[k]], [Bsq[qi]])
            MM([(msb[:], onesb[:], sq[qi][:, 0:512], k == 0, k == 7)], [Bsq[qi], Bc], [Bms])
        rmsnorm_rstd(msb, Bms, D)
        for k in range(NKC):
            STT(hT[:, k * T:(k + 1) * T], xT[:, k * T:(k + 1) * T], pfw[:, k:k + 1], msb[:], ALU.mult, ALU.mult,
                [BxT[k], Bms, Bc], [BhT[k]])

    def stage8():
        for j in range(11):
            rt, rb = load_slot(SL_GU + j, 4096)
            for half in range(2):
                bg_, Bbg_ = fm_proj(rt, rb, half, NKC, hT, BhT, stride=512, off=0)
                bu_, Bbu_ = fm_proj(rt, rb, half, NKC, hT, BhT, stride=512, off=256)
                si = rot["sgate"] % 2
                rot["sgate"] += 1
                ACT(sgate[si], bg_[:], AF.Silu, [Bbg_], [Bsgate[si]])
                TT("dve", hid[2 * j + half], bu_[:], sgate[si], ALU.mult, [Bbu_, Bsgate[si]], [Bhid[2 * j + half]])

    def stage9():
        msb, Bms = statbank()
        for oc in range(8):
            rt, rb = load_slot(SL_D + oc, NFC * 128)
            bk_, Bbk_ = nb()
            MM([(bk_[:], rt[:, kc * 128:(kc + 1) * 128], hid[kc], kc == 0, kc == NFC - 1) for kc in range(NFC)],
               [rb] + Bhid, [Bbk_])
            qi = rot["sq"] % 2
            rot["sq"] += 1
            ACT(sq[qi][:, 0:512], bk_[:], AF.Square, [Bbk_], [Bsq[qi]])
            COPY("dve", mix[oc], bk_[:], [Bbk_], [Bmix[oc]])
            MM([(msb[:], onesb[:], sq[qi][:, 0:512], oc == 0, oc == 7)], [Bsq[qi], Bc], [Bms])
        rmsnorm_rstd(msb, Bms, D)
        out_norm_residual(mix, Bmix, pfw2, msb, Bms)

    def stage10(tile_idx):
        for blk in range(4):
            si = xtm_ctr[0] % NXTM
            xtm_ctr[0] += 1
            for half in range(2):
                bk_, Bbk_ = nb()
                TR([(bk_[:, j * 128:(j + 1) * 128], xT[:, (half * 4 + j) * T + blk * 128:(half * 4 + j) * T + (blk + 1) * 128]) for j in range(4)],
                   identf[:], BxT[half * 4:half * 4 + 4] + [Bc], [Bbk_])
                COPY("act", xtm[si][:, half * 512:(half + 1) * 512], bk_[:], [Bbk_], [Bxtm[si]])
            r0 = tile_idx * T + blk * 128
            xt_ = xtm[si]
            store_ops.append(S.dma("sp", lambda e, xt_=xt_, r0=r0: e.dma_start(out=y[r0:r0 + 128, :], in_=xt_[:]),
                                   reads=[Bxtm[si]], sembuf=Bxtm[si]))

    Bbnd = Buf("bnd")
    S.op("pool", lambda e: e.memset(Sst[0][:], 0.0), writes=[BS[0]])
    S.op("pool", lambda e: e.memset(Sst[1][:], 0.0), writes=[BS[1]])
    p1_tiles = list(range(NTALL - 1, 0, -1))
    if n_p1_tiles is not None:
        p1_tiles = p1_tiles[:n_p1_tiles] if n_p1_tiles > 0 else []
    p1_slots = None
    for pc in pieces_A:
        emit_piece(pc)
    emit_piece_store()
    restB = list(pieces_B)
    per_tile = -(-len(restB) // max(len(p1_tiles), 1)) if p1_tiles else len(restB)
    if p1_tiles:
        p1_slots = [load_slot(G_ZB, 4096), load_slot(G_I, 4096)]

    def tg(t):
        S.tag = t

    def g_casts(n):
        for _ in range(n):
            if restB:
                emit_piece(restB.pop(0))
            yield

    dec1b = sb("dec1b", 32)
    Bdec1b = Buf("dec1b")
    p1sets = [dict(kt=krTM[1], Bkt=BkrTM[1], vt=vTM, Bvt=BvTM, dct=dec[1], Bdct=Bdec[1]),
              dict(kt=krTM[0], Bkt=BkrTM[0], vt=[PT[0][0], PT[0][1], PT[1][0], PT[1][1]],
                   Bvt=[BPT[0][0], BPT[0][1], BPT[1][0], BPT[1][1]], dct=dec1b, Bdct=Bdec1b)]
    prev_chain = None
    for idx, j in enumerate(p1_tiles):
        ps = p1sets[idx % 2]
        tg(f"p1 t{j}")

        def after(j=j):
            if j <= NT2:
                S.dma("sp", lambda e: e.dma_start(out=bnd[j - 1], in_=Sst[1][:]), reads=[BS[1]], writes=[Bbnd], sembuf=Bbnd)

        genA = seq(stage1(j), stage2a(True, vt=ps["vt"], Bvt=ps["Bvt"], dct=ps["dct"], Bdct=ps["Bdct"]),
                   kr_transposes(1, kt=ps["kt"], Bkt=ps["Bkt"]))
        if prev_chain is None:
            interleave(genA, g_casts(per_tile))
        else:
            interleave(prev_chain, genA, g_casts(per_tile))
        prev_chain = state_chain(1, range(7, -1, -1), False, after=after, **ps)
    if prev_chain is not None:
        run(prev_chain)
    tg("cast rest")
    while restB:
        emit_piece(restB.pop(0))
    emit_piece_store()
    for j in range(NT2):
        if p1_tiles:
            S.dma("sp", lambda e, j=j: e.dma_start(out=Sst[1][:], in_=bnd[j]), reads=[Bbnd], writes=[BS[1]], sembuf=BS[1])
        else:
            S.op("pool", lambda e: e.memset(Sst[1][:], 0.0), writes=[BS[1]])
        tg(f"p2 t{j} s1")
        run(stage1(j))
        tg(f"p2 t{j} s2")
        if last_stage >= 2:
            run(stage2a(False))
        for si_, fn_ in ((3, stage3), (5, stage5), (6, stage6), (7, stage7),
                         (8, stage8), (9, stage9)):
            if si_ <= last_stage:
                tg(f"p2 t{j} s{si_}")
                fn_()
        tg(f"p2 t{j} s10")
        if not skip_s10:
            stage10(j)
    S.final_wait("sp", (store_ops[-NXTM:] if len(store_ops) >= NXTM else store_ops) + [b.writer for b in Bwsc if b.writer is not None])
    S.emit()
    global LAST_NAMES
    LAST_NAMES = S.names
    return nc


def _host_consts():
    identf = np.eye(128, dtype=np.float32)
    s = np.arange(128)[:, None]
    t = np.arange(128)[None, :]
    same = (s // 64) == (t // 64)
    mf = (same & (s <= t)).astype(np.float32)
    mb = (same & (s >= t)).astype(np.float32)
    masks = np.zeros((4, 128, 512), np.float32)
    masks[0] = np.tile(mf, (1, 4))
    masks[1] = np.tile(mb, (1, 4))
    tt = np.arange(512)
    masks[2] = np.broadcast_to((tt % 64 != 0).astype(np.float32), (128, 512))
    masks[3] = np.broadcast_to((tt % 64 != 63).astype(np.float32), (128, 512))
    return identf, masks


def _weights_for(flip, w_in, lb_logits, sg_spatial_w, sg_spatial_b):
    q, i_, ff, fb, g, u, v = [w_in[:, k * 512:(k + 1) * 512] for k in range(7)]
    ga = w_in[:, 3584:4608]
    gb = w_in[:, 4608:5632]
    zf, zb = (fb, ff) if flip else (ff, fb)
    w_in_r = np.ascontiguousarray(np.concatenate([zf, zb, q, i_, v, g, u, ga, gb], axis=1))
    lbl = lb_logits[:, ::-1, :] if flip else lb_logits
    ws = sg_spatial_w
    bs = sg_spatial_b
    if flip:
        ws = ws[:, ::-1, ::-1]
        bs = bs[:, ::-1]
    wst = np.ascontiguousarray(np.transpose(ws, (2, 0, 1)))
    lblT = np.transpose(lbl.reshape(2, 2, 4, 128), (3, 0, 1, 2)).reshape(128, 16)
    return w_in_r, lblT, wst, np.ascontiguousarray(bs.reshape(1, 512))


_PROGRAM_CACHE = {}
LAST_NAMES = {}


def _run(x, pre_mix_w, w_in, lb_logits, hg_norm_w, sg_ln_w, sg_ln_b, sg_spatial_w, sg_spatial_b,
         w_proj_a, w_proj_b, w_out, post_mix_w, pre_ffn_w, w_gate, w_up, w_down, post_ffn_w, **build_kw):
    x = np.asarray(x, np.float32)
    B, L, _ = x.shape
    f = lambda a: np.ascontiguousarray(np.asarray(a, np.float32))
    identf, masks = _host_consts()
    vecs = np.stack([f(pre_mix_w)[0], f(post_mix_w)[0], f(pre_ffn_w)[0], f(post_ffn_w)[0]], axis=0)
    vecsT = np.transpose(vecs.reshape(4, 8, 128), (2, 0, 1)).reshape(128, 32)
    hgwT = np.transpose(f(hg_norm_w)[0].reshape(4, 128), (1, 0))
    lnwT = np.transpose(f(sg_ln_w)[0].reshape(4, 128), (1, 0))
    lnb_bc = np.ascontiguousarray(np.broadcast_to(f(sg_ln_b)[0][None, :], (128, 512)))
    common = dict(w_pa=f(w_proj_a)[0], w_pb=f(w_proj_b)[0], w_out=f(w_out)[0], w_gate=f(w_gate)[0], w_up=f(w_up)[0],
                  w_down=f(w_down)[0], lnb_bc=lnb_bc, identf=identf, masks=masks)
    per_flip = []
    for flip in (False, True):
        w_in_r, lbl, wst, bs = _weights_for(flip, f(w_in)[0], f(lb_logits), f(sg_spatial_w)[0], f(sg_spatial_b)[0])
        smalls = np.ascontiguousarray(np.concatenate([vecsT, hgwT, lnwT, lbl], axis=1).astype(np.float32))
        per_flip.append(dict(w_in=w_in_r, smalls=smalls, wst=wst, bs=bs))
    in_maps = []
    for b in range(B):
        for flip in (False, True):
            xs = x[b, ::-1] if flip else x[b]
            m = dict(common)
            m.update(per_flip[int(flip)])
            m["xs"] = np.ascontiguousarray(xs)
            in_maps.append(m)
    key = (L, tuple(sorted(build_kw.items())))
    nc = build_program(L, **build_kw)
    res = run_bass_kernel_spmd(nc, in_maps, core_ids=list(range(len(in_maps))))
    own = L // 2
    out = np.zeros((B, L, D), np.float32)
    for b in range(B):
        y0 = res.results[2 * b]["y"]
        y1 = res.results[2 * b + 1]["y"]
        out[b, :own] = y0
        out[b, own:] = y1[::-1]
    return out


def kernel(**inputs):
    return _run(**inputs)
```

```python
import numpy as np
import ml_dtypes
import concourse.bass as bass
import concourse.mybir as mybir
from concourse.bass_utils import run_bass_kernel_spmd

F32 = mybir.dt.float32
BF16 = mybir.dt.bfloat16
AF = mybir.ActivationFunctionType
ALU = mybir.AluOpType

D = 1024
NH = 4
HD = 128
DFF = 2816
EPS = 1e-6
T = 512
NKC = 8
NFC = 22
G_ZF, G_ZB, G_Q, G_I, G_V, G_G, G_U, G_GA0, G_GA1, G_GB0, G_GB1 = range(11)
SL_AB = 11
SL_O = 13
SL_GU = 15
SL_D = 26
NSLOT = 34
NRING = 4


class Buf:
    __slots__ = ("name", "writer", "readers", "dsem", "dtotal", "alias", "excl")

    def __init__(self, name, excl=False):
        self.name = name
        self.excl = excl
        self.writer = None
        self.readers = []
        self.dsem = None
        self.dtotal = 0
        self.alias = ()


class Op:
    __slots__ = ("eng", "fn", "deps", "milestone", "count", "is_dma", "dma_sem", "dma_val", "tag")

    def __init__(self, eng, fn, deps):
        self.eng = eng
        self.fn = fn
        self.deps = deps
        self.milestone = False
        self.count = None
        self.is_dma = False
        self.dma_sem = None
        self.dma_val = 0


class Sched:
    ENGS = ("pe", "act", "dve", "pool", "sp")

    def __init__(self, nc):
        self.nc = nc
        self.ops = {e: [] for e in self.ENGS}
        self.sems = {e: nc.alloc_semaphore("sem_" + e) for e in self.ENGS}
        self.tag = ""
        self.names = {}

    def _deps(self, reads, writes):
        deps = []
        for b in reads:
            if b.writer is not None:
                deps.append(b.writer)
            if b.excl:
                deps.extend(b.readers)
        for b in writes:
            if b.writer is not None:
                deps.append(b.writer)
            deps.extend(b.readers)
            for a in b.alias:
                if a.writer is not None:
                    deps.append(a.writer)
                deps.extend(a.readers)
        return deps

    def _update(self, op, reads, writes):
        for b in writes:
            b.writer = op
            b.readers = []
        for b in reads:
            if b in writes:
                continue
            if not op.is_dma:
                b.readers = [r for r in b.readers if r.is_dma or r.eng != op.eng]
            b.readers.append(op)

    def op(self, eng, fn, reads=(), writes=()):
        o = Op(eng, fn, self._deps(reads, writes))
        o.tag = self.tag
        self.ops[eng].append(o)
        self._update(o, reads, writes)
        return o

    def dma(self, eng, fn, reads=(), writes=(), sembuf=None):
        o = Op(eng, fn, self._deps(reads, writes))
        o.tag = self.tag
        o.is_dma = True
        if sembuf.dsem is None:
            sembuf.dsem = self.nc.alloc_semaphore("dsem_" + sembuf.name)
        sembuf.dtotal += 16
        o.dma_sem = sembuf.dsem
        o.dma_val = sembuf.dtotal
        self.ops[eng].append(o)
        self._update(o, reads, writes)
        return o

    def final_wait(self, eng, deps):
        o = Op(eng, None, list(deps))
        self.ops[eng].append(o)
        return o

    def emit(self):
        nc = self.nc
        for e in self.ENGS:
            for o in self.ops[e]:
                for d in o.deps:
                    if not d.is_dma:
                        d.milestone = True
        for e in self.ENGS:
            c = 0
            for o in self.ops[e]:
                if o.milestone and not o.is_dma:
                    c += 1
                    o.count = c

        def replay(e):
            def body(engine):
                seen = {}
                for o in self.ops[e]:
                    need = {}
                    for d in o.deps:
                        if d.is_dma:
                            key, val, sem = ("d", id(d.dma_sem)), d.dma_val, d.dma_sem
                        else:
                            key, val, sem = ("e", d.eng), d.count, self.sems[d.eng]
                        if seen.get(key, 0) >= val:
                            continue
                        if key not in need or need[key][1] < val:
                            need[key] = (sem, val)
                    for key, (sem, val) in need.items():
                        engine.wait_ge(sem, val)
                        seen[key] = val
                    if o.fn is None:
                        continue
                    inst = o.fn(engine)
                    try:
                        self.names[inst.ins.name] = o.tag
                    except Exception:
                        pass
                    if o.is_dma:
                        inst.then_inc(o.dma_sem, 16)
                    elif o.milestone:
                        inst.then_inc(self.sems[e], 1)
            return body

        with nc.Block() as block:
            block.tensor(replay("pe"))
            block.scalar(replay("act"))
            block.vector(replay("dve"))
            block.gpsimd(replay("pool"))
            block.sync(replay("sp"))


def build_program(L, n_own_tiles=None, n_p1_tiles=None, debug=False, last_stage=10, do_cast=1, skip_s10=0, s1_steps=9, s3_steps=9):
    own = L // 2
    NT2 = own // T if n_own_tiles is None else n_own_tiles
    NTALL = L // T
    nc = bass.Bass("TRN2", target_bir_lowering=False)
    S = Sched(nc)

    def din(name, shape, dt=F32):
        return nc.dram_tensor(name, list(shape), dt, kind="ExternalInput").ap()

    xs = din("xs", [L, D])
    w_in = din("w_in", [D, 5632])
    w_pa = din("w_pa", [512, D])
    w_pb = din("w_pb", [512, D])
    w_out = din("w_out", [D, D])
    w_gate = din("w_gate", [D, DFF])
    w_up = din("w_up", [D, DFF])
    w_down = din("w_down", [DFF, D])
    smalls_d = din("smalls", [128, 56])
    lnb_d = din("lnb_bc", [128, 512])
    wst_d = din("wst", [128, 4, 128])
    bs_d = din("bs", [1, 512])
    identf_d = din("identf", [128, 128])
    masks_d = din("masks", [4, 128, 512])
    y = nc.dram_tensor("y", [own, D], F32, kind="ExternalOutput").ap()
    wsc = nc.dram_tensor("wsc", [NSLOT, 128, 4096], BF16).ap()
    bnd = nc.dram_tensor("bnd", [max(NT2, 1), 128, 512], F32).ap()

    def sb(name, cols, dt=F32, parts=128):
        return nc.alloc_sbuf_tensor(name + "_sb", [parts, cols], dt)

    Bc = Buf("const")
    identf = sb("identf", 128)
    identb = sb("identb", 128, BF16)
    onesb = sb("onesb", 128, BF16)
    onesf = sb("onesf", 128)
    masks = sb("masks", 4 * 512)
    maskF4, maskB4 = masks[:, 0:512], masks[:, 512:1024]
    mF, mB = masks[:, 1024:1536], masks[:, 1536:2048]
    smalls = sb("smalls", 56)
    vecsT = smalls[:, 0:32]
    hgw = smalls[:, 32:36]
    lnw = smalls[:, 36:40]
    lblT = smalls[:, 40:56]
    oml = sb("oml", 8)
    noml = sb("noml", 8)
    lnb_bc = sb("lnb_bc", 512)
    wstf = sb("wstf", 512)
    wstb = sb("wstb", 512, BF16)
    bs_sb = sb("bs_sb", 512, F32, parts=1)
    extra = sb("extra", 512)

    def cdma(out, in_):
        S.dma("sp", lambda e: e.dma_start(out=out, in_=in_), writes=[Bc], sembuf=Bc)

    cdma(identf[:], identf_d)
    cdma(masks[:].rearrange("p (a c) -> p a c", a=4), masks_d.rearrange("a p c -> p a c"))
    cdma(smalls[:], smalls_d)
    cdma(lnb_bc[:], lnb_d)
    cdma(wstf[:].rearrange("p (g t) -> p g t", g=4), wst_d)
    cdma(bs_sb[:], bs_d)

    Bwsc = [Buf(f"wsc{i}") for i in range(NSLOT)]
    w_in_v = w_in.rearrange("(kc p) c -> p kc c", p=128)
    pa_v = w_pa.rearrange("(kc p) c -> p kc c", p=128)
    pb_v = w_pb.rearrange("(kc p) c -> p kc c", p=128)
    wo_v = w_out.rearrange("(kc p) c -> p kc c", p=128)
    wg_v = w_gate.rearrange("(kc p) c -> p kc c", p=128)
    wu_v = w_up.rearrange("(kc p) c -> p kc c", p=128)
    wd_v = w_down.rearrange("(kc p) c -> p kc c", p=128)
    pieces_A, pieces_B = [], []

    def win_pieces(g, lst):
        for k0 in (0, 4):
            lst.append((g, k0 * 512, 2048, [(0, 4, 512, 512, w_in_v[:, k0:k0 + 4, g * 512:(g + 1) * 512])]))

    win_pieces(G_ZB, pieces_A)
    win_pieces(G_I, pieces_A)
    for g in range(11):
        if g not in (G_ZB, G_I):
            win_pieces(g, pieces_B)
    for j in range(2):
        pieces_B.append((SL_AB + j, 0, 2048, [(0, 4, 512, 512, pa_v[:, :, j * 512:(j + 1) * 512])]))
        pieces_B.append((SL_AB + j, 2048, 2048, [(0, 4, 512, 512, pb_v[:, :, j * 512:(j + 1) * 512])]))
    for j in range(2):
        for k0 in (0, 4):
            pieces_B.append((SL_O + j, k0 * 512, 2048, [(0, 4, 512, 512, wo_v[:, k0:k0 + 4, j * 512:(j + 1) * 512])]))
    for j in range(11):
        for k0 in (0, 4):
            pieces_B.append((SL_GU + j, k0 * 512, 2048, [(0, 4, 256, 512, wg_v[:, k0:k0 + 4, j * 256:(j + 1) * 256]),
                                                       (256, 4, 256, 512, wu_v[:, k0:k0 + 4, j * 256:(j + 1) * 256])]))
    for oc in range(8):
        pieces_B.append((SL_D + oc, 0, 2048, [(0, 16, 128, 128, wd_v[:, 0:16, oc * 128:(oc + 1) * 128])]))
        pieces_B.append((SL_D + oc, 2048, 768, [(0, 6, 128, 128, wd_v[:, 16:22, oc * 128:(oc + 1) * 128])]))

    xT = sb("xT", NKC * T)
    BxT = [Buf(f"xT{k}") for k in range(NKC)]
    hT = sb("hT", NKC * T, BF16)
    BhT = [Buf(f"hT{k}") for k in range(NKC)]
    sq = [sb(f"sq{i}", 1024, BF16) for i in range(2)]
    Bsq = [Buf(f"sq{i}") for i in range(2)]
    NXTM = 3
    xtm = [sb(f"xtm{i}", D) for i in range(NXTM)]
    Bxtm = [Buf(f"xtm{i}") for i in range(NXTM)]
    Sst = [sb(f"Sst{d}", 512) for d in range(2)]
    BS = [Buf(f"S{d}") for d in range(2)]
    dec = [sb(f"dec{d}", 32) for d in range(2)]
    Bdec = [Buf(f"dec{d}") for d in range(2)]
    ring = [sb(f"ring{i}", 4096, BF16) for i in range(NRING)]
    Bring = [Buf(f"ring{i}") for i in range(NRING)]
    sgT = sb("sgT", 4 * T, BF16)
    BsgT = [Buf(f"sgT{j}") for j in range(4)]
    guT = sb("guT", 4 * T, BF16)
    BguT = [Buf(f"guT{j}") for j in range(4)]
    xhat = sb("xhat", 4 * 512, BF16)
    Bxhat = [Buf(f"xhat{b}") for b in range(4)]
    gv = [sb(f"gv{i}", 512) for i in range(4)]
    Bgv = [Buf(f"gv{i}") for i in range(4)]
    stat = [sb(f"stat{i}", 16) for i in range(4)]
    Bstat = [Buf(f"stat{i}") for i in range(4)]
    onT = sb("onT", 4 * T, BF16)
    BonT = [Buf(f"onT{b}") for b in range(4)]
    sTt = sb("sT", 4 * T, BF16)
    BsT = [Buf(f"sT{b}") for b in range(4)]
    sga = [sb(f"sga{i}", T, BF16) for i in range(2)]
    Bsga = [Buf(f"sga{i}") for i in range(2)]
    sgb = [sb(f"sgb{i}", T, BF16) for i in range(2)]
    Bsgb = [Buf(f"sgb{i}") for i in range(2)]
    osq = [sb(f"osq{i}", 512, BF16) for i in range(2)]
    Bosq = [Buf(f"osq{i}") for i in range(2)]
    t1 = [sb(f"t1_{i}", 512, BF16) for i in range(2)]
    Bt1 = [Buf(f"t1_{i}") for i in range(2)]
    tmpr = [sb(f"tmpr{i}", 512) for i in range(2)]
    Btmpr = [Buf(f"tmpr{i}") for i in range(2)]

    ARENA_B = 72 * 1024
    arena = sb("arena", ARENA_B // 2, BF16)

    def av(off, nbytes, dt):
        a = arena[:, off // 2:(off + nbytes) // 2]
        if dt == F32:
            a = a.bitcast(F32)
        return a

    K = 1024
    gun = []
    for i in range(3):
        base = i * 8 * K
        gun.append(dict(sp=av(base, 2 * K, F32), lf=av(base + 2 * K, 2 * K, F32), c=av(base + 4 * K, 2 * K, F32),
                        e1=av(base + 6 * K, K, BF16), e2=av(base + 7 * K, K, BF16),
                        Bsp=Buf(f"sp{i}"), Blf=Buf(f"lf{i}"), Bc=Buf(f"c{i}"), Be1=Buf(f"e1{i}"), Be2=Buf(f"e2{i}")))
    kr = [av(24 * K + u * K, K, BF16) for u in range(8)]
    Bkr = [Buf(f"kr{u}") for u in range(8)]
    qr = [av(32 * K + u * K, K, BF16) for u in range(8)]
    Bqr = [Buf(f"qr{u}") for u in range(8)]
    krTM = [[av(40 * K + (d * 4 + b) * K, K, BF16) for b in range(4)] for d in range(2)]
    BkrTM = [[Buf(f"krTM{d}{b}") for b in range(4)] for d in range(2)]
    vTM = [av(48 * K + b * K, K, BF16) for b in range(4)]
    BvTM = [Buf(f"vTM{b}") for b in range(4)]
    PT = [[av(52 * K + (d * 2 + i) * K, K, BF16) for i in range(2)] for d in range(2)]
    BPT = [[Buf(f"PT{d}{i}") for i in range(2)] for d in range(2)]
    Sbf = [[av(56 * K + (d * 8 + n) * K, K, BF16) for n in range(8)] for d in range(2)]
    BSbf = [[Buf(f"Sbf{d}{n}") for n in range(8)] for d in range(2)]
    hid = [av(j * K, K, BF16) for j in range(NFC)]
    Bhid = [Buf(f"hid{j}") for j in range(NFC)]
    mix = [av(22 * K + oc * 2 * K, 2 * K, F32) for oc in range(8)]
    Bmix = [Buf(f"mix{oc}") for oc in range(8)]
    merged = [av(38 * K + oc * K, K, BF16) for oc in range(8)]
    Bmerged = [Buf(f"mg{oc}") for oc in range(8)]
    m1 = [av(46 * K + i * 2 * K, 2 * K, F32) for i in range(2)]
    Bm1 = [Buf(f"m1{i}") for i in range(2)]
    m2 = [av(50 * K + i * 2 * K, 2 * K, F32) for i in range(2)]
    Bm2 = [Buf(f"m2{i}") for i in range(2)]
    tsc = [av(54 * K + i * 2 * K, 2 * K, F32) for i in range(2)]
    Btsc = [Buf(f"tsc{i}") for i in range(2)]
    sgate = [av(58 * K + i * K, K, BF16) for i in range(2)]
    Bsgate = [Buf(f"sgate{i}") for i in range(2)]

    sgaL = [av(oc * K, K, BF16) for oc in range(8)]
    BsgaL = [Buf(f"sgaL{oc}") for oc in range(8)]
    sgbL = [av(8 * K + oc * K, K, BF16) for oc in range(8)]
    BsgbL = [Buf(f"sgbL{oc}") for oc in range(8)]
    early = []
    for g_ in gun:
        early += [g_["Bsp"], g_["Blf"], g_["Bc"], g_["Be1"], g_["Be2"]]
    early += Bkr + Bqr + BkrTM[0] + BkrTM[1] + BvTM + BPT[0] + BPT[1] + BSbf[0] + BSbf[1]
    late = Bhid + Bmix + Bmerged + Bm1 + Bm2 + Btsc + Bsgate
    gunB = list(early[:15])
    early += BsgaL + BsgbL
    for b in early:
        b.alias = tuple(late)
    for b in late:
        b.alias = tuple(early)
    for b in gunB:
        b.alias = tuple(list(b.alias) + BsgaL + BsgbL)
    for b in BsgaL + BsgbL:
        b.alias = tuple(list(b.alias) + gunB)

    stf = [av(56 * K + i * 8 * K, 8 * K, F32) for i in range(2)]
    Bstf = [Buf(f"stf{i}") for i in range(2)]
    stb = [av(32 * K + i * 4 * K, 4 * K, BF16) for i in range(2)]
    Bstb = [Buf(f"stb{i}") for i in range(2)]
    for b_ in Bstf:
        b_.alias = tuple(BSbf[0] + BSbf[1] + late)
    for b_ in Bstb:
        b_.alias = tuple(Bqr + late)
    for b_ in BSbf[0] + BSbf[1]:
        b_.alias = tuple(list(b_.alias) + Bstf)
    for b_ in Bqr:
        b_.alias = tuple(list(b_.alias) + Bstb)
    for b_ in late:
        b_.alias = tuple(list(b_.alias) + Bstf + Bstb)
    piece_ctr = [0]
    pending_store = []

    def emit_piece_store():
        if pending_store:
            (slot, doff, n, i) = pending_store.pop(0)
            S.dma("sp", lambda e: e.dma_start(out=wsc[slot][:, doff:doff + n], in_=stb[i][:, 0:n]),
                  reads=[Bstb[i]], writes=[Bwsc[slot]], sembuf=Bstb[i])

    def emit_piece(piece):
        if not do_cast:
            return
        (slot, doff, n, parts) = piece
        i = piece_ctr[0] % 2
        piece_ctr[0] += 1
        for (soff, kcn, cols, rowlen, src) in parts:
            dstv = stf[i][:, 0:kcn * rowlen].rearrange("p (k c) -> p k c", c=rowlen)[:, :, soff:soff + cols]
            S.dma("sp", lambda e, dstv=dstv, src=src: e.dma_start(out=dstv, in_=src), writes=[Bstf[i]], sembuf=Bstf[i])
        emit_piece_store()
        S.op("pool", lambda e: e.tensor_copy(out=stb[i][:, 0:n], in_=stf[i][:, 0:n]), reads=[Bstf[i]], writes=[Bstb[i]])
        pending_store.append((slot, doff, n, i))

    banks = [nc.alloc_psum_tensor(f"bank{i}", [128, 512], F32) for i in range(8)]
    Bbank = [Buf(f"bank{i}", excl=True) for i in range(8)]
    bank_ctr = [0]

    def nb():
        i = bank_ctr[0] % 7
        bank_ctr[0] += 1
        return banks[i], Bbank[i]

    def statbank():
        return banks[7], Bbank[7]

    def ACT(out, in_, func, r, w, scale=None, bias=None, accum=None):
        kw = {}
        if scale is not None:
            kw["scale"] = scale
        if bias is not None:
            kw["bias"] = bias
        if accum is not None:
            kw["accum_out"] = accum
        return S.op("act", lambda e: e.activation(out=out, in_=in_, func=func, **kw), reads=r, writes=w)

    def TT(eng, out, in0, in1, op, r, w):
        return S.op(eng, lambda e: e.tensor_tensor(out=out, in0=in0, in1=in1, op=op), reads=r, writes=w)

    def TS(eng, out, in0, s1, s2, op0, op1, r, w):
        if op1 is None:
            return S.op(eng, lambda e: e.tensor_scalar(out=out, in0=in0, scalar1=s1, scalar2=None, op0=op0), reads=r, writes=w)
        return S.op(eng, lambda e: e.tensor_scalar(out=out, in0=in0, scalar1=s1, scalar2=s2, op0=op0, op1=op1), reads=r, writes=w)

    def STT(out, in0, sc, in1, op0, op1, r, w):
        return S.op("dve", lambda e: e.scalar_tensor_tensor(out=out, in0=in0, scalar=sc, in1=in1, op0=op0, op1=op1), reads=r, writes=w)

    def COPY(eng, out, in_, r, w):
        if eng == "act":
            return ACT(out, in_, AF.Copy, r, w)
        return S.op(eng, lambda e: e.tensor_copy(out=out, in_=in_), reads=r, writes=w)

    def MM(specs, r, w):
        def fn(e):
            inst = None
            for (o_, l_, r_, st, sp_) in specs:
                inst = e.matmul(o_, l_, r_, start=st, stop=sp_)
            return inst
        return S.op("pe", fn, reads=r, writes=w)

    def TR(specs, ident, r, w):
        def fn(e):
            inst = None
            for (o_, i_) in specs:
                inst = e.transpose(o_, i_, ident)
            return inst
        return S.op("pe", fn, reads=r, writes=w)

    S.op("pool", lambda e: e.memset(onesf[:], 1.0), writes=[Bc], reads=[Bc])
    S.op("pool", lambda e: e.memset(onesb[:], 1.0), writes=[Bc], reads=[Bc])
    S.op("dve", lambda e: e.tensor_copy(out=identb[:], in_=identf[:]), writes=[Bc], reads=[Bc])
    lv = lblT.rearrange("p (l x) -> p l x", l=2)
    TT("dve", oml[:], lv[:, 0, :], lv[:, 1, :], ALU.subtract, [Bc], [Bc])
    ACT(oml[:], oml[:], AF.Exp, [Bc], [Bc])
    ACT(oml[:], oml[:], AF.Ln, [Bc], [Bc], bias=1.0)
    ACT(oml[:], oml[:], AF.Exp, [Bc], [Bc], scale=-1.0)
    TS("dve", noml[:], oml[:], -1.0, None, ALU.mult, None, [Bc], [Bc])
    COPY("act", wstb[:], wstf[:], [Bc], [Bc])
    bk, Bbk = nb()
    specs = []
    for g in range(4):
        specs.append((bk[:, g * 128:(g + 1) * 128], lnb_bc[:, g * 128:(g + 1) * 128], wstf[:, g * 128:(g + 1) * 128], True, False))
        specs.append((bk[:, g * 128:(g + 1) * 128], onesf[0:1, :], bs_sb[0:1, g * 128:(g + 1) * 128], False, True))
    MM(specs, [Bc], [Bbk])
    COPY("dve", extra[:], bk[:], [Bbk], [Bc])

    pmw = vecsT[:, 0:8]
    pmw2 = vecsT[:, 8:16]
    pfw = vecsT[:, 16:24]
    pfw2 = vecsT[:, 24:32]

    ring_ctr = [0]

    def load_slot(slot, ncols, wbuf=None):
        wbuf = Bwsc[slot]
        i = ring_ctr[0] % NRING
        ring_ctr[0] += 1
        rt, rb = ring[i], Bring[i]
        S.dma("sp", lambda e: e.dma_start(out=rt[:, 0:ncols], in_=wsc[slot][:, 0:ncols]),
              reads=[wbuf], writes=[rb], sembuf=rb)
        return rt, rb

    store_ops = []
    xtm_ctr = [0]
    gun_ctr = [0]
    rot = {"sq": 0, "gv": 0, "osq": 0, "m": 0, "sg": 0, "tsc": 0, "sgate": 0, "tmpr": 0, "pt": 0}

    def rmsnorm_rstd(msbank, Bms, n_feat):
        i = rot["tmpr"] % 2
        rot["tmpr"] += 1
        ACT(tmpr[i][:], msbank[:], AF.Ln, [Bms], [Btmpr[i]], scale=1.0 / n_feat, bias=EPS)
        ACT(msbank[:], tmpr[i][:], AF.Exp, [Btmpr[i]], [Bms], scale=-0.5)

    s1st = [sb(f"s1st{i}", 8) for i in range(NXTM)]
    Bs1st = [Buf(f"s1st{i}") for i in range(NXTM)]

    def stage1(tile_idx, need_xT=True):
        for blk in range(4):
            si = xtm_ctr[0] % NXTM
            xtm_ctr[0] += 1
            r0 = tile_idx * T + blk * 128
            xt_, bx_ = xtm[si], Bxtm[si]
            st_, Bst_ = s1st[si], Bs1st[si]
            S.dma("sp", lambda e, xt_=xt_, r0=r0: e.dma_start(out=xt_[:], in_=xs[r0:r0 + 128, :]), writes=[bx_], sembuf=bx_)
            qi = rot["sq"] % 2
            rot["sq"] += 1
            ACT(sq[qi][:], xt_[:], AF.Square, [bx_], [Bsq[qi], Bst_], accum=st_[:, 0:1])
            ACT(st_[:, 1:2], st_[:, 0:1], AF.Ln, [Bst_], [Bst_], scale=1.0 / D, bias=EPS)
            ACT(st_[:, 2:3], st_[:, 1:2], AF.Exp, [Bst_], [Bst_], scale=-0.5)
            TS("dve", sq[qi][:], xt_[:], st_[:, 2:3], None, ALU.mult, None, [bx_, Bst_], [Bsq[qi]])
            bk_, Bbk_ = nb()
            bkb = bk_[:].bitcast(BF16)
            TR([(bkb[:, k * 128:(k + 1) * 128], sq[qi][:, k * 128:(k + 1) * 128]) for k in range(NKC)],
               identb[:], [Bsq[qi], Bc], [Bbk_])
            TT("dve", hT[:].rearrange("p (k t) -> p k t", k=NKC)[:, :, blk * 128:(blk + 1) * 128],
               bkb.rearrange("p (k t) -> p k t", k=NKC), pmw.unsqueeze(2).to_broadcast([128, NKC, 128]), ALU.mult,
               [Bbk_, Bc], BhT)
            if need_xT:
                for half in range(2):
                    bk2, Bbk2 = nb()
                    TR([(bk2[:, j * 128:(j + 1) * 128], xt_[:, (half * 4 + j) * 128:(half * 4 + j + 1) * 128]) for j in range(4)],
                       identf[:], [bx_, Bc], [Bbk2])
                    xv = xT[:].rearrange("p (k t) -> p k t", k=NKC)[:, half * 4:half * 4 + 4, blk * 128:(blk + 1) * 128]
                    COPY("act" if half == 0 else "dve", xv, bk2[:].rearrange("p (k t) -> p k t", k=4), [Bbk2], BxT[half * 4:half * 4 + 4])
            yield

    def fm_proj(rt, rb, j, kcn, rhs_t, Brhs, stride=512, off=0):
        bk_, Bbk_ = nb()
        MM([(bk_[:], rt[:, kc * stride + off + j * 128: kc * stride + off + (j + 1) * 128], rhs_t[:, kc * T:(kc + 1) * T], kc == 0, kc == kcn - 1)
            for kc in range(kcn)], [rb] + list(Brhs), [Bbk_])
        return bk_, Bbk_

    def gate_unit(d, h, zb_, Bz_, dct=None, Bdct=None):
        u = d * 4 + h
        S.tag = S.tag.split("|")[0] + f"|gate d{d} h{h}"
        g_ = gun[gun_ctr[0] % 3]
        gun_ctr[0] += 1
        ACT(g_["sp"], zb_[:], AF.Exp, [Bz_], [g_["Bsp"]])
        ACT(g_["sp"], g_["sp"], AF.Ln, [g_["Bsp"]], [g_["Bsp"]], bias=1.0)
        ACT(g_["sp"], g_["sp"], AF.Exp, [g_["Bsp"]], [g_["Bsp"]], scale=-1.0)
        ACT(g_["lf"], g_["sp"], AF.Ln, [g_["Bsp"], Bc], [g_["Blf"]], scale=noml[:, u:u + 1], bias=1.0)
        if d == 0:
            S.op("dve", lambda e: e.tensor_tensor_scan(out=g_["c"], data0=mF, data1=g_["lf"], initial=0.0, op0=ALU.mult, op1=ALU.add),
                 reads=[g_["Blf"], Bc], writes=[g_["Bc"]])
        else:
            S.op("dve", lambda e: e.tensor_tensor_scan(out=g_["c"][:, ::-1], data0=mB[:, ::-1], data1=g_["lf"][:, ::-1], initial=0.0,
                                                       op0=ALU.mult, op1=ALU.add),
                 reads=[g_["Blf"], Bc], writes=[g_["Bc"]])
        ACT(g_["e2"], g_["c"], AF.Exp, [g_["Bc"]], [g_["Be2"]], scale=-1.0)
        cl = g_["c"][:, 63::64] if d == 0 else g_["c"][:, 0::64]
        if dct is None:
            dct, Bdct = dec[d], Bdec[d]
        ACT(dct[:, h * 8:(h + 1) * 8], cl, AF.Exp, [g_["Bc"]], [Bdct])
        STT(kr[u], g_["sp"], oml[:, u:u + 1], g_["e2"], ALU.mult, ALU.mult, [g_["Bsp"], g_["Be2"], Bc], [Bkr[u]])
        return g_

    def run(g):
        for _ in g:
            pass

    def seq(*gens):
        for g in gens:
            yield from g

    def interleave(*gens, weights=None):
        gens = list(gens)
        weights = list(weights) if weights else [1] * len(gens)
        alive = [True] * len(gens)
        while any(alive):
            for i, g in enumerate(gens):
                if not alive[i]:
                    continue
                for _ in range(weights[i]):
                    try:
                        next(g)
                    except StopIteration:
                        alive[i] = False
                        break

    def stage2a(phase1, vt=None, Bvt=None, dct=None, Bdct=None):
        vt = vTM if vt is None else vt
        Bvt = BvTM if Bvt is None else Bvt
        if phase1:
            rt_zb, rb_zb = p1_slots[0]
            rt_i, rb_i = p1_slots[1]
        else:
            rt_zf, rb_zf = load_slot(G_ZF, 4096)
            rt_zb, rb_zb = load_slot(G_ZB, 4096)
            rt_q, rb_q = load_slot(G_Q, 4096)
        for h in range(NH):
            gus = {}
            for d in ((1,) if phase1 else (0, 1)):
                rt, rb = (rt_zf, rb_zf) if d == 0 else (rt_zb, rb_zb)
                zb_, Bz_ = fm_proj(rt, rb, h, NKC, hT, BhT)
                if d == 1 and dct is not None:
                    gus[d] = gate_unit(d, h, zb_, Bz_, dct, Bdct)
                else:
                    gus[d] = gate_unit(d, h, zb_, Bz_)
                yield
            if not phase1:
                for d in (0, 1):
                    g_ = gus[d]
                    ACT(g_["e1"], g_["c"], AF.Exp, [g_["Bc"]], [g_["Be1"]])
                qb_, Bq_ = fm_proj(rt_q, rb_q, h, NKC, hT, BhT)
                for d in (0, 1):
                    g_ = gus[d]
                    TT("dve", qr[d * 4 + h], qb_[:], g_["e1"], ALU.mult, [Bq_, g_["Be1"]], [Bqr[d * 4 + h]])
                yield
        if not phase1:
            rt_i, rb_i = load_slot(G_I, 4096)
        for blk in range(4):
            bk_, Bbk_ = nb()
            MM([(bk_[:], hT[:, kc * T + blk * 128: kc * T + (blk + 1) * 128], rt_i[:, kc * 512:(kc + 1) * 512], kc == 0, kc == NKC - 1)
                for kc in range(NKC)], [rb_i] + BhT, [Bbk_])
            COPY("act", vt[blk], bk_[:], [Bbk_], [Bvt[blk]])
            yield

    def stage2b():
        rt_v, rb_v = load_slot(G_V, 4096)
        for blk in range(4):
            bk_, Bbk_ = nb()
            MM([(bk_[:], hT[:, kc * T + blk * 128: kc * T + (blk + 1) * 128], rt_v[:, kc * 512:(kc + 1) * 512], kc == 0, kc == NKC - 1)
                for kc in range(NKC)], [rb_v] + BhT, [Bbk_])
            gi = rot["gv"] % 4
            rot["gv"] += 1
            ACT(gv[gi][:], bk_[:], AF.Gelu_apprx_tanh, [Bbk_], [Bgv[gi]])
            st_, Bst_ = stat[gi], Bstat[gi]
            S.op("dve", lambda e, st_=st_, gi=gi: e.bn_stats(out=st_[:, 0:6], in_=gv[gi][:]), reads=[Bgv[gi]], writes=[Bst_])
            S.op("dve", lambda e, st_=st_: e.bn_aggr(out=st_[:, 6:8], in_=st_[:, 0:6]), reads=[Bst_], writes=[Bst_])
            stats_pending.append((blk, gi))
            yield
        rt_u, rb_u = load_slot(G_U, 4096)
        for j in range(4):
            bk_, Bbk_ = fm_proj(rt_u, rb_u, j, NKC, hT, BhT)
            ACT(guT[:, j * T:(j + 1) * T], bk_[:], AF.Gelu_apprx_tanh, [Bbk_], [BguT[j]])
            yield
        rt_g, rb_g = load_slot(G_G, 4096)
        for j in range(4):
            bk_, Bbk_ = fm_proj(rt_g, rb_g, j, NKC, hT, BhT)
            ACT(sgT[:, j * T:(j + 1) * T], bk_[:], AF.Silu, [Bbk_], [BsgT[j]])
            yield
        for (blk, gi) in stats_pending:
            st_, Bst_ = stat[gi], Bstat[gi]
            ACT(st_[:, 8:9], st_[:, 7:8], AF.Ln, [Bst_], [Bst_], bias=EPS)
            ACT(st_[:, 9:10], st_[:, 8:9], AF.Exp, [Bst_], [Bst_], scale=-0.5)
            STT(st_[:, 10:11], st_[:, 6:7], -1.0, st_[:, 9:10], ALU.mult, ALU.mult, [Bst_], [Bst_])
            TS("dve", xhat[:, blk * 512:(blk + 1) * 512], gv[gi][:], st_[:, 9:10], st_[:, 10:11], ALU.mult, ALU.add,
               [Bgv[gi], Bst_], [Bxhat[blk]])
            yield
        stats_pending.clear()

    stats_pending = []

    def state_chain(d, order, keep_bf, kt=None, Bkt=None, vt=None, Bvt=None, dct=None, Bdct=None, after=None):
        kt = krTM[d] if kt is None else kt
        Bkt = BkrTM[d] if Bkt is None else Bkt
        vt = vTM if vt is None else vt
        Bvt = BvTM if Bvt is None else Bvt
        dct = dec[d] if dct is None else dct
        Bdct = Bdec[d] if Bdct is None else Bdct
        for n in order:
            blk, c = n // 2, n % 2
            if keep_bf:
                COPY("pool", Sbf[d][n], Sst[d][:], [BS[d]], [BSbf[d][n]])
            bk_, Bbk_ = nb()
            specs = []
            rows = slice(c * 64, (c + 1) * 64)
            for h in range(NH):
                specs.append((bk_[:, h * 128:(h + 1) * 128], kt[blk][rows, h * 128:(h + 1) * 128],
                              vt[blk][rows, h * 128:(h + 1) * 128], True, False))
                specs.append((bk_[:, h * 128:(h + 1) * 128], identf[:], Sst[d][:, h * 128:(h + 1) * 128], False, True))
            MM(specs, [Bkt[blk], Bvt[blk], BS[d], Bc], [Bbk_])
            dv = dct[:].rearrange("p (h n) -> p h n", h=NH)[:, :, n:n + 1].to_broadcast([128, NH, 128])
            TT("dve", Sst[d][:].rearrange("p (h v) -> p h v", h=NH), bk_[:].rearrange("p (h v) -> p h v", h=NH), dv, ALU.mult,
               [Bbk_, Bdct], [BS[d]])
            yield
        if after is not None:
            after()
            yield

    def kr_transposes(d, kt=None, Bkt=None):
        kt = krTM[d] if kt is None else kt
        Bkt = BkrTM[d] if Bkt is None else Bkt
        for blk in range(4):
            bk_, Bbk_ = nb()
            bkb = bk_[:].bitcast(BF16)
            TR([(bkb[:, h * 128:(h + 1) * 128], kr[d * 4 + h][:, blk * 128:(blk + 1) * 128]) for h in range(NH)],
               identb[:], [Bkr[d * 4 + h] for h in range(NH)] + [Bc], [Bbk_])
            COPY("dve", kt[blk], bkb[:, 0:512], [Bbk_], [Bkt[blk]])
            yield

    def stage3():
        run(kr_transposes(0))
        run(kr_transposes(1))
        interleave(state_chain(0, range(8), True), state_chain(1, range(7, -1, -1), True), stage2b())
        def scores(blk):
            pts = []
            for d in (0, 1):
                bk_, Bbk_ = nb()
                MM([(bk_[:, h * 128:(h + 1) * 128], kr[d * 4 + h][:, blk * 128:(blk + 1) * 128],
                     qr[d * 4 + h][:, blk * 128:(blk + 1) * 128], True, True) for h in range(NH)],
                   [Bkr[d * 4 + h] for h in range(NH)] + [Bqr[d * 4 + h] for h in range(NH)], [Bbk_])
                pi = rot["pt"] % 2
                TT("dve", PT[d][pi], bk_[:], maskF4 if d == 0 else maskB4, ALU.mult, [Bbk_, Bc], [BPT[d][pi]])
                pts.append((PT[d][pi], BPT[d][pi]))
            rot["pt"] += 1
            return pts

        interleave(blocks_body_gen(scores), seq(stage4(), g_gates()))

    def blocks_body_gen(scores):
        nxt = scores(0)
        for blk in range(4 if s3_steps >= 3 else 0):
            pts = nxt
            if blk < 3:
                nxt = scores(blk + 1)
            if s3_steps < 4:
                continue
            ob, Bob = nb()
            specs = []
            for h in range(NH):
                hs = slice(h * 128, (h + 1) * 128)
                specs.append((ob[:, hs], vTM[blk][:, hs], pts[0][0][:, hs], True, False))
                specs.append((ob[:, hs], vTM[blk][:, hs], pts[1][0][:, hs], False, False))
                k_ = 0
                for d in (0, 1):
                    for c in (0, 1):
                        k_ += 1
                        specs.append((ob[:, h * 128 + c * 64: h * 128 + (c + 1) * 64], Sbf[d][2 * blk + c][:, hs],
                                      qr[d * 4 + h][:, blk * 128 + c * 64: blk * 128 + (c + 1) * 64], False, k_ == 4))
            MM(specs, [BvTM[blk], pts[0][1], pts[1][1]] + [BSbf[d][2 * blk + c] for d in (0, 1) for c in (0, 1)] + Bqr, [Bob])
            if s3_steps < 5:
                continue
            oi = rot["osq"] % 2
            rot["osq"] += 1
            ACT(osq[oi][:], ob[:], AF.Square, [Bob], [Bosq[oi]])
            msb, Bms = nb()
            MM([(msb[:], onesb[:], osq[oi][:], True, True)], [Bosq[oi], Bc], [Bms])
            ti = rot["tmpr"] % 2
            rot["tmpr"] += 1
            ACT(tmpr[ti][:], msb[:], AF.Ln, [Bms], [Btmpr[ti]], scale=1.0 / HD, bias=EPS)
            ACT(tmpr[ti][:], tmpr[ti][:], AF.Exp, [Btmpr[ti]], [Btmpr[ti]], scale=-0.5)

            def fn(e, ob=ob, oi=oi, ti=ti):
                inst = None
                for h in range(NH):
                    hs = slice(h * 128, (h + 1) * 128)
                    inst = e.scalar_tensor_tensor(out=t1[oi][:, hs], in0=ob[:, hs], scalar=hgw[:, h:h + 1], in1=tmpr[ti][:, hs],
                                                  op0=ALU.mult, op1=ALU.mult)
                return inst
            S.op("dve", fn, reads=[Bob, Btmpr[ti], Bc], writes=[Bt1[oi]])
            TT("dve", onT[:].rearrange("p (h t) -> p h t", h=NH)[:, :, blk * 128:(blk + 1) * 128],
               t1[oi][:].rearrange("p (h t) -> p h t", h=NH),
               sgT[:].rearrange("p (h t) -> p h t", h=NH)[:, :, blk * 128:(blk + 1) * 128], ALU.mult,
               [Bt1[oi]] + BsgT, [BonT[blk]])
            yield

    def stage4():
        for blk in range(4):
            bk_, Bbk_ = nb()
            MM([(bk_[:, g * 128:(g + 1) * 128], xhat[:, blk * 512 + g * 128: blk * 512 + (g + 1) * 128], wstb[:, g * 128:(g + 1) * 128], True, True)
                for g in range(4)], [Bxhat[blk], Bc], [Bbk_])
            oi = rot["osq"] % 2
            rot["osq"] += 1

            def fn(e, bk_=bk_, oi=oi):
                inst = None
                for g in range(4):
                    gs = slice(g * 128, (g + 1) * 128)
                    inst = e.scalar_tensor_tensor(out=t1[oi][:, gs], in0=bk_[:, gs], scalar=lnw[:, g:g + 1], in1=extra[:, gs],
                                                  op0=ALU.mult, op1=ALU.add)
                return inst
            S.op("dve", fn, reads=[Bbk_, Bc], writes=[Bt1[oi]])
            TT("dve", sTt[:].rearrange("p (h t) -> p h t", h=4)[:, :, blk * 128:(blk + 1) * 128],
               t1[oi][:].rearrange("p (h t) -> p h t", h=4),
               guT[:].rearrange("p (h t) -> p h t", h=4)[:, :, blk * 128:(blk + 1) * 128], ALU.mult,
               [Bt1[oi]] + BguT, [BsT[blk]])
            yield

    def g_gates():
        for j in range(2):
            rt_ga, rb_ga = load_slot(G_GA0 + j, 4096)
            rt_gb, rb_gb = load_slot(G_GB0 + j, 4096)
            for jj in range(4):
                oc = j * 4 + jj
                for (rt_, rb_, dst, Bdst) in ((rt_ga, rb_ga, sgaL[oc], BsgaL[oc]), (rt_gb, rb_gb, sgbL[oc], BsgbL[oc])):
                    bg_, Bbg_ = fm_proj(rt_, rb_, jj, NKC, hT, BhT)
                    gi = rot["gv"] % 4
                    rot["gv"] += 1
                    ACT(gv[gi][:], bg_[:], AF.Exp, [Bbg_], [Bgv[gi]], scale=-1.0)
                    ACT(gv[gi][:], gv[gi][:], AF.Ln, [Bgv[gi]], [Bgv[gi]], bias=1.0)
                    ACT(dst, gv[gi][:], AF.Exp, [Bgv[gi]], [Bdst], scale=-1.0)
                    yield

    def stage5():
        for j in range(2):
            rt_ab, rb_ab = load_slot(SL_AB + j, 4096)
            for jj in range(4):
                oc = j * 4 + jj
                mi = rot["m"] % 2
                rot["m"] += 1
                bya, Bbya = fm_proj(rt_ab, rb_ab, jj, 4, onT, BonT, stride=512, off=0)
                TT("dve", m1[mi], bya[:], sgaL[oc], ALU.mult, [Bbya, BsgaL[oc]], [Bm1[mi]])
                byb, Bbyb = fm_proj(rt_ab, rb_ab, jj, 4, sTt, BsT, stride=512, off=2048)
                TT("dve", m2[mi], byb[:], sgbL[oc], ALU.mult, [Bbyb, BsgbL[oc]], [Bm2[mi]])
                TT("pool", merged[oc], m1[mi], m2[mi], ALU.add, [Bm1[mi], Bm2[mi]], [Bmerged[oc]])

    def out_norm_residual(src, Bsrc, wvec, msb, Bms):
        for oc in range(8):
            ti = rot["tsc"] % 2
            rot["tsc"] += 1
            STT(tsc[ti], src[oc], wvec[:, oc:oc + 1], msb[:], ALU.mult, ALU.mult, [Bsrc[oc], Bms, Bc], [Btsc[ti]])
            TT("pool", xT[:, oc * T:(oc + 1) * T], xT[:, oc * T:(oc + 1) * T], tsc[ti], ALU.add, [BxT[oc], Btsc[ti]], [BxT[oc]])

    def stage6():
        msb, Bms = statbank()
        for j in range(2):
            rt_o, rb_o = load_slot(SL_O + j, 4096)
            for jj in range(4):
                oc = j * 4 + jj
                bk_, Bbk_ = nb()
                MM([(bk_[:], rt_o[:, kc * 512 + jj * 128: kc * 512 + (jj + 1) * 128], merged[kc], kc == 0, kc == 7) for kc in range(8)],
                   [rb_o] + Bmerged, [Bbk_])
                qi = rot["sq"] % 2
                rot["sq"] += 1
                ACT(sq[qi][:, 0:512], bk_[:], AF.Square, [Bbk_], [Bsq[qi]])
                COPY("dve", mix[oc], bk_[:], [Bbk_], [Bmix[oc]])
                MM([(msb[:], onesb[:], sq[qi][:, 0:512], oc == 0, oc == 7)], [Bsq[qi], Bc], [Bms])
        rmsnorm_rstd(msb, Bms, D)
        out_norm_residual(mix, Bmix, pmw2, msb, Bms)

    def stage7():
        msb, Bms = statbank()
        for k in range(NKC):
            qi = rot["sq"] % 2
            rot["sq"] += 1
            ACT(sq[qi][:, 0:512], xT[:, k * T:(k + 1) * T], AF.Square, [BxT[k]], [Bsq[qi]])
            MM([(msb[:], onesb[:], sq[qi][:, 0:512], k == 0, k == 7)], [Bsq[qi], Bc], [Bms])
        rmsnorm_rstd(msb, Bms, D)
        for k in range(NKC):
            STT(hT[:, k * T:(k + 1) * T], xT[:, k * T:(k + 1) * T], pfw[:, k:k + 1], msb[:], ALU.mult, ALU.mult,
                [BxT[k], Bms, Bc], [BhT[k]])

    def stage8():
        for j in range(11):
            rt, rb = load_slot(SL_GU + j, 4096)
            for half in range(2):
                bg_, Bbg_ = fm_proj(rt, rb, half, NKC, hT, BhT, stride=512, off=0)
                bu_, Bbu_ = fm_proj(rt, rb, half, NKC, hT, BhT, stride=512, off=256)
                si = rot["sgate"] % 2
                rot["sgate"] += 1
                ACT(sgate[si], bg_[:], AF.Silu, [Bbg_], [Bsgate[si]])
                TT("dve", hid[2 * j + half], bu_[:], sgate[si], ALU.mult, [Bbu_, Bsgate[si]], [Bhid[2 * j + half]])

    def stage9():
        msb, Bms = statbank()
        for oc in range(8):
            rt, rb = load_slot(SL_D + oc, NFC * 128)
            bk_, Bbk_ = nb()
            MM([(bk_[:], rt[:, kc * 128:(kc + 1) * 128], hid[kc], kc == 0, kc == NFC - 1) for kc in range(NFC)],
               [rb] + Bhid, [Bbk_])
            qi = rot["sq"] % 2
            rot["sq"] += 1
            ACT(sq[qi][:, 0:512], bk_[:], AF.Square, [Bbk_], [Bsq[qi]])
            COPY("dve", mix[oc], bk_[:], [Bbk_], [Bmix[oc]])
            MM([(msb[:], onesb[:], sq[qi][:, 0:512], oc == 0, oc == 7)], [Bsq[qi], Bc], [Bms])
        rmsnorm_rstd(msb, Bms, D)
        out_norm_residual(mix, Bmix, pfw2, msb, Bms)

    def stage10(tile_idx):
        for blk in range(4):
            si = xtm_ctr[0] % NXTM
            xtm_ctr[0] += 1
            for half in range(2):
                bk_, Bbk_ = nb()
                TR([(bk_[:, j * 128:(j + 1) * 128], xT[:, (half * 4 + j) * T + blk * 128:(half * 4 + j) * T + (blk + 1) * 128]) for j in range(4)],
                   identf[:], BxT[half * 4:half * 4 + 4] + [Bc], [Bbk_])
                COPY("act", xtm[si][:, half * 512:(half + 1) * 512], bk_[:], [Bbk_], [Bxtm[si]])
            r0 = tile_idx * T + blk * 128
            xt_ = xtm[si]
            store_ops.append(S.dma("sp", lambda e, xt_=xt_, r0=r0: e.dma_start(out=y[r0:r0 + 128, :], in_=xt_[:]),
                                   reads=[Bxtm[si]], sembuf=Bxtm[si]))

    Bbnd = Buf("bnd")
    S.op("pool", lambda e: e.memset(Sst[0][:], 0.0), writes=[BS[0]])
    S.op("pool", lambda e: e.memset(Sst[1][:], 0.0), writes=[BS[1]])
    p1_tiles = list(range(NTALL - 1, 0, -1))
    if n_p1_tiles is not None:
        p1_tiles = p1_tiles[:n_p1_tiles] if n_p1_tiles > 0 else []
    p1_slots = None
    for pc in pieces_A:
        emit_piece(pc)
    emit_piece_store()
    restB = list(pieces_B)
    per_tile = -(-len(restB) // max(len(p1_tiles), 1)) if p1_tiles else len(restB)
    if p1_tiles:
        p1_slots = [load_slot(G_ZB, 4096), load_slot(G_I, 4096)]

    def tg(t):
        S.tag = t

    def g_casts(n):
        for _ in range(n):
            if restB:
                emit_piece(restB.pop(0))
            yield

    dec1b = sb("dec1b", 32)
    Bdec1b = Buf("dec1b")
    p1sets = [dict(kt=krTM[1], Bkt=BkrTM[1], vt=vTM, Bvt=BvTM, dct=dec[1], Bdct=Bdec[1]),
              dict(kt=krTM[0], Bkt=BkrTM[0], vt=[PT[0][0], PT[0][1], PT[1][0], PT[1][1]],
                   Bvt=[BPT[0][0], BPT[0][1], BPT[1][0], BPT[1][1]], dct=dec1b, Bdct=Bdec1b)]
    prev_chain = None
    for idx, j in enumerate(p1_tiles):
        ps = p1sets[idx % 2]
        tg(f"p1 t{j}")

        def after(j=j):
            if j <= NT2:
                S.dma("sp", lambda e: e.dma_start(out=bnd[j - 1], in_=Sst[1][:]), reads=[BS[1]], writes=[Bbnd], sembuf=Bbnd)

        genA = seq(stage1(j, need_xT=False), stage2a(True, vt=ps["vt"], Bvt=ps["Bvt"], dct=ps["dct"], Bdct=ps["Bdct"]),
                   kr_transposes(1, kt=ps["kt"], Bkt=ps["Bkt"]))
        if prev_chain is None:
            interleave(genA, g_casts(per_tile))
        else:
            interleave(prev_chain, genA, g_casts(per_tile))
        prev_chain = state_chain(1, range(7, -1, -1), False, after=after, **ps)
    if prev_chain is not None:
        run(prev_chain)
    tg("cast rest")
    while restB:
        emit_piece(restB.pop(0))
    emit_piece_store()
    for j in range(NT2):
        if p1_tiles:
            S.dma("sp", lambda e, j=j: e.dma_start(out=Sst[1][:], in_=bnd[j]), reads=[Bbnd], writes=[BS[1]], sembuf=BS[1])
        else:
            S.op("pool", lambda e: e.memset(Sst[1][:], 0.0), writes=[BS[1]])
        tg(f"p2 t{j} s1")
        run(stage1(j))
        tg(f"p2 t{j} s2")
        if last_stage >= 2:
            run(stage2a(False))
        for si_, fn_ in ((3, stage3), (5, stage5), (6, stage6), (7, stage7),
                         (8, stage8), (9, stage9)):
            if si_ <= last_stage:
                tg(f"p2 t{j} s{si_}")
                fn_()
        tg(f"p2 t{j} s10")
        if not skip_s10:
            stage10(j)
    S.final_wait("sp", (store_ops[-NXTM:] if len(store_ops) >= NXTM else store_ops) + [b.writer for b in Bwsc if b.writer is not None])
    S.emit()
    global LAST_NAMES
    LAST_NAMES = S.names
    return nc


def _host_consts():
    identf = np.eye(128, dtype=np.float32)
    s = np.arange(128)[:, None]
    t = np.arange(128)[None, :]
    same = (s // 64) == (t // 64)
    mf = (same & (s <= t)).astype(np.float32)
    mb = (same & (s >= t)).astype(np.float32)
    masks = np.zeros((4, 128, 512), np.float32)
    masks[0] = np.tile(mf, (1, 4))
    masks[1] = np.tile(mb, (1, 4))
    tt = np.arange(512)
    masks[2] = np.broadcast_to((tt % 64 != 0).astype(np.float32), (128, 512))
    masks[3] = np.broadcast_to((tt % 64 != 63).astype(np.float32), (128, 512))
    return identf, masks


def _weights_for(flip, w_in, lb_logits, sg_spatial_w, sg_spatial_b):
    q, i_, ff, fb, g, u, v = [w_in[:, k * 512:(k + 1) * 512] for k in range(7)]
    ga = w_in[:, 3584:4608]
    gb = w_in[:, 4608:5632]
    zf, zb = (fb, ff) if flip else (ff, fb)
    w_in_r = np.ascontiguousarray(np.concatenate([zf, zb, q, i_, v, g, u, ga, gb], axis=1))
    lbl = lb_logits[:, ::-1, :] if flip else lb_logits
    ws = sg_spatial_w
    bs = sg_spatial_b
    if flip:
        ws = ws[:, ::-1, ::-1]
        bs = bs[:, ::-1]
    wst = np.ascontiguousarray(np.transpose(ws, (2, 0, 1)))
    lblT = np.transpose(lbl.reshape(2, 2, 4, 128), (3, 0, 1, 2)).reshape(128, 16)
    return w_in_r, lblT, wst, np.ascontiguousarray(bs.reshape(1, 512))


_PROGRAM_CACHE = {}
LAST_NAMES = {}


def _run(x, pre_mix_w, w_in, lb_logits, hg_norm_w, sg_ln_w, sg_ln_b, sg_spatial_w, sg_spatial_b,
         w_proj_a, w_proj_b, w_out, post_mix_w, pre_ffn_w, w_gate, w_up, w_down, post_ffn_w, **build_kw):
    x = np.asarray(x, np.float32)
    B, L, _ = x.shape
    f = lambda a: np.ascontiguousarray(np.asarray(a, np.float32))
    identf, masks = _host_consts()
    vecs = np.stack([f(pre_mix_w)[0], f(post_mix_w)[0], f(pre_ffn_w)[0], f(post_ffn_w)[0]], axis=0)
    vecsT = np.transpose(vecs.reshape(4, 8, 128), (2, 0, 1)).reshape(128, 32)
    hgwT = np.transpose(f(hg_norm_w)[0].reshape(4, 128), (1, 0))
    lnwT = np.transpose(f(sg_ln_w)[0].reshape(4, 128), (1, 0))
    lnb_bc = np.ascontiguousarray(np.broadcast_to(f(sg_ln_b)[0][None, :], (128, 512)))
    common = dict(w_pa=f(w_proj_a)[0], w_pb=f(w_proj_b)[0], w_out=f(w_out)[0], w_gate=f(w_gate)[0], w_up=f(w_up)[0],
                  w_down=f(w_down)[0], lnb_bc=lnb_bc, identf=identf, masks=masks)
    per_flip = []
    for flip in (False, True):
        w_in_r, lbl, wst, bs = _weights_for(flip, f(w_in)[0], f(lb_logits), f(sg_spatial_w)[0], f(sg_spatial_b)[0])
        smalls = np.ascontiguousarray(np.concatenate([vecsT, hgwT, lnwT, lbl], axis=1).astype(np.float32))
        per_flip.append(dict(w_in=w_in_r, smalls=smalls, wst=wst, bs=bs))
    in_maps = []
    for b in range(B):
        for flip in (False, True):
            xs = x[b, ::-1] if flip else x[b]
            m = dict(common)
            m.update(per_flip[int(flip)])
            m["xs"] = np.ascontiguousarray(xs)
            in_maps.append(m)
    key = (L, tuple(sorted(build_kw.items())))
    nc = build_program(L, **build_kw)
    res = run_bass_kernel_spmd(nc, in_maps, core_ids=list(range(len(in_maps))))
    own = L // 2
    out = np.zeros((B, L, D), np.float32)
    for b in range(B):
        y0 = res.results[2 * b]["y"]
        y1 = res.results[2 * b + 1]["y"]
        out[b, :own] = y0
        out[b, own:] = y1[::-1]
    return out


def kernel(**inputs):
    return _run(**inputs)
```
